# Optimizing a Trainium2 kernel written in Bass

```python
import math
import jax, jax.numpy as jnp
from jax import lax
import numpy as np

D_MODEL = 1024
BATCH = 8
SEQ = 4096
DEPTH = 2

HEAD_DIM = 64
MIX_WIDTH = D_MODEL
N_GROUPS = 4
GROUP_WIDTH = MIX_WIDTH // N_GROUPS
HEADS_PER_GROUP = GROUP_WIDTH // HEAD_DIM
FOX_HEADS = HEADS_PER_GROUP
MOBA_HEADS = HEADS_PER_GROUP
SB_HEADS = HEADS_PER_GROUP
SWA_HEADS = HEADS_PER_GROUP
SWA_KV_HEADS = SWA_HEADS // 2
SWA_KV_WIDTH = SWA_KV_HEADS * HEAD_DIM
QUERY_BLOCK = 128
MOBA_BLOCK = 256
MOBA_TOPK = 3
MOBA_QUERY_CHUNK = 32
SWA_WINDOW = 128
NUM_BUCKETS = 32
REL_MAX_DISTANCE = 1024
RMS_EPS = 1e-6
FORGET_BIAS_INIT = 2.0
NEG_INF = -1e30
ATTN_SCALE = HEAD_DIM ** -0.5
SPLIT_SIZES = (
    GROUP_WIDTH, GROUP_WIDTH, GROUP_WIDTH, FOX_HEADS, GROUP_WIDTH,
    GROUP_WIDTH, GROUP_WIDTH, GROUP_WIDTH, GROUP_WIDTH,
    GROUP_WIDTH, GROUP_WIDTH, GROUP_WIDTH, GROUP_WIDTH,
    GROUP_WIDTH, SWA_KV_WIDTH, SWA_KV_WIDTH, GROUP_WIDTH,
)
IN_WIDTH = sum(SPLIT_SIZES)

kernel_name = 'hybrid_fox_moba_stickbreak_swa_block'


def rms_norm(x, gain):
    xf = x.astype(jnp.float32)
    y = xf * lax.rsqrt(jnp.mean(xf * xf, axis=-1, keepdims=True) + RMS_EPS)
    return (y * gain.astype(jnp.float32)).astype(x.dtype)


def split_columns(proj):
    parts, off = [], 0
    for w in SPLIT_SIZES:
        parts.append(proj[..., off:off + w])
        off += w
    return parts


def to_heads(t, n):
    b, s, _ = t.shape
    return t.reshape(b, s, n, HEAD_DIM).transpose(0, 2, 1, 3)


def from_heads(o):
    b, h, s, d = o.shape
    return o.transpose(0, 2, 1, 3).reshape(b, s, h * d)


def unblock(out):
    nb, b, h, qb, d = out.shape
    return out.transpose(1, 2, 0, 3, 4).reshape(b, h, nb * qb, d)


def rel_bucket(dist):
    max_exact = NUM_BUCKETS // 2
    d = jnp.maximum(dist, 0)
    log_ratio = jnp.log(jnp.maximum(d, 1).astype(jnp.float32) / max_exact) / math.log(REL_MAX_DISTANCE / max_exact)
    large = max_exact + (log_ratio * (NUM_BUCKETS - max_exact)).astype(jnp.int32)
    large = jnp.minimum(large, NUM_BUCKETS - 1)
    return jnp.where(d < max_exact, d, large)


def forgetting_attention(q, k, v, log_f):
    b, h, t_len, d = q.shape
    c = jnp.cumsum(log_f, axis=-1)
    pos = jnp.arange(t_len)

    def block(i):
        start = i * QUERY_BLOCK
        q_i = lax.dynamic_slice_in_dim(q, start, QUERY_BLOCK, axis=2)
        c_i = lax.dynamic_slice_in_dim(c, start, QUERY_BLOCK, axis=2)
        t = start + jnp.arange(QUERY_BLOCK)
        s = jnp.einsum('bhqd,bhkd->bhqk', q_i, k).astype(jnp.float32) * ATTN_SCALE
        s = s + c_i[..., :, None] - c[..., None, :]
        s = jnp.where(t[:, None] >= pos[None, :], s, NEG_INF)
        p = jax.nn.softmax(s, axis=-1).astype(v.dtype)
        return jnp.einsum('bhqk,bhkd->bhqd', p, v)

    return unblock(lax.map(block, jnp.arange(t_len // QUERY_BLOCK)))


def moba_attention(q, k, v, rel_bias_h):
    b, h, t_len, d = q.shape
    t_pad = -(-t_len // MOBA_BLOCK) * MOBA_BLOCK
    if t_pad != t_len:
        padw = ((0, 0), (0, 0), (0, t_pad - t_len), (0, 0))
        q, k, v = jnp.pad(q, padw), jnp.pad(k, padw), jnp.pad(v, padw)
    nblk = t_pad // MOBA_BLOCK
    k_eff = min(MOBA_TOPK, nblk)
    k_blocks = k.reshape(b, h, nblk, MOBA_BLOCK, d)
    v_blocks = v.reshape(b, h, nblk, MOBA_BLOCK, d)
    k_mean = jnp.mean(k_blocks.astype(jnp.float32), axis=3)
    gate = jnp.einsum('bhtd,bhnd->bhtn', q.astype(jnp.float32), k_mean)
    q_blk = jnp.arange(t_pad) // MOBA_BLOCK
    past = jnp.arange(nblk)[None, :] < q_blk[:, None]
    gate = jnp.where(past, gate, NEG_INF)
    _, sel = lax.top_k(gate, k_eff)
    n_valid = jnp.minimum(q_blk, k_eff)
    sel_valid = jnp.arange(k_eff)[None, :] < n_valid[:, None]
    b_idx = jnp.arange(b)[:, None, None, None]
    h_idx = jnp.arange(h)[None, :, None, None]
    h_idx5 = h_idx[..., None]
    blk_off = jnp.arange(MOBA_BLOCK)

    def chunk(ci):
        start = ci * MOBA_QUERY_CHUNK
        q_c = lax.dynamic_slice_in_dim(q, start, MOBA_QUERY_CHUNK, axis=2)
        sel_c = lax.dynamic_slice_in_dim(sel, start, MOBA_QUERY_CHUNK, axis=2)
        valid_c = lax.dynamic_slice_in_dim(sel_valid, start, MOBA_QUERY_CHUNK, axis=0)
        t = start + jnp.arange(MOBA_QUERY_CHUNK)
        k_sel = k_blocks[b_idx, h_idx, sel_c]
        v_sel = v_blocks[b_idx, h_idx, sel_c]
        s_sel = jnp.einsum('bhqd,bhqnkd->bhqnk', q_c, k_sel).astype(jnp.float32) * ATTN_SCALE
        key_pos = sel_c[..., None] * MOBA_BLOCK + blk_off
        dist = t[None, None, :, None, None] - key_pos
        s_sel = s_sel + rel_bias_h[h_idx5, rel_bucket(dist)].astype(jnp.float32)
        s_sel = jnp.where(valid_c[None, None, :, :, None], s_sel, NEG_INF)
        own_start = (start // MOBA_BLOCK) * MOBA_BLOCK
        k_own = lax.dynamic_slice_in_dim(k, own_start, MOBA_BLOCK, axis=2)
        v_own = lax.dynamic_slice_in_dim(v, own_start, MOBA_BLOCK, axis=2)
        s_own = jnp.einsum('bhqd,bhkd->bhqk', q_c, k_own).astype(jnp.float32) * ATTN_SCALE
        dist_own = t[:, None] - (own_start + blk_off)[None, :]
        s_own = s_own + rel_bias_h[:, rel_bucket(dist_own)][None].astype(jnp.float32)
        s_own = jnp.where(dist_own >= 0, s_own, NEG_INF)
        n_sel = k_eff * MOBA_BLOCK
        s_all = jnp.concatenate([s_sel.reshape(b, h, MOBA_QUERY_CHUNK, n_sel), s_own], axis=-1)
        p = jax.nn.softmax(s_all, axis=-1).astype(v.dtype)
        p_sel = p[..., :n_sel].reshape(b, h, MOBA_QUERY_CHUNK, k_eff, MOBA_BLOCK)
        p_own = p[..., n_sel:]
        return (jnp.einsum('bhqnk,bhqnkd->bhqd', p_sel, v_sel)
                + jnp.einsum('bhqk,bhkd->bhqd', p_own, v_own))

    out = unblock(lax.map(chunk, jnp.arange(t_pad // MOBA_QUERY_CHUNK)))
    return out[:, :, :t_len]


def stick_breaking_attention(q, k, v):
    b, h, t_len, d = q.shape
    pos = jnp.arange(t_len)

    def block(i):
        start = i * QUERY_BLOCK
        q_i = lax.dynamic_slice_in_dim(q, start, QUERY_BLOCK, axis=2)
        t = start + jnp.arange(QUERY_BLOCK)
        z = jnp.einsum('bhqd,bhkd->bhqk', q_i, k).astype(jnp.float32) * ATTN_SCALE
        strict = pos[None, :] < t[:, None]
        log_keep = jnp.where(strict, jax.nn.log_sigmoid(-z), 0.0)
        suffix = lax.cumsum(log_keep, axis=3, reverse=True) - log_keep
        a = jnp.where(strict, jnp.exp(jax.nn.log_sigmoid(z) + suffix), 0.0)
        return jnp.einsum('bhqk,bhkd->bhqd', a.astype(v.dtype), v)

    return unblock(lax.map(block, jnp.arange(t_len // QUERY_BLOCK)))


def swa_sink_attention(q, k, v, sinks, rel_bias_h):
    b, h, t_len, d = q.shape
    hkv = k.shape[1]
    g = h // hkv
    w = SWA_WINDOW
    nb = t_len // w
    qb = q.reshape(b, hkv, g, nb, w, d)

    def band(a):
        ab = a.reshape(b, hkv, nb, w, d)
        prev = jnp.pad(ab, ((0, 0), (0, 0), (1, 0), (0, 0), (0, 0)))[:, :, :-1]
        return jnp.concatenate([prev, ab], axis=3)

    kb, vb = band(k), band(v)
    s = jnp.einsum('bkgnqd,bknsd->bkgnqs', qb, kb).astype(jnp.float32) * ATTN_SCALE
    qi = jnp.arange(w)
    kj = jnp.arange(2 * w)
    dist = qi[:, None] + w - kj[None, :]
    bias = rel_bias_h[:, rel_bucket(dist)].astype(jnp.float32).reshape(hkv, g, 1, w, 2 * w)
    key_pos = jnp.arange(nb)[:, None] * w - w + kj[None, :]
    allowed = ((dist >= 0) & (dist < w))[None] & (key_pos >= 0)[:, None, :]
    s = jnp.where(allowed, s + bias, NEG_INF)
    sink = jnp.broadcast_to(sinks.astype(jnp.float32).reshape(1, hkv, g, 1, 1, 1), s.shape[:-1] + (1,))
    p = jax.nn.softmax(jnp.concatenate([s, sink], axis=-1), axis=-1)[..., :-1]
    o = jnp.einsum('bkgnqs,bknsd->bkgnqd', p.astype(v.dtype), vb)
    return o.reshape(b, h, t_len, d)


def hybrid_layer(x, norm_gain, w_in, b_forget, fox_qk_gain, moba_qk_gain, swa_qk_gain, sinks, w_out, rel_bias):
    hn = rms_norm(x, norm_gain)
    proj = jnp.einsum('btd,de->bte', hn, w_in)
    (fq, fk, fv, ff, fg, mq, mk, mv, mg, sq, sk, sv, sg, wq, wk, wv, wg) = split_columns(proj)
    log_f = jax.nn.log_sigmoid((ff + b_forget).astype(jnp.float32)).transpose(0, 2, 1)
    o_fox = forgetting_attention(rms_norm(to_heads(fq, FOX_HEADS), fox_qk_gain[0]),
                                 rms_norm(to_heads(fk, FOX_HEADS), fox_qk_gain[1]),
                                 to_heads(fv, FOX_HEADS), log_f)
    o_moba = moba_attention(rms_norm(to_heads(mq, MOBA_HEADS), moba_qk_gain[0]),
                            rms_norm(to_heads(mk, MOBA_HEADS), moba_qk_gain[1]),
                            to_heads(mv, MOBA_HEADS), rel_bias[:, :MOBA_HEADS].T)
    o_sb = stick_breaking_attention(to_heads(sq, SB_HEADS), to_heads(sk, SB_HEADS), to_heads(sv, SB_HEADS))
    o_swa = swa_sink_attention(rms_norm(to_heads(wq, SWA_HEADS), swa_qk_gain[0]),
                               rms_norm(to_heads(wk, SWA_KV_HEADS), swa_qk_gain[1]),
                               to_heads(wv, SWA_KV_HEADS), sinks, rel_bias[:, MOBA_HEADS:].T)
    y = jnp.concatenate([from_heads(o_fox) * jax.nn.silu(fg),
                         from_heads(o_moba) * jax.nn.silu(mg),
                         from_heads(o_sb) * jax.nn.silu(sg),
                         from_heads(o_swa) * jax.nn.silu(wg)], axis=-1)
    return x + jnp.einsum('bte,ed->btd', y, w_out)


def setup_inputs(seed: int = 0) -> dict:
    key = jax.random.key(seed)
    ks = jax.random.split(key, 10)
    x = jax.random.normal(ks[0], (BATCH, SEQ, D_MODEL), jnp.float32)
    norm_gain = 1.0 + 0.02 * jax.random.normal(ks[1], (DEPTH, D_MODEL), jnp.float32)
    w_in = jax.random.normal(ks[2], (DEPTH, D_MODEL, IN_WIDTH), jnp.float32) * D_MODEL ** -0.5
    b_forget = FORGET_BIAS_INIT + 0.1 * jax.random.normal(ks[3], (DEPTH, FOX_HEADS), jnp.float32)
    fox_qk_gain = 1.0 + 0.02 * jax.random.normal(ks[4], (DEPTH, 2, HEAD_DIM), jnp.float32)
    moba_qk_gain = 1.0 + 0.02 * jax.random.normal(ks[5], (DEPTH, 2, HEAD_DIM), jnp.float32)
    swa_qk_gain = 1.0 + 0.02 * jax.random.normal(ks[6], (DEPTH, 2, HEAD_DIM), jnp.float32)
    sinks = 0.5 * jax.random.normal(ks[7], (DEPTH, SWA_HEADS), jnp.float32)
    w_out = jax.random.normal(ks[8], (DEPTH, MIX_WIDTH, D_MODEL), jnp.float32) * MIX_WIDTH ** -0.5
    rel_bias = 0.1 * jax.random.normal(ks[9], (NUM_BUCKETS, MOBA_HEADS + SWA_HEADS), jnp.float32)
    return {'x': x, 'norm_gain': norm_gain, 'w_in': w_in, 'b_forget': b_forget,
            'fox_qk_gain': fox_qk_gain, 'moba_qk_gain': moba_qk_gain, 'swa_qk_gain': swa_qk_gain,
            'sinks': sinks, 'w_out': w_out, 'rel_bias': rel_bias}


def reference(x, norm_gain, w_in, b_forget, fox_qk_gain, moba_qk_gain, swa_qk_gain, sinks, w_out, rel_bias):
    for layer in range(DEPTH):
        x = hybrid_layer(x, norm_gain[layer], w_in[layer], b_forget[layer], fox_qk_gain[layer],
                         moba_qk_gain[layer], swa_qk_gain[layer], sinks[layer], w_out[layer], rel_bias)
    return x
```

```python
import math
import numpy as np
from contextlib import ExitStack
import concourse.bass as bass
import concourse.mybir as mybir
from concourse.bass_utils import run_bass_kernel_spmd

F32 = mybir.dt.float32
BF16 = mybir.dt.bfloat16
ALU = mybir.AluOpType
AF = mybir.ActivationFunctionType
AX = mybir.AxisListType

D_MODEL = 1024
DEPTH = 2
HD = 64
IN_WIDTH = 3844
NEG = -30000.0
A_Q, A_K, A_V, A_F, A_G = 0, 256, 512, 768, 772
B_Q, B_K, B_V, B_G = 1028, 1284, 1540, 1796
C_Q, C_K, C_V, C_G = 2052, 2308, 2564, 2820
D_Q, D_K, D_V, D_G = 3076, 3332, 3460, 3588
ID_FQ, ID_FK, ID_MQ, ID_MK, ID_SQ, ID_SK, ID_WQ, ID_WK = 0, 4, 8, 12, 16, 20, 24, 28
N_QK = 30
KROWS = 80


class Buf:
    __slots__ = ("w", "r")

    def __init__(self):
        self.w = {}
        self.r = {}


class Sched:
    CE = ("pe", "act", "dve", "pool")

    def __init__(self, nc, es, n_dma_sems=16):
        self.nc = nc
        self.sems = {}
        for e in self.CE:
            self.sems[e] = es.enter_context(nc.semaphore("s_" + e))
        self.dma_names = ["d%d" % i for i in range(n_dma_sems)]
        for d in self.dma_names:
            self.sems[d] = es.enter_context(nc.semaphore("s_" + d))
        self.dma_cnt = {d: 0 for d in self.dma_names}
        self.dma_rr = 0
        self.streams = {e: [] for e in ("pe", "act", "dve", "pool", "sp")}
        self.n = {e: 0 for e in self.CE}
        self.seen = {e: {} for e in self.streams}
        self.needed = {e: set() for e in self.CE}
        self.pending = {e: {} for e in self.streams}

    def _deps(self, reads, writes, q):
        deps = dict(self.pending[q])
        self.pending[q] = {}
        for b in reads:
            for s, v in b.w.items():
                if deps.get(s, 0) < v:
                    deps[s] = v
        for b in writes:
            for s, v in b.w.items():
                if deps.get(s, 0) < v:
                    deps[s] = v
            for s, v in b.r.items():
                if deps.get(s, 0) < v:
                    deps[s] = v
        return deps

    def _waits(self, q, deps):
        waits = []
        seen = self.seen[q]
        for s, v in deps.items():
            if s == "pe" and q == "pe":
                continue
            if seen.get(s, 0) < v:
                waits.append((s, v))
                seen[s] = v
                if s in self.needed:
                    self.needed[s].add(v)
        return waits

    def _mark(self, sem, val, reads, writes):
        for b in reads:
            if b.r.get(sem, 0) < val:
                b.r[sem] = val
        for b in writes:
            if b.w.get(sem, 0) < val:
                b.w[sem] = val

    def op(self, eng, fn, reads=(), writes=()):
        deps = self._deps(reads, writes, eng)
        waits = self._waits(eng, deps)
        self.n[eng] += 1
        idx = self.n[eng]
        self.streams[eng].append((waits, _record(fn), eng, idx, False))
        self._mark(eng, idx, reads, writes)

    def dma(self, fn, reads=(), writes=(), q="sp"):
        deps = self._deps(reads, writes, q)
        d = self.dma_names[self.dma_rr]
        self.dma_rr = (self.dma_rr + 1) % len(self.dma_names)
        if self.dma_cnt[d] > 0 and deps.get(d, 0) < self.dma_cnt[d]:
            deps[d] = self.dma_cnt[d]
        waits = self._waits(q, deps)
        self.dma_cnt[d] += 16
        val = self.dma_cnt[d]
        self.streams[q].append((waits, _record(fn), d, val, True))
        self._mark(d, val, reads, writes)

    def barrier(self):
        snap = {e: self.n[e] for e in self.CE if self.n[e] > 0}
        for d in self.dma_names:
            if self.dma_cnt[d] > 0:
                snap[d] = self.dma_cnt[d]
        for q in self.pending:
            p = self.pending[q]
            for s, v in snap.items():
                if p.get(s, 0) < v:
                    p[s] = v

    def final_wait(self, q="sp"):
        self.barrier()
        deps = dict(self.pending[q])
        self.pending[q] = {}
        waits = self._waits(q, deps)
        self.streams[q].append((waits, None, None, None, False))

    def emit(self, block):
        cmap = {}
        for e in self.CE:
            ks = sorted(self.needed[e])
            cmap[e] = {k: i + 1 for i, k in enumerate(ks)}
        sems = self.sems
        streams = self.streams

        def runner(ename):
            def body(engobj):
                for (waits, fn, sem, idx, is_dma) in streams[ename]:
                    for (s, v) in waits:
                        vv = cmap[s][v] if s in cmap else v
                        engobj.wait_ge(sems[s], vv)
                    if fn is None:
                        continue
                    ins = getattr(engobj, fn[0])(*fn[1], **fn[2])
                    if is_dma:
                        ins.then_inc(sems[sem], 16)
                    elif idx in cmap[sem]:
                        ins.then_inc(sems[sem], 1)
            return body
        block.tensor(runner("pe"))
        block.scalar(runner("act"))
        block.vector(runner("dve"))
        block.gpsimd(runner("pool"))
        block.sync(runner("sp"))


class _Rec:
    def __init__(self):
        self.call = None

    def __getattr__(self, name):
        def f(*a, **k):
            self.call = (name, a, k)
            return None
        return f


def _record(fn):
    r = _Rec()
    fn(r)
    assert r.call is not None
    return r.call


class Rot:
    def __init__(self, items):
        self.items = items
        self.i = 0

    def next(self):
        it = self.items[self.i]
        self.i = (self.i + 1) % len(self.items)
        return it


def build(T, L=DEPTH, dbg=False):
    NT = T // 128
    NG = T // 512
    NB = T // 256
    nc = bass.Bass("TRN2", target_bir_lowering=False)
    x_in = nc.dram_tensor("x", [T, D_MODEL], F32, kind="ExternalInput").ap()
    norm_gain = nc.dram_tensor("norm_gain", [DEPTH, D_MODEL], F32, kind="ExternalInput").ap()
    w_in = nc.dram_tensor("w_in", [DEPTH, D_MODEL, IN_WIDTH], F32, kind="ExternalInput").ap()
    b_forget = nc.dram_tensor("b_forget", [DEPTH, 4], F32, kind="ExternalInput").ap()
    qk_gain = nc.dram_tensor("qk_gain", [DEPTH, 6, 64], F32, kind="ExternalInput").ap()
    sinks = nc.dram_tensor("sinks", [DEPTH, 4], F32, kind="ExternalInput").ap()
    w_out = nc.dram_tensor("w_out", [DEPTH, D_MODEL, D_MODEL], F32, kind="ExternalInput").ap()
    rb31 = nc.dram_tensor("rb31", [1, 8], F32, kind="ExternalInput").ap()
    bias_moba = nc.dram_tensor("bias_moba", [4, 8, 128, 128], F32, kind="ExternalInput").ap()
    bias_swa = nc.dram_tensor("bias_swa", [4, 2, 128, 128], F32, kind="ExternalInput").ap()
    cst = nc.dram_tensor("cst", [6, 128, 128], F32, kind="ExternalInput").ap()
    onehot = nc.dram_tensor("onehot", [16, T], F32, kind="ExternalInput").ap()
    out = nc.dram_tensor("out", [T, D_MODEL], F32, kind="ExternalOutput").ap()
    x1 = nc.dram_tensor("x1_scr", [T, D_MODEL], F32, kind="Internal").ap()
    qk_scr = nc.dram_tensor("qk_scr", [N_QK, KROWS, T], BF16, kind="Internal").ap()
    g_scr = nc.dram_tensor("g_scr", [8, 128, T], BF16, kind="Internal").ap()
    v_scr = nc.dram_tensor("v_scr", [T, 14, 192], BF16, kind="Internal").ap()
    if dbg:
        dbg_y = nc.dram_tensor("dbg_y", [8, 128, T], F32, kind="ExternalOutput").ap()

    with ExitStack() as es:
        S = Sched(nc, es)

        _uid = [0]

        def sb(name, shape, dt, scope=es):
            _uid[0] += 1
            return scope.enter_context(nc.sbuf_tensor("%s_%d" % (name, _uid[0]), shape, dt))

        PS = [es.enter_context(nc.psum_tensor("ps%d" % i, [128, 512], F32)) for i in range(8)]
        BPS = [Buf() for _ in range(8)]

        ident_b = sb("ident_b", [128, 128], BF16)
        ident_f = sb("ident_f", [128, 128], F32)
        tri_f = sb("tri_f", [128, 128], F32)
        ones_f = sb("ones_f", [128, 128], F32)
        ones_b = sb("ones_b", [128, 128], BF16)
        negU_b = sb("negU_b", [128, 128], BF16)
        blk_b = sb("blk_b", [128, 128], BF16)
        mask_b = sb("mask_b", [128, 3, 128], BF16)
        Bconst = Buf()
        gcols = sb("gcols", [128, DEPTH, 6], F32)
        ngain = sb("ngain", [128, DEPTH, D_MODEL], F32)
        bfg = sb("bfg", [128, DEPTH, 4], F32)
        esink = sb("esink", [128, DEPTH, 4], F32)
        rb31_t = sb("rb31_t", [128, 8], F32)
        biasM = sb("biasM", [128, 4, 8, 128], BF16)
        biasW = sb("biasW", [128, 4, 2, 128], BF16)
        xf_all = sb("xf_all", [128, NT, 4], F32)
        BxF = Buf()
        kmean = sb("kmean", [128, 2, NB], F32)
        Bkm = Buf()
        c_all = sb("c_all", [128, NT, 4], F32)
        Bc = Buf()

        with ExitStack() as e0:
            stg = sb("c_stg", [128, 6, 128], F32, e0)
            Bstg = Buf()
            S.dma(lambda e: e.dma_start(out=stg[:], in_=cst.rearrange("a p n -> p a n")), writes=[Bstg])
            S.op("dve", lambda e: e.tensor_copy(out=ident_f[:], in_=stg[:, 0, :]), reads=[Bstg], writes=[Bconst])
            S.op("dve", lambda e: e.tensor_copy(out=ident_b[:], in_=stg[:, 0, :]), reads=[Bstg], writes=[Bconst])
            S.op("dve", lambda e: e.tensor_copy(out=tri_f[:], in_=stg[:, 1, :]), reads=[Bstg], writes=[Bconst])
            S.op("dve", lambda e: e.tensor_copy(out=negU_b[:], in_=stg[:, 2, :]), reads=[Bstg], writes=[Bconst])
            S.op("dve", lambda e: e.tensor_copy(out=blk_b[:], in_=stg[:, 3, :]), reads=[Bstg], writes=[Bconst])
            S.op("dve", lambda e: e.memset(ones_f[:], 1.0), writes=[Bconst])
            S.op("dve", lambda e: e.memset(ones_b[:], 1.0), writes=[Bconst])
            S.op("dve", lambda e: e.tensor_copy(out=mask_b[:, 0, :], in_=stg[:, 4, :]), reads=[Bstg], writes=[Bconst])
            S.op("dve", lambda e: e.tensor_copy(out=mask_b[:, 1, :], in_=stg[:, 5, :]), reads=[Bstg], writes=[Bconst])
            S.op("dve", lambda e: e.tensor_scalar(out=mask_b[:, 2, :], in0=stg[:, 4, :], scalar1=-1.0, scalar2=NEG,
                                                  op0=ALU.mult, op1=ALU.add), reads=[Bstg], writes=[Bconst])
            for l in range(DEPTH):
                S.dma(lambda e, l=l: e.dma_start(out=ngain[:, l, :], in_=norm_gain[l:l + 1, :].partition_broadcast(128)),
                      writes=[Bconst])
                S.dma(lambda e, l=l: e.dma_start(out=bfg[:, l, :], in_=b_forget[l:l + 1, :].partition_broadcast(128)),
                      writes=[Bconst])
                S.dma(lambda e, l=l: e.dma_start(out=esink[:, l, :], in_=sinks[l:l + 1, :].partition_broadcast(128)),
                      writes=[Bconst])
                for half in range(2):
                    S.dma(lambda e, l=l, half=half: e.dma_start(
                        out=gcols[half * 64:(half + 1) * 64, l, :], in_=qk_gain[l].rearrange("s d -> d s"),
                        allow_slow_non_contiguous=True), writes=[Bconst])
            S.dma(lambda e: e.dma_start(out=rb31_t[:], in_=rb31[0:1, :].partition_broadcast(128)), writes=[Bconst])
            for l in range(DEPTH):
                for j in (0, 2, 4):
                    S.op("dve", lambda e, l=l, j=j: e.tensor_scalar(out=gcols[:, l, j:j + 1], in0=gcols[:, l, j:j + 1],
                                                                    scalar1=0.125, scalar2=None, op0=ALU.mult),
                         reads=[Bconst], writes=[Bconst])
                S.op("act", lambda e, l=l: e.activation(out=esink[:, l, :], in_=esink[:, l, :], func=AF.Exp),
                     reads=[Bconst], writes=[Bconst])
            bst = [sb("b_stg%d" % i, [128, 8, 128], F32, e0) for i in range(2)]
            Bbst = [Buf(), Buf()]
            for h in range(4):
                t_, b_ = bst[h % 2], Bbst[h % 2]
                S.dma(lambda e, h=h, t_=t_: e.dma_start(out=t_[:], in_=bias_moba[h].rearrange("a p n -> p a n")), writes=[b_])
                S.op("dve", lambda e, h=h, t_=t_: e.tensor_scalar(out=t_[:], in0=t_[:], scalar1=rb31_t[:, h:h + 1],
                                                                  scalar2=None, op0=ALU.subtract),
                     reads=[b_, Bconst], writes=[b_])
                S.op("dve", lambda e, h=h, t_=t_: e.tensor_tensor(out=t_[:, 0, :], in0=t_[:, 0, :], in1=stg[:, 4, :],
                                                                  op=ALU.add), reads=[b_, Bstg], writes=[b_])
                S.op("dve", lambda e, h=h, t_=t_: e.tensor_copy(out=biasM[:, h, :, :], in_=t_[:]), reads=[b_], writes=[Bconst])
            for h in range(4):
                t_, b_ = bst[h % 2], Bbst[h % 2]
                S.dma(lambda e, h=h, t_=t_: e.dma_start(out=t_[:, 0:2, :], in_=bias_swa[h].rearrange("a p n -> p a n")),
                      writes=[b_])
                S.op("dve", lambda e, t_=t_: e.tensor_tensor(out=t_[:, 0, :], in0=t_[:, 0, :], in1=stg[:, 4, :],
                                                             op=ALU.add), reads=[b_, Bstg], writes=[b_])
                S.op("dve", lambda e, t_=t_: e.tensor_tensor(out=t_[:, 1, :], in0=t_[:, 1, :], in1=stg[:, 4, :],
                                                             op=ALU.subtract), reads=[b_, Bstg], writes=[b_])
                S.op("dve", lambda e, t_=t_: e.tensor_scalar(out=t_[:, 1, :], in0=t_[:, 1, :], scalar1=NEG,
                                                             scalar2=None, op0=ALU.add), reads=[b_], writes=[b_])
                S.op("dve", lambda e, h=h, t_=t_: e.tensor_copy(out=biasW[:, h, :, :], in_=t_[:, 0:2, :]),
                     reads=[b_], writes=[Bconst])
            oh = sb("oh_stg", [KROWS, T], F32, e0)
            ohb = sb("oh_b", [KROWS, T], BF16, e0)
            Boh = Buf()
            S.dma(lambda e: e.dma_start(out=oh[64:80, :], in_=onehot[:, :]), writes=[Boh])
            S.op("dve", lambda e: e.tensor_copy(out=ohb[64:80, :], in_=oh[64:80, :]), reads=[Boh], writes=[Boh])
            Bqk = [Buf() for _ in range(N_QK)]
            for h in range(4):
                S.dma(lambda e, h=h: e.dma_start(out=qk_scr[ID_MK + h, 64:80, :], in_=ohb[64:80, :]), reads=[Boh],
                      writes=[Bqk[ID_MK + h]])
            onesr = sb("onesr", [KROWS, T], BF16, e0)
            Bor = Buf()
            S.op("pool", lambda e: e.memset(onesr[64:70, :], 1.0), writes=[Bor])
            for h in range(4):
                S.dma(lambda e, h=h: e.dma_start(out=qk_scr[ID_FK + h, 64:67, :], in_=onesr[64:67, :]), reads=[Bor],
                      writes=[Bqk[ID_FK + h]])
                S.dma(lambda e, h=h: e.dma_start(out=qk_scr[ID_FQ + h, 67:70, :], in_=onesr[64:67, :]), reads=[Bor],
                      writes=[Bqk[ID_FQ + h]])
            S.barrier()
        Bg = Buf()
        Bv = Buf()
        Bx1 = Buf()

        def run_pipeline(items):
            n = len(items)
            maxoff = max(o for it in items for (o, _) in it)
            for s_ in range(n + maxoff):
                for off in range(maxoff + 1):
                    t = s_ - off
                    if 0 <= t < n:
                        for (o, fn) in items[t]:
                            if o == off:
                                fn()

        def phase1(l, xsrc):
            with ExitStack() as e1:
                win = sb("win", [128, 8, IN_WIDTH], BF16, e1)
                Bwin = [Buf() for _ in range(8)]
                xts = Rot([(sb("xt%d" % i, [128, D_MODEL], F32, e1), Buf()) for i in range(3)])
                hns = Rot([(sb("hn%d" % i, [128, D_MODEL], BF16, e1), Buf()) for i in range(2)])
                hnTs = Rot([(sb("hnT%d" % i, [128, 8, 512], BF16, e1), Buf()) for i in range(2)])
                junk = sb("junk", [128, D_MODEL], BF16, e1)
                Bjunk = Buf()
                stat = Rot([(sb("stat%d" % i, [128, 4], F32, e1), Buf()) for i in range(3)])
                sqs = Rot([(sb("sq%d" % i, [128, 512], BF16, e1), Buf()) for i in range(3)])
                lns = Rot([(sb("ln%d" % i, [128, 512], F32, e1), Buf()) for i in range(3)])
                rss = Rot([(sb("rs%d" % i, [128, 512], F32, e1), Buf()) for i in range(3)])
                kn32 = Rot([(sb("kn%d" % i, [128, 512], F32, e1), Buf()) for i in range(2)])
                ost = Rot([(sb("ost%d" % i, [128, 512], BF16, e1), Buf()) for i in range(4)])
                vst = Rot([(sb("vst%d" % i, [128, 14, 192], BF16, e1), Buf()) for i in range(2)])
                for (t_, b_) in vst.items:
                    S.op("dve", lambda e: e.memset(t_[:], 1.0), writes=[b_])
                psF = Rot([(PS[i], BPS[i]) for i in (0, 1, 2)])
                psQ = Rot([(PS[i], BPS[i]) for i in (3,)])
                psT = Rot([(PS[i], BPS[i]) for i in (4, 5)])
                psV = Rot([(PS[i], BPS[i]) for i in (6, 7)])
                wv = w_in[l].rearrange("(c p) e -> p c e", p=128)
                wst = Rot([(sb("wst%d" % i, [128, 8, 512], F32, e1), Buf()) for i in range(2)])
                for cc in range(8):
                    c0 = cc * 512
                    cw = min(512, IN_WIDTH - c0)
                    t_, b_ = wst.next()
                    S.dma(lambda e: e.dma_start(out=t_[:, :, 0:cw], in_=wv[:, :, c0:c0 + cw]), writes=[b_])
                    S.op("dve" if cc % 2 == 0 else "pool",
                         lambda e: e.tensor_copy(out=win[:, :, c0:c0 + cw], in_=t_[:, :, 0:cw]),
                         reads=[b_], writes=[Bwin[cc]])
                S.op("dve", lambda e: e.memset(kmean[:], 0.0), writes=[Bkm])

                def wbufs(c0, w):
                    return [Bwin[k] for k in range(c0 // 512, (c0 + w - 1) // 512 + 1)]

                def prep_item(i, hnT, BhnT, j):
                    xt, bx = xts.next()
                    st, bst_ = stat.next()
                    hn, bhn = hns.next()
                    pt, bpt = psT.next()
                    ptb = pt[:].bitcast(BF16)

                    def f0():
                        S.dma(lambda e: e.dma_start(out=xt[:], in_=xsrc[i * 128:(i + 1) * 128, :]), reads=[Bx1], writes=[bx])
                        S.op("act", lambda e: e.activation(out=junk[:], in_=xt[:], func=AF.Square, accum_out=st[:, 0:1]),
                             reads=[bx], writes=[Bjunk, bst_])
                        S.op("act", lambda e: e.activation(out=st[:, 1:2], in_=st[:, 0:1], func=AF.Ln,
                                                           scale=1.0 / D_MODEL, bias=1e-6), reads=[bst_], writes=[bst_])
                        S.op("act", lambda e: e.activation(out=st[:, 2:3], in_=st[:, 1:2], func=AF.Exp, scale=-0.5),
                             reads=[bst_], writes=[bst_])
                        S.op("dve", lambda e: e.scalar_tensor_tensor(
                            out=hn[:], in0=xt[:], scalar=st[:, 2:3], in1=ngain[:, l, :], op0=ALU.mult, op1=ALU.mult),
                            reads=[bx, bst_, Bconst], writes=[bhn])

                    def f1():
                        for c in range(8):
                            S.op("pe", lambda e: e.transpose(ptb[:, c * 128:(c + 1) * 128], hn[:, c * 128:(c + 1) * 128],
                                                             ident_b[:]), reads=[bhn, Bconst], writes=[bpt])

                    def f2():
                        if j % 2 == 0:
                            S.op("act", lambda e: e.activation(
                                out=hnT[:, :, j * 128:(j + 1) * 128], in_=ptb.rearrange("p (c n) -> p c n", n=128),
                                func=AF.Copy), reads=[bpt], writes=[BhnT])
                        else:
                            S.op("dve", lambda e: e.tensor_copy(
                                out=hnT[:, :, j * 128:(j + 1) * 128], in_=ptb.rearrange("p (c n) -> p c n", n=128)),
                                reads=[bpt], writes=[BhnT])
                    return [(0, f0), (1, f1), (2, f2)]

                def qk_item(hnT, BhnT, g, col0, gj, dst_id, scale_only=False, is_mk=False):
                    M = 128
                    pp, bp = psF.next()
                    o_t, o_b = ost.next()
                    if not scale_only:
                        sq, bsq = sqs.next()
                        pq, bq = psQ.next()
                        ln_t, ln_b = lns.next()
                        rs_t, rs_b = rss.next()
                        if is_mk:
                            k32, bk32 = kn32.next()

                    def f0():
                        for c in range(8):
                            S.op("pe", lambda e: e.matmul(pp[0:M, :], lhsT=win[:, c, col0:col0 + M], rhs=hnT[:, c, :],
                                                          start=(c == 0), stop=(c == 7)),
                                 reads=wbufs(col0, M) + [BhnT], writes=[bp])

                    def store():
                        for hh in range(2):
                            S.dma(lambda e: e.dma_start(out=qk_scr[dst_id + hh, 0:64, g * 512:(g + 1) * 512],
                                                        in_=o_t[hh * 64:(hh + 1) * 64, :]),
                                  reads=[o_b], writes=[Bqk[dst_id + hh]])

                    def f1():
                        if scale_only:
                            sc = 0.125 if gj else 1.0
                            S.op("act", lambda e: e.activation(out=o_t[0:M, :], in_=pp[0:M, :], func=AF.Copy, scale=sc),
                                 reads=[bp], writes=[o_b])
                            store()
                        else:
                            S.op("act", lambda e: e.activation(out=sq[0:M, :], in_=pp[0:M, :], func=AF.Square),
                                 reads=[bp], writes=[bsq])

                    def f2():
                        if scale_only:
                            return
                        S.op("pe", lambda e: e.matmul(pq[0:M, :], lhsT=blk_b[0:M, 0:M], rhs=sq[0:M, :], start=True,
                                                      stop=True), reads=[bsq, Bconst], writes=[bq])
                        S.op("act", lambda e: e.activation(out=ln_t[0:M, :], in_=pq[0:M, :], func=AF.Ln,
                                                           scale=1.0 / 64.0, bias=1e-6), reads=[bq], writes=[ln_b])
                        S.op("act", lambda e: e.activation(out=rs_t[0:M, :], in_=ln_t[0:M, :], func=AF.Exp, scale=-0.5),
                             reads=[ln_b], writes=[rs_b])
                        if is_mk:
                            S.op("dve", lambda e: e.scalar_tensor_tensor(
                                out=k32[:, :], in0=pp[:, :], scalar=gcols[:, l, gj:gj + 1], in1=rs_t[:, :],
                                op0=ALU.mult, op1=ALU.mult), reads=[bp, rs_b, Bconst], writes=[bk32])
                            S.op("pool", lambda e: e.tensor_copy(out=o_t[:, :], in_=k32[:, :]), reads=[bk32],
                                 writes=[o_b])
                            pr = (dst_id - ID_MK) // 2
                            S.op("dve", lambda e: e.tensor_reduce(
                                out=kmean[:, pr, 2 * g:2 * g + 2], in_=k32[:, :].rearrange("p (a b) -> p a b", b=256),
                                axis=AX.X, op=ALU.add), reads=[bk32], writes=[Bkm])
                        else:
                            S.op("dve", lambda e: e.scalar_tensor_tensor(
                                out=o_t[0:M, :], in0=pp[0:M, :], scalar=gcols[0:M, l, gj:gj + 1], in1=rs_t[0:M, :],
                                op0=ALU.mult, op1=ALU.mult), reads=[bp, rs_b, Bconst], writes=[o_b])
                        store()
                    return [(0, f0), (1, f1), (2, f2)]

                def gate_item(hnT, BhnT, g, col0, gid):
                    pp, bp = psF.next()
                    ln_t, ln_b = lns.next()
                    rs_t, rs_b = rss.next()
                    o_t, o_b = ost.next()

                    def f0():
                        for c in range(8):
                            S.op("pe", lambda e: e.matmul(pp[:, :], lhsT=win[:, c, col0:col0 + 128], rhs=hnT[:, c, :],
                                                          start=(c == 0), stop=(c == 7)),
                                 reads=wbufs(col0, 128) + [BhnT], writes=[bp])

                    def f1():
                        S.op("act", lambda e: e.activation(out=ln_t[:, :], in_=pp[:, :], func=AF.Exp, scale=-1.0),
                             reads=[bp], writes=[ln_b])
                        S.op("pool", lambda e: e.tensor_scalar(out=ln_t[:, :], in0=ln_t[:, :], scalar1=1.0, scalar2=None,
                                                               op0=ALU.add), reads=[ln_b], writes=[ln_b])

                    def f2():
                        S.op("dve", lambda e: e.reciprocal(out=rs_t[:, :], in_=ln_t[:, :]), reads=[ln_b], writes=[rs_b])
                        S.op("dve", lambda e: e.tensor_tensor(out=o_t[:, :], in0=pp[:, :], in1=rs_t[:, :], op=ALU.mult),
                             reads=[bp, rs_b], writes=[o_b])
                        S.dma(lambda e: e.dma_start(out=g_scr[gid, :, g * 512:(g + 1) * 512], in_=o_t[:, :]),
                              reads=[o_b], writes=[Bg])
                    return [(0, f0), (1, f1), (2, f2)]

                def v_item(hnT, BhnT, g, j):
                    i = g * 4 + j
                    pa, ba = psV.next()
                    pb, bb = psV.next()
                    vt, bvt = vst.next()

                    def f0():
                        specs = [(pa, ba, 0, A_V, 256), (pa, ba, 256, B_V, 256), (pb, bb, 0, C_V, 256),
                                 (pb, bb, 256, D_V, 128), (pb, bb, 384, A_F, 4)]
                        for (pp, bp, o0, c0, w) in specs:
                            for c in range(8):
                                S.op("pe", lambda e: e.matmul(
                                    pp[:, o0:o0 + w], lhsT=hnT[:, c, j * 128:(j + 1) * 128], rhs=win[:, c, c0:c0 + w],
                                    start=(c == 0), stop=(c == 7)), reads=wbufs(c0, w) + [BhnT], writes=[bp])

                    def f1():
                        S.op("act", lambda e: e.activation(
                            out=vt[:, 0:8, 64:128], in_=pa[:, 0:512].rearrange("p (h d) -> p h d", d=64), func=AF.Copy),
                            reads=[ba], writes=[bvt])
                        S.op("dve", lambda e: e.tensor_copy(
                            out=vt[:, 8:14, 64:128], in_=pb[:, 0:384].rearrange("p (h d) -> p h d", d=64)),
                            reads=[bb], writes=[bvt])
                        S.op("dve", lambda e: e.tensor_tensor(out=xf_all[:, i, :], in0=pb[:, 384:388], in1=bfg[:, l, :],
                                                              op=ALU.add), reads=[bb, Bconst], writes=[BxF])
                        S.dma(lambda e: e.dma_start(out=v_scr[i * 128:(i + 1) * 128, :, :], in_=vt[:]), reads=[bvt],
                              writes=[Bv])
                    return [(0, f0), (1, f1)]

                items = []
                hn_bufs = [hnTs.next() for _ in range(2)]
                for j in range(4):
                    items.append(prep_item(j, hn_bufs[0][0], hn_bufs[0][1], j))
                items.append([])
                items.append([])
                for g in range(NG):
                    hnT, BhnT = hn_bufs[g % 2]
                    gi = []
                    for pr in range(2):
                        gi.append(qk_item(hnT, BhnT, g, A_Q + pr * 128, 0, ID_FQ + 2 * pr))
                        gi.append(qk_item(hnT, BhnT, g, A_K + pr * 128, 1, ID_FK + 2 * pr))
                        gi.append(gate_item(hnT, BhnT, g, A_G + pr * 128, pr))
                    gi.append(v_item(hnT, BhnT, g, 0))
                    for pr in range(2):
                        gi.append(qk_item(hnT, BhnT, g, B_Q + pr * 128, 2, ID_MQ + 2 * pr))
                        gi.append(qk_item(hnT, BhnT, g, B_K + pr * 128, 3, ID_MK + 2 * pr, is_mk=True))
                        gi.append(gate_item(hnT, BhnT, g, B_G + pr * 128, 2 + pr))
                        if pr == 0:
                            gi.append(v_item(hnT, BhnT, g, 1))
                    gi.append(v_item(hnT, BhnT, g, 2))
                    for pr in range(2):
                        gi.append(qk_item(hnT, BhnT, g, C_Q + pr * 128, 1, ID_SQ + 2 * pr, scale_only=True))
                        gi.append(qk_item(hnT, BhnT, g, C_K + pr * 128, 0, ID_SK + 2 * pr, scale_only=True))
                        gi.append(gate_item(hnT, BhnT, g, C_G + pr * 128, 4 + pr))
                    gi.append(v_item(hnT, BhnT, g, 3))
                    for pr in range(2):
                        gi.append(qk_item(hnT, BhnT, g, D_Q + pr * 128, 4, ID_WQ + 2 * pr))
                        gi.append(gate_item(hnT, BhnT, g, D_G + pr * 128, 6 + pr))
                    gi.append(qk_item(hnT, BhnT, g, D_K, 5, ID_WK))
                    if g + 1 < NG:
                        nh, nb = hn_bufs[(g + 1) % 2]
                        pos = [3, 9, 16, 22]
                        for j in range(4):
                            gi.insert(pos[j] + j, prep_item((g + 1) * 4 + j, nh, nb, j))
                    items.extend(gi)
                run_pipeline(items)
                S.barrier()

        def phase2(l, yT, ByT):
            with ExitStack() as e2:
                QAs = Rot([(sb("QA%d" % i, [KROWS, T], BF16, e2), Buf()) for i in range(2)])
                KAs = Rot([(sb("KA%d" % i, [KROWS, T], BF16, e2), Buf()) for i in range(2)])
                VAs = Rot([(sb("VA%d" % i, [128, NT, 192], BF16, e2), Buf()) for i in range(2)])
                GTs = Rot([(sb("GT%d" % i, [128, T], BF16, e2), Buf()) for i in range(2)])
                Ps = Rot([(sb("P%d" % i, [128, 512], BF16, e2), Buf()) for i in range(3)])
                rdn = Rot([(sb("rdn%d" % i, [128, 512], F32, e2), Buf()) for i in range(2)])
                tmo = Rot([(sb("tmo%d" % i, [128, 512], F32, e2), Buf()) for i in range(2)])
                psS = Rot([(PS[i], BPS[i]) for i in (0, 1, 2)])
                psO = Rot([(PS[i], BPS[i]) for i in (3, 4)])
                psM = Rot([(PS[i], BPS[i]) for i in (5, 6)])
                psC = Rot([(PS[i], BPS[i]) for i in (7,)])

                def load_head(qid, kid, vh, nq, nk=None):
                    nk = nq if nk is None else nk
                    QA, bQ = QAs.next()
                    KA, bK = KAs.next()
                    VA, bV = VAs.next()
                    S.dma(lambda e: e.dma_start(out=QA[0:nq, :], in_=qk_scr[qid, 0:nq, :]), reads=[Bqk[qid]],
                          writes=[bQ])
                    S.dma(lambda e: e.dma_start(out=KA[0:nk, :], in_=qk_scr[kid, 0:nk, :]), reads=[Bqk[kid]],
                          writes=[bK])
                    S.dma(lambda e: e.dma_start(out=VA[:], in_=v_scr[:, vh, :].rearrange("(n p) d -> p n d", p=128)),
                          reads=[Bv], writes=[bV])
                    return QA, bQ, KA, bK, VA, bV

                def load_gate(gid):
                    GT, bG = GTs.next()
                    S.dma(lambda e: e.dma_start(out=GT[:], in_=g_scr[gid]), reads=[Bg], writes=[bG])
                    return GT, bG

                def finalize(O, bO, par, GT, bG, chunk, g, normalize, extra_den=None):
                    orow = slice(0, 64) if par == 0 else slice(64, 128)
                    drow = slice(64, 128) if par == 0 else slice(0, 64)
                    cols = slice(g * 512, (g + 1) * 512)
                    if normalize:
                        r_t, r_b = rdn.next()
                        if extra_den is not None:
                            S.op("dve", lambda e: e.tensor_scalar(out=r_t[orow, :], in0=O[drow, :], scalar1=extra_den[drow],
                                                                  scalar2=None, op0=ALU.add),
                                 reads=[bO, Bconst], writes=[r_b])
                            S.op("dve", lambda e: e.reciprocal(out=r_t[orow, :], in_=r_t[orow, :]), reads=[r_b],
                                 writes=[r_b])
                        else:
                            S.op("dve", lambda e: e.reciprocal(out=r_t[orow, :], in_=O[drow, :]), reads=[bO],
                                 writes=[r_b])
                        t_t, t_b = tmo.next()
                        S.op("dve", lambda e: e.tensor_tensor(out=t_t[orow, :], in0=O[orow, :], in1=r_t[orow, :],
                                                              op=ALU.mult), reads=[bO, r_b], writes=[t_b])
                        S.op("pool", lambda e: e.tensor_tensor(out=yT[orow, chunk, cols], in0=t_t[orow, :],
                                                               in1=GT[orow, cols], op=ALU.mult),
                             reads=[t_b, bG], writes=[ByT])
                    else:
                        S.op("dve", lambda e: e.tensor_tensor(out=yT[orow, chunk, cols], in0=O[orow, :],
                                                              in1=GT[orow, cols], op=ALU.mult),
                             reads=[bO, bG], writes=[ByT])

                def vslice(VA, ki, par):
                    return VA[:, ki, 64:192] if par == 0 else VA[:, ki, 0:128]

                with ExitStack() as ef:
                    NF = NT * 4
                    xf2 = xf_all[:].rearrange("p n h -> p (n h)")
                    S.op("act", lambda e: e.activation(out=xf2, in_=xf2, func=AF.Exp, scale=-1.0), reads=[BxF], writes=[BxF])
                    S.op("act", lambda e: e.activation(out=xf2, in_=xf2, func=AF.Ln, bias=1.0), reads=[BxF], writes=[BxF])
                    S.op("dve", lambda e: e.tensor_scalar(out=xf2, in0=xf2, scalar1=-1.0, scalar2=None, op0=ALU.mult),
                         reads=[BxF], writes=[BxF])
                    p1, b1 = psM.next()
                    p2, b2 = psM.next()
                    S.op("pe", lambda e: e.matmul(p1[:, 0:NF], lhsT=tri_f[:], rhs=xf2, start=True, stop=True),
                         reads=[BxF, Bconst], writes=[b1])
                    S.op("pe", lambda e: e.matmul(p2[:, 0:NF], lhsT=ones_f[:], rhs=xf2, start=True, stop=True),
                         reads=[BxF, Bconst], writes=[b2])
                    tot = sb("f_tot", [128, NT, 4], F32, ef)
                    exc = sb("f_exc", [128, NT, 4], F32, ef)
                    Bt = Buf()
                    S.op("dve", lambda e: e.tensor_copy(out=tot[:].rearrange("p n h -> p (n h)"), in_=p2[:, 0:NF]),
                         reads=[b2], writes=[Bt])
                    S.op("dve", lambda e: e.memset(exc[:, 0, :], 0.0), writes=[Bt])
                    for i in range(1, NT):
                        S.op("dve", lambda e, i=i: e.tensor_tensor(out=exc[:, i, :], in0=exc[:, i - 1, :],
                                                                   in1=tot[:, i - 1, :], op=ALU.add),
                             reads=[Bt], writes=[Bt])
                    S.op("dve", lambda e: e.tensor_tensor(out=c_all[:].rearrange("p n h -> p (n h)"), in0=p1[:, 0:NF],
                                                          in1=exc[:].rearrange("p n h -> p (n h)"), op=ALU.add),
                         reads=[b1, Bt], writes=[Bc])
                    spl = sb("f_spl", [128, NT, 4, 6], BF16, ef)
                    r1 = sb("f_r1", [128, NT, 4], F32, ef)
                    r2 = sb("f_r2", [128, NT, 4], F32, ef)
                    Bs = Buf()
                    S.op("dve", lambda e: e.tensor_copy(out=spl[:, :, :, 0], in_=c_all[:]), reads=[Bc], writes=[Bs])
                    S.op("dve", lambda e: e.tensor_tensor(out=r1[:], in0=c_all[:], in1=spl[:, :, :, 0], op=ALU.subtract),
                         reads=[Bc, Bs], writes=[Bs])
                    S.op("dve", lambda e: e.tensor_copy(out=spl[:, :, :, 1], in_=r1[:]), reads=[Bs], writes=[Bs])
                    S.op("dve", lambda e: e.tensor_tensor(out=r2[:], in0=r1[:], in1=spl[:, :, :, 1], op=ALU.subtract),
                         reads=[Bs], writes=[Bs])
                    S.op("dve", lambda e: e.tensor_copy(out=spl[:, :, :, 2], in_=r2[:]), reads=[Bs], writes=[Bs])
                    S.op("dve", lambda e: e.tensor_scalar(out=spl[:, :, :, 3:6], in0=spl[:, :, :, 0:3], scalar1=-1.0,
                                                          scalar2=None, op0=ALU.mult), reads=[Bs], writes=[Bs])
                    augs = Rot([(sb("f_aug%d" % i, [KROWS, 512], BF16, ef), Buf()) for i in range(2)])
                    for h in range(4):
                        for g in range(NG):
                            pm, bm = psM.next()
                            for j in range(4):
                                i = g * 4 + j
                                S.op("pe", lambda e, pm=pm, i=i, j=j, h=h: e.matmul(
                                    pm[64:70, j * 128:(j + 1) * 128], lhsT=spl[:, i, h, :], rhs=ident_b[:],
                                    start=True, stop=True), reads=[Bs, Bconst], writes=[bm])
                            a_t, a_b = augs.next()
                            S.op("act", lambda e, pm=pm, a_t=a_t: e.activation(out=a_t[64:70, :], in_=pm[64:70, :],
                                                                               func=AF.Copy), reads=[bm], writes=[a_b])
                            S.dma(lambda e, a_t=a_t, h=h, g=g: e.dma_start(
                                out=qk_scr[ID_FQ + h, 64:67, g * 512:(g + 1) * 512], in_=a_t[64:67, :]),
                                reads=[a_b], writes=[Bqk[ID_FQ + h]])
                            S.dma(lambda e, a_t=a_t, h=h, g=g: e.dma_start(
                                out=qk_scr[ID_FK + h, 67:70, g * 512:(g + 1) * 512], in_=a_t[67:70, :]),
                                reads=[a_b], writes=[Bqk[ID_FK + h]])

                S.barrier()
                def softmax_tile(QA, qbufs, KA, bK, VA, bV, O, bO, g, ki, par, krows, bias_tiles, act_bias, fin):
                    j = max(ki - 4 * g, 0)
                    c0 = j * 128
                    last = 4 * g + 3
                    S_, bS = psS.next()
                    P, bP = Ps.next()

                    def f0():
                        S.op("pe", lambda e: e.matmul(
                            S_[:, c0:512], lhsT=KA[0:krows, ki * 128:(ki + 1) * 128],
                            rhs=QA[0:krows, g * 512 + c0:(g + 1) * 512], start=True, stop=True),
                            reads=list(qbufs) + [bK], writes=[bS])
                        if bias_tiles is None:
                            if ki >= 4 * g:
                                S.op("pe", lambda e: e.matmul(
                                    S_[:, c0:c0 + 128], lhsT=ident_b[:], rhs=mask_b[:, 0, :], start=False, stop=True),
                                    reads=[Bconst], writes=[bS])
                        else:
                            for jj in range(j, 4):
                                dl = 4 * g + jj - ki
                                if dl < 8:
                                    S.op("pe", lambda e: e.matmul(
                                        S_[:, jj * 128:(jj + 1) * 128], lhsT=ident_b[:], rhs=bias_tiles[:, dl, :],
                                        start=False, stop=True), reads=[Bconst], writes=[bS])

                    def f1():
                        if act_bias is None:
                            S.op("act", lambda e: e.activation(out=P[:, c0:512], in_=S_[:, c0:512], func=AF.Exp),
                                 reads=[bS], writes=[bP])
                        else:
                            S.op("act", lambda e: e.activation(out=P[:, c0:512], in_=S_[:, c0:512], func=AF.Exp,
                                                               bias=act_bias), reads=[bS, Bconst], writes=[bP])

                    def f2():
                        S.op("pe", lambda e: e.matmul(
                            O[:, c0:512], lhsT=vslice(VA, ki, par), rhs=P[:, c0:512], start=(ki == 0),
                            stop=(ki == last)), reads=[bP, bV], writes=[bO])
                        if ki == last:
                            fin()
                    return [(0, f0), (1, f1), (2, f2)]

                def softmax_head(QA, bQ, KA, bK, VA, bV, GT, bG, par, chunk, krows, bias_tiles=None, act_bias=None,
                                 qaug=None, hooks=None):
                    items = []
                    for g in range(NG):
                        O, bO = psO.next()
                        qbufs = [bQ] if qaug is None else [bQ, qaug[g % 2]]

                        def fin(O=O, bO=bO, g=g):
                            finalize(O, bO, par, GT, bG, chunk, g, True)
                        for ki in range(4 * g + 4):
                            it = softmax_tile(QA, qbufs, KA, bK, VA, bV, O, bO, g, ki, par, krows, bias_tiles, act_bias,
                                              fin)
                            if hooks is not None:
                                if ki == 0 and (g, 0) in hooks:
                                    it.insert(0, (0, hooks[(g, 0)]))
                                if ki == 4 * g + 3 and (g, 1) in hooks:
                                    it.append((0, hooks[(g, 1)]))
                            items.append(it)
                    run_pipeline(items)

                for h in range(4):
                    QA, bQ, KA, bK, VA, bV = load_head(ID_FQ + h, ID_FK + h, h, 70, 70)
                    if h % 2 == 0:
                        GT, bG = load_gate(h // 2)
                    softmax_head(QA, bQ, KA, bK, VA, bV, GT, bG, h % 2, h // 2, 70)

                with ExitStack() as em:
                    km_b = sb("km_b", [64, 4, NB], BF16, em)
                    Bkmb = Buf()
                    for h in range(4):
                        S.op("dve", lambda e: e.tensor_copy(out=km_b[0:64, h, :],
                                                            in_=kmean[(h % 2) * 64:(h % 2) * 64 + 64, h // 2, :]),
                             reads=[Bkm], writes=[Bkmb])
                    gms = Rot([(sb("gm%d" % i, [128, 4, 16], F32, em), Buf()) for i in range(2)])
                    t8s = Rot([(sb("t8%d" % i, [128, 4, 8], F32, em), Buf()) for i in range(2)])
                    msl = Rot([(sb("msl%d" % i, [128, 4, 16], BF16, em), Buf()) for i in range(2)])

                    def make_sel(QA, bQ, qaug, h, g):
                        st = {}

                        def part1():
                            pg, bg_ = psM.next()
                            for j in range(4):
                                i = g * 4 + j
                                S.op("pe", lambda e: e.matmul(
                                    pg[:, j * 16:j * 16 + NB], lhsT=QA[0:64, i * 128:(i + 1) * 128], rhs=km_b[0:64, h, :],
                                    start=True, stop=True), reads=[bQ, Bkmb], writes=[bg_])
                            gm, bgm = gms.next()
                            S.op("pool", lambda e: e.memset(gm[:], -1e30), writes=[bgm])
                            for j in range(4):
                                qb = (g * 4 + j) // 2
                                if qb > 0:
                                    S.op("dve", lambda e: e.tensor_copy(out=gm[:, j, 0:qb], in_=pg[:, j * 16:j * 16 + qb]),
                                         reads=[bg_], writes=[bgm])
                            t8, bt8 = t8s.next()
                            for j in range(4):
                                S.op("dve", lambda e: e.max(out=t8[:, j, :], in_=gm[:, j, :]), reads=[bgm], writes=[bt8])
                            ms, bms = msl.next()
                            for j in range(4):
                                S.op("dve", lambda e: e.tensor_scalar(
                                    out=ms[:, j, :], in0=gm[:, j, :], scalar1=t8[:, j, 2:3], scalar2=NEG, op0=ALU.is_lt,
                                    op1=ALU.mult), reads=[bgm, bt8], writes=[bms])
                                qb = (g * 4 + j) // 2
                                S.op("dve", lambda e: e.memset(ms[:, j, qb:qb + 1], 0.0), writes=[bms])
                            st["ms"] = (ms, bms)

                        def part2():
                            ms, bms = st["ms"]
                            pm, bm = psM.next()
                            for j in range(4):
                                S.op("pe", lambda e: e.matmul(
                                    pm[64:80, j * 128:(j + 1) * 128], lhsT=ms[:, j, :], rhs=ident_b[:], start=True,
                                    stop=True), reads=[bms, Bconst], writes=[bm])
                            S.op("act", lambda e: e.activation(out=QA[64:80, g * 512:(g + 1) * 512], in_=pm[64:80, :],
                                                               func=AF.Copy), reads=[bm], writes=[qaug[g % 2]])
                        return part1, part2

                    for h in range(4):
                        QA, bQ, KA, bK, VA, bV = load_head(ID_MQ + h, ID_MK + h, 4 + h, 64, 80)
                        if h % 2 == 0:
                            GT, bG = load_gate(2 + h // 2)
                        qaug = [Buf(), Buf()]
                        hooks = {}
                        p1, p2 = make_sel(QA, bQ, qaug, h, 0)
                        p1()
                        p2()
                        for g in range(NG - 1):
                            p1, p2 = make_sel(QA, bQ, qaug, h, g + 1)
                            hooks[(g, 0)] = p1
                            hooks[(g, 1)] = p2
                        softmax_head(QA, bQ, KA, bK, VA, bV, GT, bG, h % 2, 2 + h // 2, 80,
                                     bias_tiles=biasM[:, h, :, :], act_bias=rb31_t[:, h:h + 1], qaug=qaug, hooks=hooks)

                S.barrier()
                with ExitStack() as esb:
                    Es = Rot([(sb("sbE%d" % i, [128, 512], F32, esb), Buf()) for i in range(3)])
                    SPs = Rot([(sb("sbSP%d" % i, [128, 512], BF16, esb), Buf()) for i in range(3)])
                    ARs = Rot([(sb("sbAR%d" % i, [128, 512], F32, esb), Buf()) for i in range(3)])
                    carry = sb("sbcarry", [128, 512], F32, esb)
                    Bcar = Buf()
                    psT3 = Rot([(PS[i], BPS[i]) for i in (5, 6, 7)])

                    def sb_tile(QA, bQ, KA, bK, VA, bV, O, bO, g, ki, par, fin):
                        j = max(ki - 4 * g, 0)
                        c0 = j * 128
                        first = 4 * g + 3
                        Z, bZ = psS.next()
                        E, bE = Es.next()
                        SP, bSP = SPs.next()
                        Tb, bT = psT3.next()
                        AR, bAR = ARs.next()
                        P, bP = Ps.next()

                        def fA():
                            S.op("pe", lambda e: e.matmul(
                                Z[:, c0:512], lhsT=KA[0:64, ki * 128:(ki + 1) * 128],
                                rhs=QA[0:64, g * 512 + c0:(g + 1) * 512], start=True, stop=True),
                                reads=[bQ, bK], writes=[bZ])
                            if ki >= 4 * g:
                                S.op("pe", lambda e: e.matmul(
                                    Z[:, c0:c0 + 128], lhsT=ident_b[:], rhs=mask_b[:, 1, :], start=False, stop=True),
                                    reads=[Bconst], writes=[bZ])

                        def fB():
                            S.op("act", lambda e: e.activation(out=E[:, c0:512], in_=Z[:, c0:512], func=AF.Exp),
                                 reads=[bZ], writes=[bE])
                            S.op("act", lambda e: e.activation(out=SP[:, c0:512], in_=E[:, c0:512], func=AF.Ln, bias=1.0),
                                 reads=[bE], writes=[bSP])

                        def fC():
                            S.op("pe", lambda e: e.matmul(Z[:, c0:512], lhsT=negU_b[:], rhs=SP[:, c0:512], start=False,
                                                          stop=True), reads=[bSP, Bconst], writes=[bZ])
                            S.op("pe", lambda e: e.matmul(Tb[:, c0:512], lhsT=ones_b[:], rhs=SP[:, c0:512], start=True,
                                                          stop=True), reads=[bSP, Bconst], writes=[bT])

                        def fD():
                            if ki == first:
                                S.op("dve", lambda e: e.memset(carry[:], 0.0), writes=[Bcar])
                            S.op("dve", lambda e: e.tensor_tensor(out=AR[:, c0:512], in0=Z[:, c0:512],
                                                                  in1=carry[:, c0:512], op=ALU.subtract),
                                 reads=[bZ, Bcar], writes=[bAR])
                            S.op("dve", lambda e: e.tensor_tensor(out=carry[:, c0:512], in0=Tb[:, c0:512],
                                                                  in1=carry[:, c0:512], op=ALU.add),
                                 reads=[bT, Bcar], writes=[Bcar])

                        def fE():
                            S.op("act", lambda e: e.activation(out=P[:, c0:512], in_=AR[:, c0:512], func=AF.Exp),
                                 reads=[bAR], writes=[bP])

                        def fF():
                            lh = VA[:, ki, 64:128] if par == 0 else VA[:, ki, 0:128]
                            orows = slice(0, 64) if par == 0 else slice(0, 128)
                            S.op("pe", lambda e: e.matmul(O[orows, c0:512], lhsT=lh, rhs=P[:, c0:512],
                                                          start=(ki == first), stop=(ki == 0)),
                                 reads=[bP, bV], writes=[bO])
                            if ki == 0:
                                fin()
                        return [(0, fA), (0, fB), (1, fC), (1, fD), (2, fE), (3, fF)]

                    for h in range(4):
                        QA, bQ, KA, bK, VA, bV = load_head(ID_SQ + h, ID_SK + h, 8 + h, 64, 64)
                        par = h % 2
                        if par == 0:
                            GT, bG = load_gate(4 + h // 2)
                        items = []
                        for g in range(NG):
                            O, bO = psO.next()

                            def fin(O=O, bO=bO, g=g, par=par, GT=GT, bG=bG, h=h):
                                finalize(O, bO, par, GT, bG, 4 + h // 2, g, False)
                            for ki in range(4 * g + 3, -1, -1):
                                items.append(sb_tile(QA, bQ, KA, bK, VA, bV, O, bO, g, ki, par, fin))
                        run_pipeline(items)

                S.barrier()
                KA = None
                for h in range(4):
                    kv = h // 2
                    par = h % 2
                    if par == 0:
                        QA, bQ, KA, bK, VA, bV = load_head(ID_WQ + h, ID_WK + kv, 12 + kv, 64)
                        GT, bG = load_gate(6 + h // 2)
                    else:
                        QA, bQ = QAs.next()
                        S.dma(lambda e, QA=QA, h=h: e.dma_start(out=QA[0:64, :], in_=qk_scr[ID_WQ + h, 0:64, :]),
                              reads=[Bqk[ID_WQ + h]], writes=[bQ])
                    for g in range(NG):
                        O, bO = psO.next()
                        for jp in range(2):
                            S_, bS = psS.next()
                            for jq in range(2):
                                j = jp * 2 + jq
                                qi = g * 4 + j
                                qcols = slice(qi * 128, (qi + 1) * 128)
                                for dl in (1, 0):
                                    ki = qi - dl
                                    if ki < 0:
                                        continue
                                    sc = slice(jq * 256 + (1 - dl) * 128, jq * 256 + (1 - dl) * 128 + 128)
                                    S.op("pe", lambda e, S_=S_, sc=sc, ki=ki, qcols=qcols, QA=QA, KA=KA: e.matmul(
                                        S_[:, sc], lhsT=KA[0:64, ki * 128:(ki + 1) * 128], rhs=QA[0:64, qcols],
                                        start=True, stop=True), reads=[bQ, bK], writes=[bS])
                                    S.op("pe", lambda e, S_=S_, sc=sc, dl=dl, h=h: e.matmul(
                                        S_[:, sc], lhsT=ident_b[:], rhs=biasW[:, h, dl, :], start=False, stop=True),
                                        reads=[Bconst], writes=[bS])
                            P, bP = Ps.next()
                            a0 = 128 if (g == 0 and jp == 0) else 0
                            S.op("act", lambda e, S_=S_, P=P, a0=a0: e.activation(out=P[:, a0:512], in_=S_[:, a0:512],
                                                                                  func=AF.Exp), reads=[bS], writes=[bP])
                            for jq in range(2):
                                j = jp * 2 + jq
                                qi = g * 4 + j
                                dls = [d_ for d_ in (1, 0) if qi - d_ >= 0]
                                for n_, dl in enumerate(dls):
                                    ki = qi - dl
                                    sc = slice(jq * 256 + (1 - dl) * 128, jq * 256 + (1 - dl) * 128 + 128)
                                    S.op("pe", lambda e, O=O, P=P, sc=sc, ki=ki, j=j, n_=n_, dls=dls, VA=VA, par=par: e.matmul(
                                        O[:, j * 128:(j + 1) * 128], lhsT=vslice(VA, ki, par), rhs=P[:, sc],
                                        start=(n_ == 0), stop=(n_ == len(dls) - 1)), reads=[bP, bV], writes=[bO])
                        finalize(O, bO, par, GT, bG, 6 + h // 2, g, True, extra_den=esink[:, l, h:h + 1])
                S.barrier()

        def phase3(l, yT, ByT, xsrc, dst, Bdst):
            with ExitStack() as e3:
                wo = sb("wo", [128, 8, D_MODEL], BF16, e3)
                Bwo = Buf()
                wst = Rot([(sb("wost%d" % i, [128, 8, 512], F32, e3), Buf()) for i in range(2)])
                wv = w_out[l].rearrange("(c p) e -> p c e", p=128)
                for cc in range(2):
                    t_, b_ = wst.next()
                    S.dma(lambda e, t_=t_, cc=cc: e.dma_start(out=t_[:], in_=wv[:, :, cc * 512:(cc + 1) * 512]), writes=[b_])
                    S.op("dve" if cc == 0 else "pool",
                         lambda e, t_=t_, cc=cc: e.tensor_copy(out=wo[:, :, cc * 512:(cc + 1) * 512], in_=t_[:]),
                         reads=[b_], writes=[Bwo])
                xts = Rot([(sb("x3t%d" % i, [128, D_MODEL], F32, e3), Buf()) for i in range(3)])
                ots = Rot([(sb("o3t%d" % i, [128, D_MODEL], F32, e3), Buf()) for i in range(3)])
                psA = Rot([(PS[i], BPS[i]) for i in range(8)])
                for i in range(NT):
                    xt, bx = xts.next()
                    S.dma(lambda e, xt=xt, i=i: e.dma_start(out=xt[:], in_=xsrc[i * 128:(i + 1) * 128, :]),
                          reads=[Bx1], writes=[bx])
                    ot, bo = ots.next()
                    for half in range(2):
                        pp, bp = psA.next()
                        for c in range(8):
                            S.op("pe", lambda e, pp=pp, c=c, i=i, half=half: e.matmul(
                                pp[:, :], lhsT=yT[:, c, i * 128:(i + 1) * 128], rhs=wo[:, c, half * 512:(half + 1) * 512],
                                start=(c == 0), stop=(c == 7)), reads=[ByT, Bwo], writes=[bp])
                        S.op("dve", lambda e, pp=pp, xt=xt, ot=ot, half=half: e.tensor_tensor(
                            out=ot[:, half * 512:(half + 1) * 512], in0=pp[:, :], in1=xt[:, half * 512:(half + 1) * 512],
                            op=ALU.add), reads=[bp, bx], writes=[bo])
                    S.dma(lambda e, ot=ot, i=i: e.dma_start(out=dst[i * 128:(i + 1) * 128, :], in_=ot[:]),
                          reads=[bo], writes=[Bdst])
                S.barrier()

        Bout = Buf()
        for l in range(L):
            xsrc = x_in if l == 0 else x1
            dst = out if l == L - 1 else x1
            phase1(l, xsrc)
            with ExitStack() as ey:
                yT = sb("yT", [128, 8, T], BF16, ey)
                ByT = Buf()
                phase2(l, yT, ByT)
                if dbg and l == 0:
                    with ExitStack() as ed:
                        dt_ = sb("dbgt", [128, T], F32, ed)
                        Bd = Buf()
                        for c in range(8):
                            S.op("dve", lambda e, c=c: e.tensor_copy(out=dt_[:], in_=yT[:, c, :]), reads=[ByT], writes=[Bd])
                            S.dma(lambda e, c=c: e.dma_start(out=dbg_y[c], in_=dt_[:]), reads=[Bd], writes=[Bout])
                        S.barrier()
                phase3(l, yT, ByT, xsrc, dst, Bx1 if dst is x1 else Bout)
        S.final_wait()
        block = es.enter_context(nc.Block())
        S.emit(block)
    return nc


def _rel_bucket(dist):
    d = np.maximum(dist, 0)
    lr = np.log(np.maximum(d, 1).astype(np.float32) / 16) / math.log(1024 / 16)
    large = 16 + (lr * 16).astype(np.int32)
    large = np.minimum(large, 31)
    return np.where(d < 16, d, large)


def host_consts(T):
    k = np.arange(128)[:, None]
    q = np.arange(128)[None, :]
    cst = np.zeros((6, 128, 128), np.float32)
    cst[0] = np.eye(128, dtype=np.float32)
    cst[1] = (k <= q).astype(np.float32)
    cst[2] = -(k >= q).astype(np.float32)
    cst[3] = ((k // 64) == (q // 64)).astype(np.float32)
    cst[4] = np.where(k <= q, 0.0, NEG).astype(np.float32)
    cst[5] = np.where(k < q, 0.0, NEG).astype(np.float32)
    onehot = (np.arange(T)[None, :] // 256 == np.arange(16)[:, None]).astype(np.float32)
    idx_m = np.stack([_rel_bucket(dl * 128 + q - k) for dl in range(8)])
    idx_w = np.stack([_rel_bucket(dl * 128 + q - k) for dl in range(2)])
    return cst, onehot, idx_m, idx_w


_CACHE = {}


def kernel(x, norm_gain, w_in, b_forget, fox_qk_gain, moba_qk_gain, swa_qk_gain, sinks, w_out, rel_bias):
    x = np.asarray(x, dtype=np.float32)
    B, T, _ = x.shape
    if T not in _CACHE:
        _CACHE[T] = build(T)
    nc = _CACHE[T]
    cst, onehot, idx_m, idx_w = host_consts(T)
    rel_bias = np.asarray(rel_bias, dtype=np.float32)
    bias_moba = np.ascontiguousarray(np.stack([rel_bias[idx_m, h] for h in range(4)]))
    bias_swa = np.ascontiguousarray(np.stack([rel_bias[idx_w, 4 + h] for h in range(4)]))
    qk_gain = np.ascontiguousarray(np.concatenate(
        [np.asarray(fox_qk_gain, np.float32), np.asarray(moba_qk_gain, np.float32),
         np.asarray(swa_qk_gain, np.float32)], axis=1))
    shared = {
        "norm_gain": np.ascontiguousarray(np.asarray(norm_gain, np.float32)),
        "w_in": np.ascontiguousarray(np.asarray(w_in, np.float32)),
        "b_forget": np.ascontiguousarray(np.asarray(b_forget, np.float32)),
        "qk_gain": qk_gain,
        "sinks": np.ascontiguousarray(np.asarray(sinks, np.float32)),
        "w_out": np.ascontiguousarray(np.asarray(w_out, np.float32)),
        "rb31": np.ascontiguousarray(rel_bias[31:32, :]),
        "bias_moba": bias_moba, "bias_swa": bias_swa, "cst": cst, "onehot": onehot,
    }
    in_maps = []
    for b in range(B):
        m = dict(shared)
        m["x"] = np.ascontiguousarray(x[b])
        in_maps.append(m)
    res = run_bass_kernel_spmd(nc, in_maps, core_ids=list(range(B)))
    return np.stack([np.asarray(r["out"], dtype=np.float32) for r in res.results], axis=0)
```

```python
import math
import numpy as np
from contextlib import ExitStack
import concourse.bass as bass
import concourse.mybir as mybir
from concourse.bass_utils import run_bass_kernel_spmd

F32 = mybir.dt.float32
BF16 = mybir.dt.bfloat16
ALU = mybir.AluOpType
AF = mybir.ActivationFunctionType
AX = mybir.AxisListType

D_MODEL = 1024
DEPTH = 2
HD = 64
IN_WIDTH = 3844
NEG = -30000.0
A_Q, A_K, A_V, A_F, A_G = 0, 256, 512, 768, 772
B_Q, B_K, B_V, B_G = 1028, 1284, 1540, 1796
C_Q, C_K, C_V, C_G = 2052, 2308, 2564, 2820
D_Q, D_K, D_V, D_G = 3076, 3332, 3460, 3588
ID_FQ, ID_FK, ID_MQ, ID_MK, ID_SQ, ID_SK, ID_WQ, ID_WK = 0, 4, 8, 12, 16, 20, 24, 28
N_QK = 30
KROWS = 80


class Buf:
    __slots__ = ("w", "r")

    def __init__(self):
        self.w = {}
        self.r = {}


class Sched:
    CE = ("pe", "act", "dve", "pool")

    def __init__(self, nc, es, n_dma_sems=16):
        self.nc = nc
        self.sems = {}
        for e in self.CE:
            self.sems[e] = es.enter_context(nc.semaphore("s_" + e))
        self.dma_names = ["d%d" % i for i in range(n_dma_sems)]
        for d in self.dma_names:
            self.sems[d] = es.enter_context(nc.semaphore("s_" + d))
        self.dma_cnt = {d: 0 for d in self.dma_names}
        self.dma_rr = 0
        self.streams = {e: [] for e in ("pe", "act", "dve", "pool", "sp")}
        self.n = {e: 0 for e in self.CE}
        self.seen = {e: {} for e in self.streams}
        self.needed = {e: set() for e in self.CE}
        self.pending = {e: {} for e in self.streams}

    def _deps(self, reads, writes, q):
        deps = dict(self.pending[q])
        self.pending[q] = {}
        for b in reads:
            for s, v in b.w.items():
                if deps.get(s, 0) < v:
                    deps[s] = v
        for b in writes:
            for s, v in b.w.items():
                if deps.get(s, 0) < v:
                    deps[s] = v
            for s, v in b.r.items():
                if deps.get(s, 0) < v:
                    deps[s] = v
        return deps

    def _waits(self, q, deps):
        waits = []
        seen = self.seen[q]
        for s, v in deps.items():
            if s == "pe" and q == "pe":
                continue
            if seen.get(s, 0) < v:
                waits.append((s, v))
                seen[s] = v
                if s in self.needed:
                    self.needed[s].add(v)
        return waits

    def _mark(self, sem, val, reads, writes):
        for b in reads:
            if b.r.get(sem, 0) < val:
                b.r[sem] = val
        for b in writes:
            if b.w.get(sem, 0) < val:
                b.w[sem] = val

    def op(self, eng, fn, reads=(), writes=()):
        deps = self._deps(reads, writes, eng)
        waits = self._waits(eng, deps)
        self.n[eng] += 1
        idx = self.n[eng]
        self.streams[eng].append((waits, _record(fn), eng, idx, False))
        self._mark(eng, idx, reads, writes)

    def dma(self, fn, reads=(), writes=(), q="sp"):
        deps = self._deps(reads, writes, q)
        d = self.dma_names[self.dma_rr]
        self.dma_rr = (self.dma_rr + 1) % len(self.dma_names)
        if self.dma_cnt[d] > 0 and deps.get(d, 0) < self.dma_cnt[d]:
            deps[d] = self.dma_cnt[d]
        waits = self._waits(q, deps)
        self.dma_cnt[d] += 16
        val = self.dma_cnt[d]
        self.streams[q].append((waits, _record(fn), d, val, True))
        self._mark(d, val, reads, writes)

    def barrier(self):
        snap = {e: self.n[e] for e in self.CE if self.n[e] > 0}
        for d in self.dma_names:
            if self.dma_cnt[d] > 0:
                snap[d] = self.dma_cnt[d]
        for q in self.pending:
            p = self.pending[q]
            for s, v in snap.items():
                if p.get(s, 0) < v:
                    p[s] = v

    def final_wait(self, q="sp"):
        self.barrier()
        deps = dict(self.pending[q])
        self.pending[q] = {}
        waits = self._waits(q, deps)
        self.streams[q].append((waits, None, None, None, False))

    def emit(self, block):
        cmap = {}
        for e in self.CE:
            ks = sorted(self.needed[e])
            cmap[e] = {k: i + 1 for i, k in enumerate(ks)}
        sems = self.sems
        streams = self.streams

        def runner(ename):
            def body(engobj):
                for (waits, fn, sem, idx, is_dma) in streams[ename]:
                    for (s, v) in waits:
                        vv = cmap[s][v] if s in cmap else v
                        engobj.wait_ge(sems[s], vv)
                    if fn is None:
                        continue
                    ins = getattr(engobj, fn[0])(*fn[1], **fn[2])
                    if is_dma:
                        ins.then_inc(sems[sem], 16)
                    elif idx in cmap[sem]:
                        ins.then_inc(sems[sem], 1)
            return body
        block.tensor(runner("pe"))
        block.scalar(runner("act"))
        block.vector(runner("dve"))
        block.gpsimd(runner("pool"))
        block.sync(runner("sp"))


class _Rec:
    def __init__(self):
        self.call = None

    def __getattr__(self, name):
        def f(*a, **k):
            self.call = (name, a, k)
            return None
        return f


def _record(fn):
    r = _Rec()
    fn(r)
    assert r.call is not None
    return r.call


class Rot:
    def __init__(self, items):
        self.items = items
        self.i = 0

    def next(self):
        it = self.items[self.i]
        self.i = (self.i + 1) % len(self.items)
        return it


def build(T, L=DEPTH, dbg=False):
    NT = T // 128
    NG = T // 512
    NB = T // 256
    nc = bass.Bass("TRN2", target_bir_lowering=False)
    x_in = nc.dram_tensor("x", [T, D_MODEL], F32, kind="ExternalInput").ap()
    norm_gain = nc.dram_tensor("norm_gain", [DEPTH, D_MODEL], F32, kind="ExternalInput").ap()
    w_in = nc.dram_tensor("w_in", [DEPTH, D_MODEL, IN_WIDTH], F32, kind="ExternalInput").ap()
    b_forget = nc.dram_tensor("b_forget", [DEPTH, 4], F32, kind="ExternalInput").ap()
    qk_gain = nc.dram_tensor("qk_gain", [DEPTH, 6, 64], F32, kind="ExternalInput").ap()
    sinks = nc.dram_tensor("sinks", [DEPTH, 4], F32, kind="ExternalInput").ap()
    w_out = nc.dram_tensor("w_out", [DEPTH, D_MODEL, D_MODEL], F32, kind="ExternalInput").ap()
    rb31 = nc.dram_tensor("rb31", [1, 8], F32, kind="ExternalInput").ap()
    bias_moba = nc.dram_tensor("bias_moba", [4, 8, 128, 128], F32, kind="ExternalInput").ap()
    bias_swa = nc.dram_tensor("bias_swa", [4, 2, 128, 128], F32, kind="ExternalInput").ap()
    cst = nc.dram_tensor("cst", [6, 128, 128], F32, kind="ExternalInput").ap()
    onehot = nc.dram_tensor("onehot", [16, T], F32, kind="ExternalInput").ap()
    out = nc.dram_tensor("out", [T, D_MODEL], F32, kind="ExternalOutput").ap()
    x1 = nc.dram_tensor("x1_scr", [T, D_MODEL], F32, kind="Internal").ap()
    qk_scr = nc.dram_tensor("qk_scr", [N_QK, KROWS, T], BF16, kind="Internal").ap()
    g_scr = nc.dram_tensor("g_scr", [8, 128, T], BF16, kind="Internal").ap()
    v_scr = nc.dram_tensor("v_scr", [T, 14, 192], BF16, kind="Internal").ap()
    if dbg:
        dbg_y = nc.dram_tensor("dbg_y", [8, 128, T], F32, kind="ExternalOutput").ap()

    with ExitStack() as es:
        S = Sched(nc, es)

        _uid = [0]

        def sb(name, shape, dt, scope=es):
            _uid[0] += 1
            return scope.enter_context(nc.sbuf_tensor("%s_%d" % (name, _uid[0]), shape, dt))

        PS = [es.enter_context(nc.psum_tensor("ps%d" % i, [128, 512], F32)) for i in range(8)]
        BPS = [Buf() for _ in range(8)]

        ident_b = sb("ident_b", [128, 128], BF16)
        ident_f = sb("ident_f", [128, 128], F32)
        tri_f = sb("tri_f", [128, 128], F32)
        ones_f = sb("ones_f", [128, 128], F32)
        ones_b = sb("ones_b", [128, 128], BF16)
        negU_b = sb("negU_b", [128, 128], BF16)
        blk_b = sb("blk_b", [128, 128], BF16)
        mask_b = sb("mask_b", [128, 3, 128], BF16)
        Bconst = Buf()
        gcols = sb("gcols", [128, DEPTH, 6], F32)
        ngain = sb("ngain", [128, DEPTH, D_MODEL], F32)
        bfg = sb("bfg", [128, DEPTH, 4], F32)
        esink = sb("esink", [128, DEPTH, 4], F32)
        rb31_t = sb("rb31_t", [128, 8], F32)
        biasM = sb("biasM", [128, 4, 8, 128], BF16)
        biasW = sb("biasW", [128, 4, 2, 128], BF16)
        xf_all = sb("xf_all", [128, NT, 4], F32)
        BxF = Buf()
        kmean = sb("kmean", [128, 2, NB], F32)
        Bkm = Buf()
        c_all = sb("c_all", [128, NT, 4], F32)
        Bc = Buf()

        with ExitStack() as e0:
            stg = sb("c_stg", [128, 6, 128], F32, e0)
            Bstg = Buf()
            S.dma(lambda e: e.dma_start(out=stg[:], in_=cst.rearrange("a p n -> p a n")), writes=[Bstg])
            S.op("dve", lambda e: e.tensor_copy(out=ident_f[:], in_=stg[:, 0, :]), reads=[Bstg], writes=[Bconst])
            S.op("dve", lambda e: e.tensor_copy(out=ident_b[:], in_=stg[:, 0, :]), reads=[Bstg], writes=[Bconst])
            S.op("dve", lambda e: e.tensor_copy(out=tri_f[:], in_=stg[:, 1, :]), reads=[Bstg], writes=[Bconst])
            S.op("dve", lambda e: e.tensor_copy(out=negU_b[:], in_=stg[:, 2, :]), reads=[Bstg], writes=[Bconst])
            S.op("dve", lambda e: e.tensor_copy(out=blk_b[:], in_=stg[:, 3, :]), reads=[Bstg], writes=[Bconst])
            S.op("dve", lambda e: e.memset(ones_f[:], 1.0), writes=[Bconst])
            S.op("dve", lambda e: e.memset(ones_b[:], 1.0), writes=[Bconst])
            S.op("dve", lambda e: e.tensor_copy(out=mask_b[:, 0, :], in_=stg[:, 4, :]), reads=[Bstg], writes=[Bconst])
            S.op("dve", lambda e: e.tensor_copy(out=mask_b[:, 1, :], in_=stg[:, 5, :]), reads=[Bstg], writes=[Bconst])
            S.op("dve", lambda e: e.tensor_scalar(out=mask_b[:, 2, :], in0=stg[:, 4, :], scalar1=-1.0, scalar2=NEG,
                                                  op0=ALU.mult, op1=ALU.add), reads=[Bstg], writes=[Bconst])
            for l in range(DEPTH):
                S.dma(lambda e, l=l: e.dma_start(out=ngain[:, l, :], in_=norm_gain[l:l + 1, :].partition_broadcast(128)),
                      writes=[Bconst])
                S.dma(lambda e, l=l: e.dma_start(out=bfg[:, l, :], in_=b_forget[l:l + 1, :].partition_broadcast(128)),
                      writes=[Bconst])
                S.dma(lambda e, l=l: e.dma_start(out=esink[:, l, :], in_=sinks[l:l + 1, :].partition_broadcast(128)),
                      writes=[Bconst])
                for half in range(2):
                    S.dma(lambda e, l=l, half=half: e.dma_start(
                        out=gcols[half * 64:(half + 1) * 64, l, :], in_=qk_gain[l].rearrange("s d -> d s"),
                        allow_slow_non_contiguous=True), writes=[Bconst])
            S.dma(lambda e: e.dma_start(out=rb31_t[:], in_=rb31[0:1, :].partition_broadcast(128)), writes=[Bconst])
            for l in range(DEPTH):
                for j in (0, 2, 4):
                    S.op("dve", lambda e, l=l, j=j: e.tensor_scalar(out=gcols[:, l, j:j + 1], in0=gcols[:, l, j:j + 1],
                                                                    scalar1=0.125, scalar2=None, op0=ALU.mult),
                         reads=[Bconst], writes=[Bconst])
                S.op("act", lambda e, l=l: e.activation(out=esink[:, l, :], in_=esink[:, l, :], func=AF.Exp),
                     reads=[Bconst], writes=[Bconst])
            bst = [sb("b_stg%d" % i, [128, 8, 128], F32, e0) for i in range(2)]
            Bbst = [Buf(), Buf()]
            for h in range(4):
                t_, b_ = bst[h % 2], Bbst[h % 2]
                S.dma(lambda e, h=h, t_=t_: e.dma_start(out=t_[:], in_=bias_moba[h].rearrange("a p n -> p a n")), writes=[b_])
                S.op("dve", lambda e, h=h, t_=t_: e.tensor_scalar(out=t_[:], in0=t_[:], scalar1=rb31_t[:, h:h + 1],
                                                                  scalar2=None, op0=ALU.subtract),
                     reads=[b_, Bconst], writes=[b_])
                S.op("dve", lambda e, h=h, t_=t_: e.tensor_tensor(out=t_[:, 0, :], in0=t_[:, 0, :], in1=stg[:, 4, :],
                                                                  op=ALU.add), reads=[b_, Bstg], writes=[b_])
                S.op("dve", lambda e, h=h, t_=t_: e.tensor_copy(out=biasM[:, h, :, :], in_=t_[:]), reads=[b_], writes=[Bconst])
            for h in range(4):
                t_, b_ = bst[h % 2], Bbst[h % 2]
                S.dma(lambda e, h=h, t_=t_: e.dma_start(out=t_[:, 0:2, :], in_=bias_swa[h].rearrange("a p n -> p a n")),
                      writes=[b_])
                S.op("dve", lambda e, t_=t_: e.tensor_tensor(out=t_[:, 0, :], in0=t_[:, 0, :], in1=stg[:, 4, :],
                                                             op=ALU.add), reads=[b_, Bstg], writes=[b_])
                S.op("dve", lambda e, t_=t_: e.tensor_tensor(out=t_[:, 1, :], in0=t_[:, 1, :], in1=stg[:, 4, :],
                                                             op=ALU.subtract), reads=[b_, Bstg], writes=[b_])
                S.op("dve", lambda e, t_=t_: e.tensor_scalar(out=t_[:, 1, :], in0=t_[:, 1, :], scalar1=NEG,
                                                             scalar2=None, op0=ALU.add), reads=[b_], writes=[b_])
                S.op("dve", lambda e, h=h, t_=t_: e.tensor_copy(out=biasW[:, h, :, :], in_=t_[:, 0:2, :]),
                     reads=[b_], writes=[Bconst])
            oh = sb("oh_stg", [KROWS, T], F32, e0)
            ohb = sb("oh_b", [KROWS, T], BF16, e0)
            Boh = Buf()
            S.dma(lambda e: e.dma_start(out=oh[64:80, :], in_=onehot[:, :]), writes=[Boh])
            S.op("dve", lambda e: e.tensor_copy(out=ohb[64:80, :], in_=oh[64:80, :]), reads=[Boh], writes=[Boh])
            Bqk = [Buf() for _ in range(N_QK)]
            for h in range(4):
                S.dma(lambda e, h=h: e.dma_start(out=qk_scr[ID_MK + h, 64:80, :], in_=ohb[64:80, :]), reads=[Boh],
                      writes=[Bqk[ID_MK + h]])
            onesr = sb("onesr", [KROWS, T], BF16, e0)
            Bor = Buf()
            S.op("pool", lambda e: e.memset(onesr[64:70, :], 1.0), writes=[Bor])
            for h in range(4):
                S.dma(lambda e, h=h: e.dma_start(out=qk_scr[ID_FK + h, 64:67, :], in_=onesr[64:67, :]), reads=[Bor],
                      writes=[Bqk[ID_FK + h]])
                S.dma(lambda e, h=h: e.dma_start(out=qk_scr[ID_FQ + h, 67:70, :], in_=onesr[64:67, :]), reads=[Bor],
                      writes=[Bqk[ID_FQ + h]])
            S.barrier()
        Bg = Buf()
        Bv = Buf()
        Bx1 = Buf()

        def run_pipeline(items):
            n = len(items)
            maxoff = max(o for it in items for (o, _) in it)
            for s_ in range(n + maxoff):
                for off in range(maxoff + 1):
                    t = s_ - off
                    if 0 <= t < n:
                        for (o, fn) in items[t]:
                            if o == off:
                                fn()

        def phase1(l, xsrc):
            with ExitStack() as e1:
                win = sb("win", [128, 8, IN_WIDTH], BF16, e1)
                Bwin = [Buf() for _ in range(8)]
                xts = Rot([(sb("xt%d" % i, [128, D_MODEL], F32, e1), Buf()) for i in range(3)])
                hns = Rot([(sb("hn%d" % i, [128, D_MODEL], BF16, e1), Buf()) for i in range(2)])
                hnTs = Rot([(sb("hnT%d" % i, [128, 8, 512], BF16, e1), Buf()) for i in range(2)])
                junk = sb("junk", [128, D_MODEL], BF16, e1)
                Bjunk = Buf()
                stat = Rot([(sb("stat%d" % i, [128, 4], F32, e1), Buf()) for i in range(3)])
                sqs = Rot([(sb("sq%d" % i, [128, 512], BF16, e1), Buf()) for i in range(3)])
                lns = Rot([(sb("ln%d" % i, [128, 512], F32, e1), Buf()) for i in range(3)])
                rss = Rot([(sb("rs%d" % i, [128, 512], F32, e1), Buf()) for i in range(3)])
                kn32 = Rot([(sb("kn%d" % i, [128, 512], F32, e1), Buf()) for i in range(2)])
                ost = Rot([(sb("ost%d" % i, [128, 512], BF16, e1), Buf()) for i in range(4)])
                vst = Rot([(sb("vst%d" % i, [128, 14, 192], BF16, e1), Buf()) for i in range(2)])
                for (t_, b_) in vst.items:
                    S.op("dve", lambda e: e.memset(t_[:], 1.0), writes=[b_])
                psF = Rot([(PS[i], BPS[i]) for i in (0, 1, 2)])
                psQ = Rot([(PS[i], BPS[i]) for i in (3,)])
                psT = Rot([(PS[i], BPS[i]) for i in (4, 5)])
                psV = Rot([(PS[i], BPS[i]) for i in (6, 7)])
                wv = w_in[l].rearrange("(c p) e -> p c e", p=128)
                wst = Rot([(sb("wst%d" % i, [128, 8, 512], F32, e1), Buf()) for i in range(2)])
                for cc in range(8):
                    c0 = cc * 512
                    cw = min(512, IN_WIDTH - c0)
                    t_, b_ = wst.next()
                    S.dma(lambda e: e.dma_start(out=t_[:, :, 0:cw], in_=wv[:, :, c0:c0 + cw]), writes=[b_])
                    S.op("dve" if cc % 2 == 0 else "pool",
                         lambda e: e.tensor_copy(out=win[:, :, c0:c0 + cw], in_=t_[:, :, 0:cw]),
                         reads=[b_], writes=[Bwin[cc]])
                S.op("dve", lambda e: e.memset(kmean[:], 0.0), writes=[Bkm])

                def wbufs(c0, w):
                    return [Bwin[k] for k in range(c0 // 512, (c0 + w - 1) // 512 + 1)]

                def prep_item(i, hnT, BhnT, j):
                    xt, bx = xts.next()
                    st, bst_ = stat.next()
                    hn, bhn = hns.next()
                    pt, bpt = psT.next()
                    ptb = pt[:].bitcast(BF16)

                    def f0():
                        S.dma(lambda e: e.dma_start(out=xt[:], in_=xsrc[i * 128:(i + 1) * 128, :]), reads=[Bx1], writes=[bx])
                        S.op("act", lambda e: e.activation(out=junk[:], in_=xt[:], func=AF.Square, accum_out=st[:, 0:1]),
                             reads=[bx], writes=[Bjunk, bst_])
                        S.op("act", lambda e: e.activation(out=st[:, 1:2], in_=st[:, 0:1], func=AF.Ln,
                                                           scale=1.0 / D_MODEL, bias=1e-6), reads=[bst_], writes=[bst_])
                        S.op("act", lambda e: e.activation(out=st[:, 2:3], in_=st[:, 1:2], func=AF.Exp, scale=-0.5),
                             reads=[bst_], writes=[bst_])
                        S.op("dve", lambda e: e.scalar_tensor_tensor(
                            out=hn[:], in0=xt[:], scalar=st[:, 2:3], in1=ngain[:, l, :], op0=ALU.mult, op1=ALU.mult),
                            reads=[bx, bst_, Bconst], writes=[bhn])

                    def f1():
                        for c in range(8):
                            S.op("pe", lambda e: e.transpose(ptb[:, c * 128:(c + 1) * 128], hn[:, c * 128:(c + 1) * 128],
                                                             ident_b[:]), reads=[bhn, Bconst], writes=[bpt])

                    def f2():
                        if j % 2 == 0:
                            S.op("act", lambda e: e.activation(
                                out=hnT[:, :, j * 128:(j + 1) * 128], in_=ptb.rearrange("p (c n) -> p c n", n=128),
                                func=AF.Copy), reads=[bpt], writes=[BhnT])
                        else:
                            S.op("dve", lambda e: e.tensor_copy(
                                out=hnT[:, :, j * 128:(j + 1) * 128], in_=ptb.rearrange("p (c n) -> p c n", n=128)),
                                reads=[bpt], writes=[BhnT])
                    return [(0, f0), (1, f1), (2, f2)]

                def qk_item(hnT, BhnT, g, col0, gj, dst_id, scale_only=False, is_mk=False):
                    M = 128
                    pp, bp = psF.next()
                    o_t, o_b = ost.next()
                    if not scale_only:
                        sq, bsq = sqs.next()
                        pq, bq = psQ.next()
                        ln_t, ln_b = lns.next()
                        rs_t, rs_b = rss.next()
                        if is_mk:
                            k32, bk32 = kn32.next()

                    def f0():
                        for c in range(8):
                            S.op("pe", lambda e: e.matmul(pp[0:M, :], lhsT=win[:, c, col0:col0 + M], rhs=hnT[:, c, :],
                                                          start=(c == 0), stop=(c == 7)),
                                 reads=wbufs(col0, M) + [BhnT], writes=[bp])

                    def store():
                        for hh in range(2):
                            S.dma(lambda e: e.dma_start(out=qk_scr[dst_id + hh, 0:64, g * 512:(g + 1) * 512],
                                                        in_=o_t[hh * 64:(hh + 1) * 64, :]),
                                  reads=[o_b], writes=[Bqk[dst_id + hh]])

                    def f1():
                        if scale_only:
                            sc = 0.125 if gj else 1.0
                            S.op("act", lambda e: e.activation(out=o_t[0:M, :], in_=pp[0:M, :], func=AF.Copy, scale=sc),
                                 reads=[bp], writes=[o_b])
                            store()
                        else:
                            S.op("act", lambda e: e.activation(out=sq[0:M, :], in_=pp[0:M, :], func=AF.Square),
                                 reads=[bp], writes=[bsq])

                    def f2():
                        if scale_only:
                            return
                        S.op("pe", lambda e: e.matmul(pq[0:M, :], lhsT=blk_b[0:M, 0:M], rhs=sq[0:M, :], start=True,
                                                      stop=True), reads=[bsq, Bconst], writes=[bq])
                        S.op("act", lambda e: e.activation(out=ln_t[0:M, :], in_=pq[0:M, :], func=AF.Ln,
                                                           scale=1.0 / 64.0, bias=1e-6), reads=[bq], writes=[ln_b])
                        S.op("act", lambda e: e.activation(out=rs_t[0:M, :], in_=ln_t[0:M, :], func=AF.Exp, scale=-0.5),
                             reads=[ln_b], writes=[rs_b])
                        if is_mk:
                            S.op("dve", lambda e: e.scalar_tensor_tensor(
                                out=k32[:, :], in0=pp[:, :], scalar=gcols[:, l, gj:gj + 1], in1=rs_t[:, :],
                                op0=ALU.mult, op1=ALU.mult), reads=[bp, rs_b, Bconst], writes=[bk32])
                            S.op("act", lambda e: e.activation(out=o_t[:, :], in_=k32[:, :], func=AF.Copy), reads=[bk32],
                                 writes=[o_b])
                            pr = (dst_id - ID_MK) // 2
                            S.op("dve", lambda e: e.tensor_reduce(
                                out=kmean[:, pr, 2 * g:2 * g + 2], in_=k32[:, :].rearrange("p (a b) -> p a b", b=256),
                                axis=AX.X, op=ALU.add), reads=[bk32], writes=[Bkm])
                        else:
                            S.op("dve", lambda e: e.scalar_tensor_tensor(
                                out=o_t[0:M, :], in0=pp[0:M, :], scalar=gcols[0:M, l, gj:gj + 1], in1=rs_t[0:M, :],
                                op0=ALU.mult, op1=ALU.mult), reads=[bp, rs_b, Bconst], writes=[o_b])
                        store()
                    return [(0, f0), (1, f1), (2, f2)]

                def gate_item(hnT, BhnT, g, col0, gid):
                    pp, bp = psF.next()
                    o_t, o_b = ost.next()

                    def f0():
                        for c in range(8):
                            S.op("pe", lambda e: e.matmul(pp[:, :], lhsT=win[:, c, col0:col0 + 128], rhs=hnT[:, c, :],
                                                          start=(c == 0), stop=(c == 7)),
                                 reads=wbufs(col0, 128) + [BhnT], writes=[bp])

                    def f1():
                        S.op("act", lambda e: e.activation(out=o_t[:, :], in_=pp[:, :], func=AF.Silu),
                             reads=[bp], writes=[o_b])
                        S.dma(lambda e: e.dma_start(out=g_scr[gid, :, g * 512:(g + 1) * 512], in_=o_t[:, :]),
                              reads=[o_b], writes=[Bg])
                    return [(0, f0), (1, f1)]

                def v_item(hnT, BhnT, g, j):
                    i = g * 4 + j
                    pa, ba = psV.next()
                    pb, bb = psV.next()
                    vt, bvt = vst.next()

                    def f0():
                        specs = [(pa, ba, 0, A_V, 256), (pa, ba, 256, B_V, 256), (pb, bb, 0, C_V, 256),
                                 (pb, bb, 256, D_V, 128), (pb, bb, 384, A_F, 4)]
                        for (pp, bp, o0, c0, w) in specs:
                            for c in range(8):
                                S.op("pe", lambda e: e.matmul(
                                    pp[:, o0:o0 + w], lhsT=hnT[:, c, j * 128:(j + 1) * 128], rhs=win[:, c, c0:c0 + w],
                                    start=(c == 0), stop=(c == 7)), reads=wbufs(c0, w) + [BhnT], writes=[bp])

                    def f1():
                        S.op("act", lambda e: e.activation(
                            out=vt[:, 0:8, 64:128], in_=pa[:, 0:512].rearrange("p (h d) -> p h d", d=64), func=AF.Copy),
                            reads=[ba], writes=[bvt])
                        S.op("dve", lambda e: e.tensor_copy(
                            out=vt[:, 8:14, 64:128], in_=pb[:, 0:384].rearrange("p (h d) -> p h d", d=64)),
                            reads=[bb], writes=[bvt])
                        S.op("dve", lambda e: e.tensor_tensor(out=xf_all[:, i, :], in0=pb[:, 384:388], in1=bfg[:, l, :],
                                                              op=ALU.add), reads=[bb, Bconst], writes=[BxF])
                        S.dma(lambda e: e.dma_start(out=v_scr[i * 128:(i + 1) * 128, :, :], in_=vt[:]), reads=[bvt],
                              writes=[Bv])
                    return [(0, f0), (1, f1)]

                items = []
                hn_bufs = [hnTs.next() for _ in range(2)]
                for j in range(4):
                    items.append(prep_item(j, hn_bufs[0][0], hn_bufs[0][1], j))
                items.append([])
                items.append([])
                for g in range(NG):
                    hnT, BhnT = hn_bufs[g % 2]
                    gi = []
                    for pr in range(2):
                        gi.append(qk_item(hnT, BhnT, g, A_Q + pr * 128, 0, ID_FQ + 2 * pr))
                        gi.append(qk_item(hnT, BhnT, g, A_K + pr * 128, 1, ID_FK + 2 * pr))
                    gi.append(v_item(hnT, BhnT, g, 0))
                    for pr in range(2):
                        gi.append(qk_item(hnT, BhnT, g, B_Q + pr * 128, 2, ID_MQ + 2 * pr))
                        gi.append(qk_item(hnT, BhnT, g, B_K + pr * 128, 3, ID_MK + 2 * pr, is_mk=True))
                    gi.append(v_item(hnT, BhnT, g, 1))
                    for pr in range(2):
                        gi.append(qk_item(hnT, BhnT, g, C_Q + pr * 128, 1, ID_SQ + 2 * pr, scale_only=True))
                        gi.append(qk_item(hnT, BhnT, g, C_K + pr * 128, 0, ID_SK + 2 * pr, scale_only=True))
                    gi.append(v_item(hnT, BhnT, g, 2))
                    for pr in range(2):
                        gi.append(qk_item(hnT, BhnT, g, D_Q + pr * 128, 4, ID_WQ + 2 * pr))
                    gi.append(qk_item(hnT, BhnT, g, D_K, 5, ID_WK))
                    gi.append(v_item(hnT, BhnT, g, 3))
                    for m, gc in enumerate((A_G, B_G, C_G, D_G)):
                        for pr in range(2):
                            gi.append(gate_item(hnT, BhnT, g, gc + pr * 128, m * 2 + pr))
                    if g + 1 < NG:
                        nh, nb = hn_bufs[(g + 1) % 2]
                        pos = [2, 7, 12, 17]
                        for j in range(4):
                            gi.insert(pos[j] + j, prep_item((g + 1) * 4 + j, nh, nb, j))
                    items.extend(gi)
                run_pipeline(items)
                S.barrier()

        def phase2(l, yT, ByT):
            with ExitStack() as e2:
                QAs = Rot([(sb("QA%d" % i, [KROWS, T], BF16, e2), Buf()) for i in range(2)])
                KAs = Rot([(sb("KA%d" % i, [KROWS, T], BF16, e2), Buf()) for i in range(2)])
                VAs = Rot([(sb("VA%d" % i, [128, NT, 192], BF16, e2), Buf()) for i in range(2)])
                GTs = Rot([(sb("GT%d" % i, [128, T], BF16, e2), Buf()) for i in range(2)])
                Ps = Rot([(sb("P%d" % i, [128, 512], BF16, e2), Buf()) for i in range(3)])
                rdn = Rot([(sb("rdn%d" % i, [128, 512], F32, e2), Buf()) for i in range(2)])
                tmo = Rot([(sb("tmo%d" % i, [128, 512], F32, e2), Buf()) for i in range(2)])
                psS = Rot([(PS[i], BPS[i]) for i in (0, 1, 2)])
                psO = Rot([(PS[i], BPS[i]) for i in (3, 4)])
                psM = Rot([(PS[i], BPS[i]) for i in (5, 6)])
                psC = Rot([(PS[i], BPS[i]) for i in (7,)])

                def load_head(qid, kid, vh, nq, nk=None):
                    nk = nq if nk is None else nk
                    QA, bQ = QAs.next()
                    KA, bK = KAs.next()
                    VA, bV = VAs.next()
                    S.dma(lambda e: e.dma_start(out=QA[0:nq, :], in_=qk_scr[qid, 0:nq, :]), reads=[Bqk[qid]],
                          writes=[bQ])
                    S.dma(lambda e: e.dma_start(out=KA[0:nk, :], in_=qk_scr[kid, 0:nk, :]), reads=[Bqk[kid]],
                          writes=[bK])
                    S.dma(lambda e: e.dma_start(out=VA[:], in_=v_scr[:, vh, :].rearrange("(n p) d -> p n d", p=128)),
                          reads=[Bv], writes=[bV])
                    return QA, bQ, KA, bK, VA, bV

                def load_gate(gid):
                    GT, bG = GTs.next()
                    S.dma(lambda e: e.dma_start(out=GT[:], in_=g_scr[gid]), reads=[Bg], writes=[bG])
                    return GT, bG

                def finalize(O, bO, par, GT, bG, chunk, g, normalize, extra_den=None):
                    orow = slice(0, 64) if par == 0 else slice(64, 128)
                    drow = slice(64, 128) if par == 0 else slice(0, 64)
                    cols = slice(g * 512, (g + 1) * 512)
                    if normalize:
                        r_t, r_b = rdn.next()
                        if extra_den is not None:
                            S.op("dve", lambda e: e.tensor_scalar(out=r_t[orow, :], in0=O[drow, :], scalar1=extra_den[drow],
                                                                  scalar2=None, op0=ALU.add),
                                 reads=[bO, Bconst], writes=[r_b])
                            S.op("dve", lambda e: e.reciprocal(out=r_t[orow, :], in_=r_t[orow, :]), reads=[r_b],
                                 writes=[r_b])
                        else:
                            S.op("dve", lambda e: e.reciprocal(out=r_t[orow, :], in_=O[drow, :]), reads=[bO],
                                 writes=[r_b])
                        t_t, t_b = tmo.next()
                        S.op("dve", lambda e: e.tensor_tensor(out=t_t[orow, :], in0=O[orow, :], in1=r_t[orow, :],
                                                              op=ALU.mult), reads=[bO, r_b], writes=[t_b])
                        S.op("pool", lambda e: e.tensor_tensor(out=yT[orow, chunk, cols], in0=t_t[orow, :],
                                                               in1=GT[orow, cols], op=ALU.mult),
                             reads=[t_b, bG], writes=[ByT])
                    else:
                        S.op("dve", lambda e: e.tensor_tensor(out=yT[orow, chunk, cols], in0=O[orow, :],
                                                              in1=GT[orow, cols], op=ALU.mult),
                             reads=[bO, bG], writes=[ByT])

                def vslice(VA, ki, par):
                    return VA[:, ki, 64:192] if par == 0 else VA[:, ki, 0:128]

                with ExitStack() as ef:
                    NF = NT * 4
                    xf2 = xf_all[:].rearrange("p n h -> p (n h)")
                    S.op("act", lambda e: e.activation(out=xf2, in_=xf2, func=AF.Exp, scale=-1.0), reads=[BxF], writes=[BxF])
                    S.op("act", lambda e: e.activation(out=xf2, in_=xf2, func=AF.Ln, bias=1.0), reads=[BxF], writes=[BxF])
                    S.op("dve", lambda e: e.tensor_scalar(out=xf2, in0=xf2, scalar1=-1.0, scalar2=None, op0=ALU.mult),
                         reads=[BxF], writes=[BxF])
                    p1, b1 = psM.next()
                    p2, b2 = psM.next()
                    S.op("pe", lambda e: e.matmul(p1[:, 0:NF], lhsT=tri_f[:], rhs=xf2, start=True, stop=True),
                         reads=[BxF, Bconst], writes=[b1])
                    S.op("pe", lambda e: e.matmul(p2[:, 0:NF], lhsT=ones_f[:], rhs=xf2, start=True, stop=True),
                         reads=[BxF, Bconst], writes=[b2])
                    tot = sb("f_tot", [128, NT, 4], F32, ef)
                    exc = sb("f_exc", [128, NT, 4], F32, ef)
                    Bt = Buf()
                    S.op("dve", lambda e: e.tensor_copy(out=tot[:].rearrange("p n h -> p (n h)"), in_=p2[:, 0:NF]),
                         reads=[b2], writes=[Bt])
                    S.op("dve", lambda e: e.memset(exc[:, 0, :], 0.0), writes=[Bt])
                    for i in range(1, NT):
                        S.op("dve", lambda e, i=i: e.tensor_tensor(out=exc[:, i, :], in0=exc[:, i - 1, :],
                                                                   in1=tot[:, i - 1, :], op=ALU.add),
                             reads=[Bt], writes=[Bt])
                    S.op("dve", lambda e: e.tensor_tensor(out=c_all[:].rearrange("p n h -> p (n h)"), in0=p1[:, 0:NF],
                                                          in1=exc[:].rearrange("p n h -> p (n h)"), op=ALU.add),
                         reads=[b1, Bt], writes=[Bc])
                    spl = sb("f_spl", [128, NT, 4, 6], BF16, ef)
                    r1 = sb("f_r1", [128, NT, 4], F32, ef)
                    r2 = sb("f_r2", [128, NT, 4], F32, ef)
                    Bs = Buf()
                    S.op("dve", lambda e: e.tensor_copy(out=spl[:, :, :, 0], in_=c_all[:]), reads=[Bc], writes=[Bs])
                    S.op("dve", lambda e: e.tensor_tensor(out=r1[:], in0=c_all[:], in1=spl[:, :, :, 0], op=ALU.subtract),
                         reads=[Bc, Bs], writes=[Bs])
                    S.op("dve", lambda e: e.tensor_copy(out=spl[:, :, :, 1], in_=r1[:]), reads=[Bs], writes=[Bs])
                    S.op("dve", lambda e: e.tensor_tensor(out=r2[:], in0=r1[:], in1=spl[:, :, :, 1], op=ALU.subtract),
                         reads=[Bs], writes=[Bs])
                    S.op("dve", lambda e: e.tensor_copy(out=spl[:, :, :, 2], in_=r2[:]), reads=[Bs], writes=[Bs])
                    S.op("dve", lambda e: e.tensor_scalar(out=spl[:, :, :, 3:6], in0=spl[:, :, :, 0:3], scalar1=-1.0,
                                                          scalar2=None, op0=ALU.mult), reads=[Bs], writes=[Bs])
                    augs = Rot([(sb("f_aug%d" % i, [KROWS, 512], BF16, ef), Buf()) for i in range(2)])
                    for h in range(4):
                        for g in range(NG):
                            pm, bm = psM.next()
                            for j in range(4):
                                i = g * 4 + j
                                S.op("pe", lambda e, pm=pm, i=i, j=j, h=h: e.matmul(
                                    pm[64:70, j * 128:(j + 1) * 128], lhsT=spl[:, i, h, :], rhs=ident_b[:],
                                    start=True, stop=True), reads=[Bs, Bconst], writes=[bm])
                            a_t, a_b = augs.next()
                            S.op("act", lambda e, pm=pm, a_t=a_t: e.activation(out=a_t[64:70, :], in_=pm[64:70, :],
                                                                               func=AF.Copy), reads=[bm], writes=[a_b])
                            S.dma(lambda e, a_t=a_t, h=h, g=g: e.dma_start(
                                out=qk_scr[ID_FQ + h, 64:67, g * 512:(g + 1) * 512], in_=a_t[64:67, :]),
                                reads=[a_b], writes=[Bqk[ID_FQ + h]])
                            S.dma(lambda e, a_t=a_t, h=h, g=g: e.dma_start(
                                out=qk_scr[ID_FK + h, 67:70, g * 512:(g + 1) * 512], in_=a_t[67:70, :]),
                                reads=[a_b], writes=[Bqk[ID_FK + h]])

                S.barrier()
                def softmax_tile(QA, qbufs, KA, bK, VA, bV, O, bO, g, ki, par, krows, bias_tiles, act_bias, fin):
                    j = max(ki - 4 * g, 0)
                    c0 = j * 128
                    last = 4 * g + 3
                    S_, bS = psS.next()
                    P, bP = Ps.next()

                    def f0():
                        S.op("pe", lambda e: e.matmul(
                            S_[:, c0:512], lhsT=KA[0:krows, ki * 128:(ki + 1) * 128],
                            rhs=QA[0:krows, g * 512 + c0:(g + 1) * 512], start=True, stop=True),
                            reads=list(qbufs) + [bK], writes=[bS])
                        if bias_tiles is None:
                            if ki >= 4 * g:
                                S.op("pe", lambda e: e.matmul(
                                    S_[:, c0:c0 + 128], lhsT=ident_b[:], rhs=mask_b[:, 0, :], start=False, stop=True),
                                    reads=[Bconst], writes=[bS])
                        else:
                            for jj in range(j, 4):
                                dl = 4 * g + jj - ki
                                if dl < 8:
                                    S.op("pe", lambda e: e.matmul(
                                        S_[:, jj * 128:(jj + 1) * 128], lhsT=ident_b[:], rhs=bias_tiles[:, dl, :],
                                        start=False, stop=True), reads=[Bconst], writes=[bS])

                    def f1():
                        if act_bias is None:
                            S.op("act", lambda e: e.activation(out=P[:, c0:512], in_=S_[:, c0:512], func=AF.Exp),
                                 reads=[bS], writes=[bP])
                        else:
                            S.op("act", lambda e: e.activation(out=P[:, c0:512], in_=S_[:, c0:512], func=AF.Exp,
                                                               bias=act_bias), reads=[bS, Bconst], writes=[bP])

                    def f2():
                        S.op("pe", lambda e: e.matmul(
                            O[:, c0:512], lhsT=vslice(VA, ki, par), rhs=P[:, c0:512], start=(ki == 0),
                            stop=(ki == last)), reads=[bP, bV], writes=[bO])
                        if ki == last:
                            fin()
                    return [(0, f0), (1, f1), (2, f2)]

                def softmax_head(QA, bQ, KA, bK, VA, bV, GT, bG, par, chunk, krows, bias_tiles=None, act_bias=None,
                                 qaug=None, hooks=None):
                    items = []
                    for g in range(NG):
                        O, bO = psO.next()
                        qbufs = [bQ] if qaug is None else [bQ, qaug[g % 2]]

                        def fin(O=O, bO=bO, g=g):
                            finalize(O, bO, par, GT, bG, chunk, g, True)
                        for ki in range(4 * g + 4):
                            it = softmax_tile(QA, qbufs, KA, bK, VA, bV, O, bO, g, ki, par, krows, bias_tiles, act_bias,
                                              fin)
                            if hooks is not None:
                                if ki == 0 and (g, 0) in hooks:
                                    it.insert(0, (0, hooks[(g, 0)]))
                                if ki == 4 * g + 3 and (g, 1) in hooks:
                                    it.append((0, hooks[(g, 1)]))
                            items.append(it)
                    run_pipeline(items)

                for h in range(4):
                    QA, bQ, KA, bK, VA, bV = load_head(ID_FQ + h, ID_FK + h, h, 70, 70)
                    if h % 2 == 0:
                        GT, bG = load_gate(h // 2)
                    softmax_head(QA, bQ, KA, bK, VA, bV, GT, bG, h % 2, h // 2, 70)

                with ExitStack() as em:
                    km_b = sb("km_b", [64, 4, NB], BF16, em)
                    Bkmb = Buf()
                    for h in range(4):
                        S.op("dve", lambda e: e.tensor_copy(out=km_b[0:64, h, :],
                                                            in_=kmean[(h % 2) * 64:(h % 2) * 64 + 64, h // 2, :]),
                             reads=[Bkm], writes=[Bkmb])
                    gms = Rot([(sb("gm%d" % i, [128, 4, 16], F32, em), Buf()) for i in range(2)])
                    t8s = Rot([(sb("t8%d" % i, [128, 4, 8], F32, em), Buf()) for i in range(2)])
                    msl = Rot([(sb("msl%d" % i, [128, 4, 16], BF16, em), Buf()) for i in range(2)])

                    def make_sel(QA, bQ, qaug, h, g):
                        st = {}

                        def part1():
                            pg, bg_ = psM.next()
                            for j in range(4):
                                i = g * 4 + j
                                S.op("pe", lambda e: e.matmul(
                                    pg[:, j * 16:j * 16 + NB], lhsT=QA[0:64, i * 128:(i + 1) * 128], rhs=km_b[0:64, h, :],
                                    start=True, stop=True), reads=[bQ, Bkmb], writes=[bg_])
                            gm, bgm = gms.next()
                            S.op("pool", lambda e: e.memset(gm[:], -1e30), writes=[bgm])
                            for j in range(4):
                                qb = (g * 4 + j) // 2
                                if qb > 0:
                                    S.op("dve", lambda e: e.tensor_copy(out=gm[:, j, 0:qb], in_=pg[:, j * 16:j * 16 + qb]),
                                         reads=[bg_], writes=[bgm])
                            t8, bt8 = t8s.next()
                            for j in range(4):
                                S.op("dve", lambda e: e.max(out=t8[:, j, :], in_=gm[:, j, :]), reads=[bgm], writes=[bt8])
                            ms, bms = msl.next()
                            for j in range(4):
                                S.op("dve", lambda e: e.tensor_scalar(
                                    out=ms[:, j, :], in0=gm[:, j, :], scalar1=t8[:, j, 2:3], scalar2=NEG, op0=ALU.is_lt,
                                    op1=ALU.mult), reads=[bgm, bt8], writes=[bms])
                                qb = (g * 4 + j) // 2
                                S.op("dve", lambda e: e.memset(ms[:, j, qb:qb + 1], 0.0), writes=[bms])
                            st["ms"] = (ms, bms)

                        def part2():
                            ms, bms = st["ms"]
                            pm, bm = psM.next()
                            for j in range(4):
                                S.op("pe", lambda e: e.matmul(
                                    pm[64:80, j * 128:(j + 1) * 128], lhsT=ms[:, j, :], rhs=ident_b[:], start=True,
                                    stop=True), reads=[bms, Bconst], writes=[bm])
                            S.op("act", lambda e: e.activation(out=QA[64:80, g * 512:(g + 1) * 512], in_=pm[64:80, :],
                                                               func=AF.Copy), reads=[bm], writes=[qaug[g % 2]])
                        return part1, part2

                    for h in range(4):
                        QA, bQ, KA, bK, VA, bV = load_head(ID_MQ + h, ID_MK + h, 4 + h, 64, 80)
                        if h % 2 == 0:
                            GT, bG = load_gate(2 + h // 2)
                        qaug = [Buf(), Buf()]
                        hooks = {}
                        p1, p2 = make_sel(QA, bQ, qaug, h, 0)
                        p1()
                        p2()
                        for g in range(NG - 1):
                            p1, p2 = make_sel(QA, bQ, qaug, h, g + 1)
                            hooks[(g, 0)] = p1
                            hooks[(g, 1)] = p2
                        softmax_head(QA, bQ, KA, bK, VA, bV, GT, bG, h % 2, 2 + h // 2, 80,
                                     bias_tiles=biasM[:, h, :, :], act_bias=rb31_t[:, h:h + 1], qaug=qaug, hooks=hooks)

                S.barrier()
                with ExitStack() as esb:
                    Es = Rot([(sb("sbE%d" % i, [128, 512], F32, esb), Buf()) for i in range(3)])
                    SPs = Rot([(sb("sbSP%d" % i, [128, 512], BF16, esb), Buf()) for i in range(3)])
                    ARs = Rot([(sb("sbAR%d" % i, [128, 512], F32, esb), Buf()) for i in range(3)])
                    carry = sb("sbcarry", [128, 512], F32, esb)
                    Bcar = Buf()
                    psT3 = Rot([(PS[i], BPS[i]) for i in (5, 6, 7)])

                    def sb_tile(QA, bQ, KA, bK, VA, bV, O, bO, g, ki, par, fin):
                        j = max(ki - 4 * g, 0)
                        c0 = j * 128
                        first = 4 * g + 3
                        Z, bZ = psS.next()
                        E, bE = Es.next()
                        SP, bSP = SPs.next()
                        Tb, bT = psT3.next()
                        AR, bAR = ARs.next()
                        P, bP = Ps.next()

                        def fA():
                            S.op("pe", lambda e: e.matmul(
                                Z[:, c0:512], lhsT=KA[0:64, ki * 128:(ki + 1) * 128],
                                rhs=QA[0:64, g * 512 + c0:(g + 1) * 512], start=True, stop=True),
                                reads=[bQ, bK], writes=[bZ])
                            if ki >= 4 * g:
                                S.op("pe", lambda e: e.matmul(
                                    Z[:, c0:c0 + 128], lhsT=ident_b[:], rhs=mask_b[:, 1, :], start=False, stop=True),
                                    reads=[Bconst], writes=[bZ])

                        def fB():
                            S.op("act", lambda e: e.activation(out=E[:, c0:512], in_=Z[:, c0:512], func=AF.Exp),
                                 reads=[bZ], writes=[bE])
                            S.op("act", lambda e: e.activation(out=SP[:, c0:512], in_=E[:, c0:512], func=AF.Ln, bias=1.0),
                                 reads=[bE], writes=[bSP])

                        def fC():
                            S.op("pe", lambda e: e.matmul(Z[:, c0:512], lhsT=negU_b[:], rhs=SP[:, c0:512], start=False,
                                                          stop=True), reads=[bSP, Bconst], writes=[bZ])
                            S.op("pe", lambda e: e.matmul(Tb[:, c0:512], lhsT=ones_b[:], rhs=SP[:, c0:512], start=True,
                                                          stop=True), reads=[bSP, Bconst], writes=[bT])

                        def fD():
                            if ki == first:
                                S.op("dve", lambda e: e.memset(carry[:], 0.0), writes=[Bcar])
                            S.op("dve", lambda e: e.tensor_tensor(out=AR[:, c0:512], in0=Z[:, c0:512],
                                                                  in1=carry[:, c0:512], op=ALU.subtract),
                                 reads=[bZ, Bcar], writes=[bAR])
                            S.op("dve", lambda e: e.tensor_tensor(out=carry[:, c0:512], in0=Tb[:, c0:512],
                                                                  in1=carry[:, c0:512], op=ALU.add),
                                 reads=[bT, Bcar], writes=[Bcar])

                        def fE():
                            S.op("act", lambda e: e.activation(out=P[:, c0:512], in_=AR[:, c0:512], func=AF.Exp),
                                 reads=[bAR], writes=[bP])

                        def fF():
                            lh = VA[:, ki, 64:128] if par == 0 else VA[:, ki, 0:128]
                            orows = slice(0, 64) if par == 0 else slice(0, 128)
                            S.op("pe", lambda e: e.matmul(O[orows, c0:512], lhsT=lh, rhs=P[:, c0:512],
                                                          start=(ki == first), stop=(ki == 0)),
                                 reads=[bP, bV], writes=[bO])
                            if ki == 0:
                                fin()
                        return [(0, fA), (0, fB), (1, fC), (1, fD), (2, fE), (3, fF)]

                    for h in range(4):
                        QA, bQ, KA, bK, VA, bV = load_head(ID_SQ + h, ID_SK + h, 8 + h, 64, 64)
                        par = h % 2
                        if par == 0:
                            GT, bG = load_gate(4 + h // 2)
                        items = []
                        for g in range(NG):
                            O, bO = psO.next()

                            def fin(O=O, bO=bO, g=g, par=par, GT=GT, bG=bG, h=h):
                                finalize(O, bO, par, GT, bG, 4 + h // 2, g, False)
                            for ki in range(4 * g + 3, -1, -1):
                                items.append(sb_tile(QA, bQ, KA, bK, VA, bV, O, bO, g, ki, par, fin))
                        run_pipeline(items)

                S.barrier()
                KA = None
                for h in range(4):
                    kv = h // 2
                    par = h % 2
                    if par == 0:
                        QA, bQ, KA, bK, VA, bV = load_head(ID_WQ + h, ID_WK + kv, 12 + kv, 64)
                        GT, bG = load_gate(6 + h // 2)
                    else:
                        QA, bQ = QAs.next()
                        S.dma(lambda e, QA=QA, h=h: e.dma_start(out=QA[0:64, :], in_=qk_scr[ID_WQ + h, 0:64, :]),
                              reads=[Bqk[ID_WQ + h]], writes=[bQ])
                    for g in range(NG):
                        O, bO = psO.next()
                        for jp in range(2):
                            S_, bS = psS.next()
                            for jq in range(2):
                                j = jp * 2 + jq
                                qi = g * 4 + j
                                qcols = slice(qi * 128, (qi + 1) * 128)
                                for dl in (1, 0):
                                    ki = qi - dl
                                    if ki < 0:
                                        continue
                                    sc = slice(jq * 256 + (1 - dl) * 128, jq * 256 + (1 - dl) * 128 + 128)
                                    S.op("pe", lambda e, S_=S_, sc=sc, ki=ki, qcols=qcols, QA=QA, KA=KA: e.matmul(
                                        S_[:, sc], lhsT=KA[0:64, ki * 128:(ki + 1) * 128], rhs=QA[0:64, qcols],
                                        start=True, stop=True), reads=[bQ, bK], writes=[bS])
                                    S.op("pe", lambda e, S_=S_, sc=sc, dl=dl, h=h: e.matmul(
                                        S_[:, sc], lhsT=ident_b[:], rhs=biasW[:, h, dl, :], start=False, stop=True),
                                        reads=[Bconst], writes=[bS])
                            P, bP = Ps.next()
                            a0 = 128 if (g == 0 and jp == 0) else 0
                            S.op("act", lambda e, S_=S_, P=P, a0=a0: e.activation(out=P[:, a0:512], in_=S_[:, a0:512],
                                                                                  func=AF.Exp), reads=[bS], writes=[bP])
                            for jq in range(2):
                                j = jp * 2 + jq
                                qi = g * 4 + j
                                dls = [d_ for d_ in (1, 0) if qi - d_ >= 0]
                                for n_, dl in enumerate(dls):
                                    ki = qi - dl
                                    sc = slice(jq * 256 + (1 - dl) * 128, jq * 256 + (1 - dl) * 128 + 128)
                                    S.op("pe", lambda e, O=O, P=P, sc=sc, ki=ki, j=j, n_=n_, dls=dls, VA=VA, par=par: e.matmul(
                                        O[:, j * 128:(j + 1) * 128], lhsT=vslice(VA, ki, par), rhs=P[:, sc],
                                        start=(n_ == 0), stop=(n_ == len(dls) - 1)), reads=[bP, bV], writes=[bO])
                        finalize(O, bO, par, GT, bG, 6 + h // 2, g, True, extra_den=esink[:, l, h:h + 1])
                S.barrier()

        def phase3(l, yT, ByT, xsrc, dst, Bdst):
            with ExitStack() as e3:
                wo = sb("wo", [128, 8, D_MODEL], BF16, e3)
                Bwo = Buf()
                wst = Rot([(sb("wost%d" % i, [128, 8, 512], F32, e3), Buf()) for i in range(2)])
                wv = w_out[l].rearrange("(c p) e -> p c e", p=128)
                for cc in range(2):
                    t_, b_ = wst.next()
                    S.dma(lambda e, t_=t_, cc=cc: e.dma_start(out=t_[:], in_=wv[:, :, cc * 512:(cc + 1) * 512]), writes=[b_])
                    S.op("dve" if cc == 0 else "pool",
                         lambda e, t_=t_, cc=cc: e.tensor_copy(out=wo[:, :, cc * 512:(cc + 1) * 512], in_=t_[:]),
                         reads=[b_], writes=[Bwo])
                xts = Rot([(sb("x3t%d" % i, [128, D_MODEL], F32, e3), Buf()) for i in range(3)])
                ots = Rot([(sb("o3t%d" % i, [128, D_MODEL], F32, e3), Buf()) for i in range(3)])
                psA = Rot([(PS[i], BPS[i]) for i in range(8)])
                for i in range(NT):
                    xt, bx = xts.next()
                    S.dma(lambda e, xt=xt, i=i: e.dma_start(out=xt[:], in_=xsrc[i * 128:(i + 1) * 128, :]),
                          reads=[Bx1], writes=[bx])
                    ot, bo = ots.next()
                    for half in range(2):
                        pp, bp = psA.next()
                        for c in range(8):
                            S.op("pe", lambda e, pp=pp, c=c, i=i, half=half: e.matmul(
                                pp[:, :], lhsT=yT[:, c, i * 128:(i + 1) * 128], rhs=wo[:, c, half * 512:(half + 1) * 512],
                                start=(c == 0), stop=(c == 7)), reads=[ByT, Bwo], writes=[bp])
                        S.op("dve", lambda e, pp=pp, xt=xt, ot=ot, half=half: e.tensor_tensor(
                            out=ot[:, half * 512:(half + 1) * 512], in0=pp[:, :], in1=xt[:, half * 512:(half + 1) * 512],
                            op=ALU.add), reads=[bp, bx], writes=[bo])
                    S.dma(lambda e, ot=ot, i=i: e.dma_start(out=dst[i * 128:(i + 1) * 128, :], in_=ot[:]),
                          reads=[bo], writes=[Bdst])
                S.barrier()

        Bout = Buf()
        for l in range(L):
            xsrc = x_in if l == 0 else x1
            dst = out if l == L - 1 else x1
            phase1(l, xsrc)
            with ExitStack() as ey:
                yT = sb("yT", [128, 8, T], BF16, ey)
                ByT = Buf()
                phase2(l, yT, ByT)
                if dbg and l == 0:
                    with ExitStack() as ed:
                        dt_ = sb("dbgt", [128, T], F32, ed)
                        Bd = Buf()
                        for c in range(8):
                            S.op("dve", lambda e, c=c: e.tensor_copy(out=dt_[:], in_=yT[:, c, :]), reads=[ByT], writes=[Bd])
                            S.dma(lambda e, c=c: e.dma_start(out=dbg_y[c], in_=dt_[:]), reads=[Bd], writes=[Bout])
                        S.barrier()
                phase3(l, yT, ByT, xsrc, dst, Bx1 if dst is x1 else Bout)
        S.final_wait()
        block = es.enter_context(nc.Block())
        S.emit(block)
    return nc


def _rel_bucket(dist):
    d = np.maximum(dist, 0)
    lr = np.log(np.maximum(d, 1).astype(np.float32) / 16) / math.log(1024 / 16)
    large = 16 + (lr * 16).astype(np.int32)
    large = np.minimum(large, 31)
    return np.where(d < 16, d, large)


def host_consts(T):
    k = np.arange(128)[:, None]
    q = np.arange(128)[None, :]
    cst = np.zeros((6, 128, 128), np.float32)
    cst[0] = np.eye(128, dtype=np.float32)
    cst[1] = (k <= q).astype(np.float32)
    cst[2] = -(k >= q).astype(np.float32)
    cst[3] = ((k // 64) == (q // 64)).astype(np.float32)
    cst[4] = np.where(k <= q, 0.0, NEG).astype(np.float32)
    cst[5] = np.where(k < q, 0.0, NEG).astype(np.float32)
    onehot = (np.arange(T)[None, :] // 256 == np.arange(16)[:, None]).astype(np.float32)
    idx_m = np.stack([_rel_bucket(dl * 128 + q - k) for dl in range(8)])
    idx_w = np.stack([_rel_bucket(dl * 128 + q - k) for dl in range(2)])
    return cst, onehot, idx_m, idx_w


_CACHE = {}


def kernel(x, norm_gain, w_in, b_forget, fox_qk_gain, moba_qk_gain, swa_qk_gain, sinks, w_out, rel_bias):
    x = np.asarray(x, dtype=np.float32)
    B, T, _ = x.shape
    if T not in _CACHE:
        _CACHE[T] = build(T)
    nc = _CACHE[T]
    cst, onehot, idx_m, idx_w = host_consts(T)
    rel_bias = np.asarray(rel_bias, dtype=np.float32)
    bias_moba = np.ascontiguousarray(np.stack([rel_bias[idx_m, h] for h in range(4)]))
    bias_swa = np.ascontiguousarray(np.stack([rel_bias[idx_w, 4 + h] for h in range(4)]))
    qk_gain = np.ascontiguousarray(np.concatenate(
        [np.asarray(fox_qk_gain, np.float32), np.asarray(moba_qk_gain, np.float32),
         np.asarray(swa_qk_gain, np.float32)], axis=1))
    shared = {
        "norm_gain": np.ascontiguousarray(np.asarray(norm_gain, np.float32)),
        "w_in": np.ascontiguousarray(np.asarray(w_in, np.float32)),
        "b_forget": np.ascontiguousarray(np.asarray(b_forget, np.float32)),
        "qk_gain": qk_gain,
        "sinks": np.ascontiguousarray(np.asarray(sinks, np.float32)),
        "w_out": np.ascontiguousarray(np.asarray(w_out, np.float32)),
        "rb31": np.ascontiguousarray(rel_bias[31:32, :]),
        "bias_moba": bias_moba, "bias_swa": bias_swa, "cst": cst, "onehot": onehot,
    }
    in_maps = []
    for b in range(B):
        m = dict(shared)
        m["x"] = np.ascontiguousarray(x[b])
        in_maps.append(m)
    res = run_bass_kernel_spmd(nc, in_maps, core_ids=list(range(B)))
    return np.stack([np.asarray(r["out"], dtype=np.float32) for r in res.results], axis=0)
```

```python
import math
import numpy as np
from contextlib import ExitStack
import concourse.bass as bass
import concourse.mybir as mybir
from concourse.bass_utils import run_bass_kernel_spmd

F32 = mybir.dt.float32
BF16 = mybir.dt.bfloat16
ALU = mybir.AluOpType
AF = mybir.ActivationFunctionType
AX = mybir.AxisListType

D_MODEL = 1024
DEPTH = 2
HD = 64
IN_WIDTH = 3844
NEG = -30000.0
A_Q, A_K, A_V, A_F, A_G = 0, 256, 512, 768, 772
B_Q, B_K, B_V, B_G = 1028, 1284, 1540, 1796
C_Q, C_K, C_V, C_G = 2052, 2308, 2564, 2820
D_Q, D_K, D_V, D_G = 3076, 3332, 3460, 3588
ID_FQ, ID_FK, ID_MQ, ID_MK, ID_SQ, ID_SK, ID_WQ, ID_WK = 0, 4, 8, 12, 16, 20, 24, 28
N_QK = 30
KROWS = 80


class Buf:
    __slots__ = ("w", "r")

    def __init__(self):
        self.w = {}
        self.r = {}


class Sched:
    CE = ("pe", "act", "dve", "pool")

    def __init__(self, nc, es, n_dma_sems=16):
        self.nc = nc
        self.sems = {}
        for e in self.CE:
            self.sems[e] = es.enter_context(nc.semaphore("s_" + e))
        self.dma_names = ["d%d" % i for i in range(n_dma_sems)]
        for d in self.dma_names:
            self.sems[d] = es.enter_context(nc.semaphore("s_" + d))
        self.dma_cnt = {d: 0 for d in self.dma_names}
        self.dma_rr = 0
        self.streams = {e: [] for e in ("pe", "act", "dve", "pool", "sp")}
        self.n = {e: 0 for e in self.CE}
        self.seen = {e: {} for e in self.streams}
        self.needed = {e: set() for e in self.CE}
        self.pending = {e: {} for e in self.streams}

    def _deps(self, reads, writes, q):
        deps = dict(self.pending[q])
        self.pending[q] = {}
        for b in reads:
            for s, v in b.w.items():
                if deps.get(s, 0) < v:
                    deps[s] = v
        for b in writes:
            for s, v in b.w.items():
                if deps.get(s, 0) < v:
                    deps[s] = v
            for s, v in b.r.items():
                if deps.get(s, 0) < v:
                    deps[s] = v
        return deps

    def _waits(self, q, deps):
        waits = []
        seen = self.seen[q]
        for s, v in deps.items():
            if s == "pe" and q == "pe":
                continue
            if seen.get(s, 0) < v:
                waits.append((s, v))
                seen[s] = v
                if s in self.needed:
                    self.needed[s].add(v)
        return waits

    def _mark(self, sem, val, reads, writes):
        for b in reads:
            if b.r.get(sem, 0) < val:
                b.r[sem] = val
        for b in writes:
            if b.w.get(sem, 0) < val:
                b.w[sem] = val

    def op(self, eng, fn, reads=(), writes=()):
        deps = self._deps(reads, writes, eng)
        waits = self._waits(eng, deps)
        self.n[eng] += 1
        idx = self.n[eng]
        self.streams[eng].append((waits, _record(fn), eng, idx, False))
        self._mark(eng, idx, reads, writes)

    def dma(self, fn, reads=(), writes=(), q="sp"):
        deps = self._deps(reads, writes, q)
        d = self.dma_names[self.dma_rr]
        self.dma_rr = (self.dma_rr + 1) % len(self.dma_names)
        if self.dma_cnt[d] > 0 and deps.get(d, 0) < self.dma_cnt[d]:
            deps[d] = self.dma_cnt[d]
        waits = self._waits(q, deps)
        self.dma_cnt[d] += 16
        val = self.dma_cnt[d]
        self.streams[q].append((waits, _record(fn), d, val, True))
        self._mark(d, val, reads, writes)

    def barrier(self):
        snap = {e: self.n[e] for e in self.CE if self.n[e] > 0}
        for d in self.dma_names:
            if self.dma_cnt[d] > 0:
                snap[d] = self.dma_cnt[d]
        for q in self.pending:
            p = self.pending[q]
            for s, v in snap.items():
                if p.get(s, 0) < v:
                    p[s] = v

    def final_wait(self, q="sp"):
        self.barrier()
        deps = dict(self.pending[q])
        self.pending[q] = {}
        waits = self._waits(q, deps)
        self.streams[q].append((waits, None, None, None, False))

    def emit(self, block):
        cmap = {}
        for e in self.CE:
            ks = sorted(self.needed[e])
            cmap[e] = {k: i + 1 for i, k in enumerate(ks)}
        sems = self.sems
        streams = self.streams

        def runner(ename):
            def body(engobj):
                for (waits, fn, sem, idx, is_dma) in streams[ename]:
                    for (s, v) in waits:
                        vv = cmap[s][v] if s in cmap else v
                        engobj.wait_ge(sems[s], vv)
                    if fn is None:
                        continue
                    ins = getattr(engobj, fn[0])(*fn[1], **fn[2])
                    if is_dma:
                        ins.then_inc(sems[sem], 16)
                    elif idx in cmap[sem]:
                        ins.then_inc(sems[sem], 1)
            return body
        block.tensor(runner("pe"))
        block.scalar(runner("act"))
        block.vector(runner("dve"))
        block.gpsimd(runner("pool"))
        block.sync(runner("sp"))


class _Rec:
    def __init__(self):
        self.call = None

    def __getattr__(self, name):
        def f(*a, **k):
            self.call = (name, a, k)
            return None
        return f


def _record(fn):
    r = _Rec()
    fn(r)
    assert r.call is not None
    return r.call


class Rot:
    def __init__(self, items):
        self.items = items
        self.i = 0

    def next(self):
        it = self.items[self.i]
        self.i = (self.i + 1) % len(self.items)
        return it


def build(T, L=DEPTH, dbg=False):
    NT = T // 128
    NG = T // 512
    NB = T // 256
    nc = bass.Bass("TRN2", target_bir_lowering=False)
    x_in = nc.dram_tensor("x", [T, D_MODEL], F32, kind="ExternalInput").ap()
    norm_gain = nc.dram_tensor("norm_gain", [DEPTH, D_MODEL], F32, kind="ExternalInput").ap()
    w_in = nc.dram_tensor("w_in", [DEPTH, D_MODEL, IN_WIDTH], F32, kind="ExternalInput").ap()
    b_forget = nc.dram_tensor("b_forget", [DEPTH, 4], F32, kind="ExternalInput").ap()
    qk_gain = nc.dram_tensor("qk_gain", [DEPTH, 6, 64], F32, kind="ExternalInput").ap()
    sinks = nc.dram_tensor("sinks", [DEPTH, 4], F32, kind="ExternalInput").ap()
    w_out = nc.dram_tensor("w_out", [DEPTH, D_MODEL, D_MODEL], F32, kind="ExternalInput").ap()
    rb31 = nc.dram_tensor("rb31", [1, 8], F32, kind="ExternalInput").ap()
    bias_moba = nc.dram_tensor("bias_moba", [4, 8, 128, 128], F32, kind="ExternalInput").ap()
    bias_swa = nc.dram_tensor("bias_swa", [4, 2, 128, 128], F32, kind="ExternalInput").ap()
    cst = nc.dram_tensor("cst", [6, 128, 128], F32, kind="ExternalInput").ap()
    onehot = nc.dram_tensor("onehot", [16, T], F32, kind="ExternalInput").ap()
    out = nc.dram_tensor("out", [T, D_MODEL], F32, kind="ExternalOutput").ap()
    x1 = nc.dram_tensor("x1_scr", [T, D_MODEL], F32, kind="Internal").ap()
    qk_scr = nc.dram_tensor("qk_scr", [N_QK, KROWS, T], BF16, kind="Internal").ap()
    g_scr = nc.dram_tensor("g_scr", [8, 128, T], BF16, kind="Internal").ap()
    v_scr = nc.dram_tensor("v_scr", [T, 14, 192], BF16, kind="Internal").ap()
    if dbg:
        dbg_y = nc.dram_tensor("dbg_y", [8, 128, T], F32, kind="ExternalOutput").ap()

    with ExitStack() as es:
        S = Sched(nc, es)

        _uid = [0]

        def sb(name, shape, dt, scope=es):
            _uid[0] += 1
            return scope.enter_context(nc.sbuf_tensor("%s_%d" % (name, _uid[0]), shape, dt))

        PS = [es.enter_context(nc.psum_tensor("ps%d" % i, [128, 512], F32)) for i in range(8)]
        BPS = [Buf() for _ in range(8)]

        ident_b = sb("ident_b", [128, 128], BF16)
        ident_f = sb("ident_f", [128, 128], F32)
        tri_f = sb("tri_f", [128, 128], F32)
        ones_f = sb("ones_f", [128, 128], F32)
        ones_b = sb("ones_b", [128, 128], BF16)
        negU_b = sb("negU_b", [128, 128], BF16)
        blk_b = sb("blk_b", [128, 128], BF16)
        mask_b = sb("mask_b", [128, 3, 128], BF16)
        Bconst = Buf()
        gcols = sb("gcols", [128, DEPTH, 6], F32)
        ngain = sb("ngain", [128, DEPTH, D_MODEL], F32)
        bfg = sb("bfg", [128, DEPTH, 4], F32)
        esink = sb("esink", [128, DEPTH, 4], F32)
        rb31_t = sb("rb31_t", [128, 8], F32)
        biasM = sb("biasM", [128, 4, 8, 128], BF16)
        biasW = sb("biasW", [128, 4, 2, 128], BF16)
        xf_all = sb("xf_all", [128, NT, 4], F32)
        BxF = Buf()
        kmean = sb("kmean", [128, 2, NB], F32)
        Bkm = Buf()
        c_all = sb("c_all", [128, NT, 4], F32)
        Bc = Buf()

        with ExitStack() as e0:
            stg = sb("c_stg", [128, 6, 128], F32, e0)
            Bstg = Buf()
            S.dma(lambda e: e.dma_start(out=stg[:], in_=cst.rearrange("a p n -> p a n")), writes=[Bstg])
            S.op("dve", lambda e: e.tensor_copy(out=ident_f[:], in_=stg[:, 0, :]), reads=[Bstg], writes=[Bconst])
            S.op("dve", lambda e: e.tensor_copy(out=ident_b[:], in_=stg[:, 0, :]), reads=[Bstg], writes=[Bconst])
            S.op("dve", lambda e: e.tensor_copy(out=tri_f[:], in_=stg[:, 1, :]), reads=[Bstg], writes=[Bconst])
            S.op("dve", lambda e: e.tensor_copy(out=negU_b[:], in_=stg[:, 2, :]), reads=[Bstg], writes=[Bconst])
            S.op("dve", lambda e: e.tensor_copy(out=blk_b[:], in_=stg[:, 3, :]), reads=[Bstg], writes=[Bconst])
            S.op("dve", lambda e: e.memset(ones_f[:], 1.0), writes=[Bconst])
            S.op("dve", lambda e: e.memset(ones_b[:], 1.0), writes=[Bconst])
            S.op("dve", lambda e: e.tensor_copy(out=mask_b[:, 0, :], in_=stg[:, 4, :]), reads=[Bstg], writes=[Bconst])
            S.op("dve", lambda e: e.tensor_copy(out=mask_b[:, 1, :], in_=stg[:, 5, :]), reads=[Bstg], writes=[Bconst])
            S.op("dve", lambda e: e.tensor_scalar(out=mask_b[:, 2, :], in0=stg[:, 4, :], scalar1=-1.0, scalar2=NEG,
                                                  op0=ALU.mult, op1=ALU.add), reads=[Bstg], writes=[Bconst])
            for l in range(DEPTH):
                S.dma(lambda e, l=l: e.dma_start(out=ngain[:, l, :], in_=norm_gain[l:l + 1, :].partition_broadcast(128)),
                      writes=[Bconst])
                S.dma(lambda e, l=l: e.dma_start(out=bfg[:, l, :], in_=b_forget[l:l + 1, :].partition_broadcast(128)),
                      writes=[Bconst])
                S.dma(lambda e, l=l: e.dma_start(out=esink[:, l, :], in_=sinks[l:l + 1, :].partition_broadcast(128)),
                      writes=[Bconst])
                for half in range(2):
                    S.dma(lambda e, l=l, half=half: e.dma_start(
                        out=gcols[half * 64:(half + 1) * 64, l, :], in_=qk_gain[l].rearrange("s d -> d s"),
                        allow_slow_non_contiguous=True), writes=[Bconst])
            S.dma(lambda e: e.dma_start(out=rb31_t[:], in_=rb31[0:1, :].partition_broadcast(128)), writes=[Bconst])
            for l in range(DEPTH):
                for j in (0, 2, 4):
                    S.op("dve", lambda e, l=l, j=j: e.tensor_scalar(out=gcols[:, l, j:j + 1], in0=gcols[:, l, j:j + 1],
                                                                    scalar1=0.125, scalar2=None, op0=ALU.mult),
                         reads=[Bconst], writes=[Bconst])
                S.op("act", lambda e, l=l: e.activation(out=esink[:, l, :], in_=esink[:, l, :], func=AF.Exp),
                     reads=[Bconst], writes=[Bconst])
            bst = [sb("b_stg%d" % i, [128, 8, 128], F32, e0) for i in range(2)]
            Bbst = [Buf(), Buf()]
            for h in range(4):
                t_, b_ = bst[h % 2], Bbst[h % 2]
                S.dma(lambda e, h=h, t_=t_: e.dma_start(out=t_[:], in_=bias_moba[h].rearrange("a p n -> p a n")), writes=[b_])
                S.op("dve", lambda e, h=h, t_=t_: e.tensor_scalar(out=t_[:], in0=t_[:], scalar1=rb31_t[:, h:h + 1],
                                                                  scalar2=None, op0=ALU.subtract),
                     reads=[b_, Bconst], writes=[b_])
                S.op("dve", lambda e, h=h, t_=t_: e.tensor_tensor(out=t_[:, 0, :], in0=t_[:, 0, :], in1=stg[:, 4, :],
                                                                  op=ALU.add), reads=[b_, Bstg], writes=[b_])
                S.op("dve", lambda e, h=h, t_=t_: e.tensor_copy(out=biasM[:, h, :, :], in_=t_[:]), reads=[b_], writes=[Bconst])
            for h in range(4):
                t_, b_ = bst[h % 2], Bbst[h % 2]
                S.dma(lambda e, h=h, t_=t_: e.dma_start(out=t_[:, 0:2, :], in_=bias_swa[h].rearrange("a p n -> p a n")),
                      writes=[b_])
                S.op("dve", lambda e, t_=t_: e.tensor_tensor(out=t_[:, 0, :], in0=t_[:, 0, :], in1=stg[:, 4, :],
                                                             op=ALU.add), reads=[b_, Bstg], writes=[b_])
                S.op("dve", lambda e, t_=t_: e.tensor_tensor(out=t_[:, 1, :], in0=t_[:, 1, :], in1=stg[:, 4, :],
                                                             op=ALU.subtract), reads=[b_, Bstg], writes=[b_])
                S.op("dve", lambda e, t_=t_: e.tensor_scalar(out=t_[:, 1, :], in0=t_[:, 1, :], scalar1=NEG,
                                                             scalar2=None, op0=ALU.add), reads=[b_], writes=[b_])
                S.op("dve", lambda e, h=h, t_=t_: e.tensor_copy(out=biasW[:, h, :, :], in_=t_[:, 0:2, :]),
                     reads=[b_], writes=[Bconst])
            oh = sb("oh_stg", [KROWS, T], F32, e0)
            ohb = sb("oh_b", [KROWS, T], BF16, e0)
            Boh = Buf()
            S.dma(lambda e: e.dma_start(out=oh[64:80, :], in_=onehot[:, :]), writes=[Boh])
            S.op("dve", lambda e: e.tensor_copy(out=ohb[64:80, :], in_=oh[64:80, :]), reads=[Boh], writes=[Boh])
            Bqk = [Buf() for _ in range(N_QK)]
            for h in range(4):
                S.dma(lambda e, h=h: e.dma_start(out=qk_scr[ID_MK + h, 64:80, :], in_=ohb[64:80, :]), reads=[Boh],
                      writes=[Bqk[ID_MK + h]])
            onesr = sb("onesr", [KROWS, T], BF16, e0)
            Bor = Buf()
            S.op("pool", lambda e: e.memset(onesr[64:70, :], 1.0), writes=[Bor])
            for h in range(4):
                S.dma(lambda e, h=h: e.dma_start(out=qk_scr[ID_FK + h, 64:67, :], in_=onesr[64:67, :]), reads=[Bor],
                      writes=[Bqk[ID_FK + h]])
                S.dma(lambda e, h=h: e.dma_start(out=qk_scr[ID_FQ + h, 67:70, :], in_=onesr[64:67, :]), reads=[Bor],
                      writes=[Bqk[ID_FQ + h]])
            S.barrier()
        Bg = Buf()
        Bv = Buf()
        Bx1 = Buf()

        def run_pipeline(items):
            n = len(items)
            maxoff = max(o for it in items for (o, _) in it)
            for s_ in range(n + maxoff):
                for off in range(maxoff + 1):
                    t = s_ - off
                    if 0 <= t < n:
                        for (o, fn) in items[t]:
                            if o == off:
                                fn()

        def phase1(l, xsrc):
            with ExitStack() as e1:
                win = sb("win", [128, 8, IN_WIDTH], BF16, e1)
                Bwin = [Buf() for _ in range(8)]
                xts = Rot([(sb("xt%d" % i, [128, D_MODEL], F32, e1), Buf()) for i in range(3)])
                hns = Rot([(sb("hn%d" % i, [128, D_MODEL], BF16, e1), Buf()) for i in range(2)])
                hnTs = Rot([(sb("hnT%d" % i, [128, 8, 512], BF16, e1), Buf()) for i in range(2)])
                junk = sb("junk", [128, D_MODEL], BF16, e1)
                Bjunk = Buf()
                stat = Rot([(sb("stat%d" % i, [128, 4], F32, e1), Buf()) for i in range(3)])
                sqs = Rot([(sb("sq%d" % i, [128, 512], BF16, e1), Buf()) for i in range(3)])
                lns = Rot([(sb("ln%d" % i, [128, 512], F32, e1), Buf()) for i in range(3)])
                rss = Rot([(sb("rs%d" % i, [128, 512], F32, e1), Buf()) for i in range(3)])
                kn32 = Rot([(sb("kn%d" % i, [128, 512], F32, e1), Buf()) for i in range(2)])
                ost = Rot([(sb("ost%d" % i, [128, 512], BF16, e1), Buf()) for i in range(4)])
                vst = Rot([(sb("vst%d" % i, [128, 14, 192], BF16, e1), Buf()) for i in range(2)])
                for (t_, b_) in vst.items:
                    S.op("dve", lambda e: e.memset(t_[:], 1.0), writes=[b_])
                psF = Rot([(PS[i], BPS[i]) for i in (0, 1, 2)])
                psQ = Rot([(PS[i], BPS[i]) for i in (3,)])
                psT = Rot([(PS[i], BPS[i]) for i in (4, 5)])
                psV = Rot([(PS[i], BPS[i]) for i in (6, 7)])
                wv = w_in[l].rearrange("(c p) e -> p c e", p=128)
                wst = Rot([(sb("wst%d" % i, [128, 8, 256], F32, e1), Buf()) for i in range(5)])
                for cc in range((IN_WIDTH + 255) // 256):
                    c0 = cc * 256
                    cw = min(256, IN_WIDTH - c0)
                    t_, b_ = wst.next()
                    S.dma(lambda e: e.dma_start(out=t_[:, :, 0:cw], in_=wv[:, :, c0:c0 + cw]), writes=[b_])
                    if cc % 2 == 0:
                        S.op("dve", lambda e: e.tensor_copy(out=win[:, :, c0:c0 + cw], in_=t_[:, :, 0:cw]),
                             reads=[b_], writes=[Bwin[c0 // 512]])
                    else:
                        S.op("act", lambda e: e.activation(out=win[:, :, c0:c0 + cw], in_=t_[:, :, 0:cw], func=AF.Copy),
                             reads=[b_], writes=[Bwin[c0 // 512]])
                S.op("dve", lambda e: e.memset(kmean[:], 0.0), writes=[Bkm])

                def wbufs(c0, w):
                    return [Bwin[k] for k in range(c0 // 512, (c0 + w - 1) // 512 + 1)]

                def prep_item(i, hnT, BhnT, j):
                    xt, bx = xts.next()
                    st, bst_ = stat.next()
                    hn, bhn = hns.next()
                    pt, bpt = psT.next()
                    ptb = pt[:].bitcast(BF16)

                    def f0():
                        S.dma(lambda e: e.dma_start(out=xt[:], in_=xsrc[i * 128:(i + 1) * 128, :]), reads=[Bx1], writes=[bx])
                        S.op("act", lambda e: e.activation(out=junk[:], in_=xt[:], func=AF.Square, accum_out=st[:, 0:1]),
                             reads=[bx], writes=[Bjunk, bst_])
                        S.op("act", lambda e: e.activation(out=st[:, 1:2], in_=st[:, 0:1], func=AF.Ln,
                                                           scale=1.0 / D_MODEL, bias=1e-6), reads=[bst_], writes=[bst_])
                        S.op("act", lambda e: e.activation(out=st[:, 2:3], in_=st[:, 1:2], func=AF.Exp, scale=-0.5),
                             reads=[bst_], writes=[bst_])
                        S.op("dve", lambda e: e.scalar_tensor_tensor(
                            out=hn[:], in0=xt[:], scalar=st[:, 2:3], in1=ngain[:, l, :], op0=ALU.mult, op1=ALU.mult),
                            reads=[bx, bst_, Bconst], writes=[bhn])

                    def f1():
                        for c in range(8):
                            S.op("pe", lambda e: e.transpose(ptb[:, c * 128:(c + 1) * 128], hn[:, c * 128:(c + 1) * 128],
                                                             ident_b[:]), reads=[bhn, Bconst], writes=[bpt])

                    def f2():
                        if j % 2 == 0:
                            S.op("act", lambda e: e.activation(
                                out=hnT[:, :, j * 128:(j + 1) * 128], in_=ptb.rearrange("p (c n) -> p c n", n=128),
                                func=AF.Copy), reads=[bpt], writes=[BhnT])
                        else:
                            S.op("dve", lambda e: e.tensor_copy(
                                out=hnT[:, :, j * 128:(j + 1) * 128], in_=ptb.rearrange("p (c n) -> p c n", n=128)),
                                reads=[bpt], writes=[BhnT])
                    return [(0, f0), (1, f1), (2, f2)]

                def qk_item(hnT, BhnT, g, col0, gj, dst_id, scale_only=False, is_mk=False):
                    M = 128
                    pp, bp = psF.next()
                    o_t, o_b = ost.next()
                    if not scale_only:
                        sq, bsq = sqs.next()
                        pq, bq = psQ.next()
                        ln_t, ln_b = lns.next()
                        rs_t, rs_b = rss.next()
                        if is_mk:
                            k32, bk32 = kn32.next()

                    def f0():
                        for c in range(8):
                            S.op("pe", lambda e: e.matmul(pp[0:M, :], lhsT=win[:, c, col0:col0 + M], rhs=hnT[:, c, :],
                                                          start=(c == 0), stop=(c == 7)),
                                 reads=wbufs(col0, M) + [BhnT], writes=[bp])

                    def store():
                        for hh in range(2):
                            S.dma(lambda e: e.dma_start(out=qk_scr[dst_id + hh, 0:64, g * 512:(g + 1) * 512],
                                                        in_=o_t[hh * 64:(hh + 1) * 64, :]),
                                  reads=[o_b], writes=[Bqk[dst_id + hh]])

                    def f1():
                        if scale_only:
                            sc = 0.125 if gj else 1.0
                            S.op("act", lambda e: e.activation(out=o_t[0:M, :], in_=pp[0:M, :], func=AF.Copy, scale=sc),
                                 reads=[bp], writes=[o_b])
                            store()
                        else:
                            S.op("act", lambda e: e.activation(out=sq[0:M, :], in_=pp[0:M, :], func=AF.Square),
                                 reads=[bp], writes=[bsq])

                    def f2():
                        if scale_only:
                            return
                        S.op("pe", lambda e: e.matmul(pq[0:M, :], lhsT=blk_b[0:M, 0:M], rhs=sq[0:M, :], start=True,
                                                      stop=True), reads=[bsq, Bconst], writes=[bq])
                        S.op("act", lambda e: e.activation(out=ln_t[0:M, :], in_=pq[0:M, :], func=AF.Ln,
                                                           scale=1.0 / 64.0, bias=1e-6), reads=[bq], writes=[ln_b])
                        S.op("act", lambda e: e.activation(out=rs_t[0:M, :], in_=ln_t[0:M, :], func=AF.Exp, scale=-0.5),
                             reads=[ln_b], writes=[rs_b])
                        if is_mk:
                            S.op("dve", lambda e: e.scalar_tensor_tensor(
                                out=k32[:, :], in0=pp[:, :], scalar=gcols[:, l, gj:gj + 1], in1=rs_t[:, :],
                                op0=ALU.mult, op1=ALU.mult), reads=[bp, rs_b, Bconst], writes=[bk32])
                            S.op("act", lambda e: e.activation(out=o_t[:, :], in_=k32[:, :], func=AF.Copy), reads=[bk32],
                                 writes=[o_b])
                            pr = (dst_id - ID_MK) // 2
                            S.op("dve", lambda e: e.tensor_reduce(
                                out=kmean[:, pr, 2 * g:2 * g + 2], in_=k32[:, :].rearrange("p (a b) -> p a b", b=256),
                                axis=AX.X, op=ALU.add), reads=[bk32], writes=[Bkm])
                        else:
                            S.op("dve", lambda e: e.scalar_tensor_tensor(
                                out=o_t[0:M, :], in0=pp[0:M, :], scalar=gcols[0:M, l, gj:gj + 1], in1=rs_t[0:M, :],
                                op0=ALU.mult, op1=ALU.mult), reads=[bp, rs_b, Bconst], writes=[o_b])
                        store()
                    return [(0, f0), (1, f1), (2, f2)]

                def gate_item(hnT, BhnT, g, col0, gid):
                    pp, bp = psF.next()
                    o_t, o_b = ost.next()

                    def f0():
                        for c in range(8):
                            S.op("pe", lambda e: e.matmul(pp[:, :], lhsT=win[:, c, col0:col0 + 128], rhs=hnT[:, c, :],
                                                          start=(c == 0), stop=(c == 7)),
                                 reads=wbufs(col0, 128) + [BhnT], writes=[bp])

                    def f1():
                        S.op("act", lambda e: e.activation(out=o_t[:, :], in_=pp[:, :], func=AF.Silu),
                             reads=[bp], writes=[o_b])
                        S.dma(lambda e: e.dma_start(out=g_scr[gid, :, g * 512:(g + 1) * 512], in_=o_t[:, :]),
                              reads=[o_b], writes=[Bg])
                    return [(0, f0), (1, f1)]

                def v_item(hnT, BhnT, g, j):
                    i = g * 4 + j
                    pa, ba = psV.next()
                    pb, bb = psV.next()
                    vt, bvt = vst.next()

                    def f0():
                        specs = [(pa, ba, 0, A_V, 256), (pa, ba, 256, B_V, 256), (pb, bb, 0, C_V, 256),
                                 (pb, bb, 256, D_V, 128), (pb, bb, 384, A_F, 4)]
                        for (pp, bp, o0, c0, w) in specs:
                            for c in range(8):
                                S.op("pe", lambda e: e.matmul(
                                    pp[:, o0:o0 + w], lhsT=hnT[:, c, j * 128:(j + 1) * 128], rhs=win[:, c, c0:c0 + w],
                                    start=(c == 0), stop=(c == 7)), reads=wbufs(c0, w) + [BhnT], writes=[bp])

                    def f1():
                        S.op("act", lambda e: e.activation(
                            out=vt[:, 0:8, 64:128], in_=pa[:, 0:512].rearrange("p (h d) -> p h d", d=64), func=AF.Copy),
                            reads=[ba], writes=[bvt])
                        S.op("dve", lambda e: e.tensor_copy(
                            out=vt[:, 8:14, 64:128], in_=pb[:, 0:384].rearrange("p (h d) -> p h d", d=64)),
                            reads=[bb], writes=[bvt])
                        S.op("dve", lambda e: e.tensor_tensor(out=xf_all[:, i, :], in0=pb[:, 384:388], in1=bfg[:, l, :],
                                                              op=ALU.add), reads=[bb, Bconst], writes=[BxF])
                        S.dma(lambda e: e.dma_start(out=v_scr[i * 128:(i + 1) * 128, :, :], in_=vt[:]), reads=[bvt],
                              writes=[Bv])
                    return [(0, f0), (1, f1)]

                items = []
                hn_bufs = [hnTs.next() for _ in range(2)]
                for j in range(4):
                    items.append(prep_item(j, hn_bufs[0][0], hn_bufs[0][1], j))
                items.append([])
                items.append([])
                for g in range(NG):
                    hnT, BhnT = hn_bufs[g % 2]
                    gi = []
                    for pr in range(2):
                        gi.append(qk_item(hnT, BhnT, g, A_Q + pr * 128, 0, ID_FQ + 2 * pr))
                        gi.append(qk_item(hnT, BhnT, g, A_K + pr * 128, 1, ID_FK + 2 * pr))
                    gi.append(v_item(hnT, BhnT, g, 0))
                    for pr in range(2):
                        gi.append(qk_item(hnT, BhnT, g, B_Q + pr * 128, 2, ID_MQ + 2 * pr))
                        gi.append(qk_item(hnT, BhnT, g, B_K + pr * 128, 3, ID_MK + 2 * pr, is_mk=True))
                    gi.append(v_item(hnT, BhnT, g, 1))
                    for pr in range(2):
                        gi.append(qk_item(hnT, BhnT, g, C_Q + pr * 128, 1, ID_SQ + 2 * pr, scale_only=True))
                        gi.append(qk_item(hnT, BhnT, g, C_K + pr * 128, 0, ID_SK + 2 * pr, scale_only=True))
                    gi.append(v_item(hnT, BhnT, g, 2))
                    for pr in range(2):
                        gi.append(qk_item(hnT, BhnT, g, D_Q + pr * 128, 4, ID_WQ + 2 * pr))
                    gi.append(qk_item(hnT, BhnT, g, D_K, 5, ID_WK))
                    gi.append(v_item(hnT, BhnT, g, 3))
                    for m, gc in enumerate((A_G, B_G, C_G, D_G)):
                        for pr in range(2):
                            gi.append(gate_item(hnT, BhnT, g, gc + pr * 128, m * 2 + pr))
                    if g + 1 < NG:
                        nh, nb = hn_bufs[(g + 1) % 2]
                        pos = [2, 7, 12, 17]
                        for j in range(4):
                            gi.insert(pos[j] + j, prep_item((g + 1) * 4 + j, nh, nb, j))
                    items.extend(gi)
                run_pipeline(items)
                S.barrier()

        def phase2(l, yT, ByT):
            with ExitStack() as e2:
                QAs = Rot([(sb("QA%d" % i, [KROWS, T], BF16, e2), Buf()) for i in range(2)])
                KAs = Rot([(sb("KA%d" % i, [KROWS, T], BF16, e2), Buf()) for i in range(2)])
                VAs = Rot([(sb("VA%d" % i, [128, NT, 192], BF16, e2), Buf()) for i in range(2)])
                GTs = Rot([(sb("GT%d" % i, [128, T], BF16, e2), Buf()) for i in range(2)])
                Ps = Rot([(sb("P%d" % i, [128, 512], BF16, e2), Buf()) for i in range(3)])
                rdn = Rot([(sb("rdn%d" % i, [128, 512], F32, e2), Buf()) for i in range(2)])
                tmo = Rot([(sb("tmo%d" % i, [128, 512], F32, e2), Buf()) for i in range(2)])
                psS = Rot([(PS[i], BPS[i]) for i in (0, 1, 2)])
                psO = Rot([(PS[i], BPS[i]) for i in (3, 4)])
                psM = Rot([(PS[i], BPS[i]) for i in (5, 6)])
                psC = Rot([(PS[i], BPS[i]) for i in (7,)])

                def load_head(qid, kid, vh, nq, nk=None):
                    nk = nq if nk is None else nk
                    QA, bQ = QAs.next()
                    KA, bK = KAs.next()
                    VA, bV = VAs.next()
                    S.dma(lambda e: e.dma_start(out=QA[0:nq, :], in_=qk_scr[qid, 0:nq, :]), reads=[Bqk[qid]],
                          writes=[bQ])
                    S.dma(lambda e: e.dma_start(out=KA[0:nk, :], in_=qk_scr[kid, 0:nk, :]), reads=[Bqk[kid]],
                          writes=[bK])
                    S.dma(lambda e: e.dma_start(out=VA[:], in_=v_scr[:, vh, :].rearrange("(n p) d -> p n d", p=128)),
                          reads=[Bv], writes=[bV])
                    return QA, bQ, KA, bK, VA, bV

                def load_gate(gid):
                    GT, bG = GTs.next()
                    S.dma(lambda e: e.dma_start(out=GT[:], in_=g_scr[gid]), reads=[Bg], writes=[bG])
                    return GT, bG

                def finalize(O, bO, par, GT, bG, chunk, g, normalize, extra_den=None):
                    orow = slice(0, 64) if par == 0 else slice(64, 128)
                    drow = slice(64, 128) if par == 0 else slice(0, 64)
                    cols = slice(g * 512, (g + 1) * 512)
                    if normalize:
                        r_t, r_b = rdn.next()
                        t_t, t_b = tmo.next()
                        if extra_den is not None:
                            S.op("act", lambda e: e.activation(out=r_t[orow, :], in_=O[drow, :], func=AF.Ln,
                                                               bias=extra_den[drow]), reads=[bO, Bconst], writes=[r_b])
                            S.op("act", lambda e: e.activation(out=r_t[orow, :], in_=r_t[orow, :], func=AF.Exp,
                                                               scale=-1.0), reads=[r_b], writes=[r_b])
                            S.op("dve", lambda e: e.tensor_tensor(out=t_t[orow, :], in0=O[orow, :], in1=r_t[orow, :],
                                                                  op=ALU.mult), reads=[bO, r_b], writes=[t_b])
                            S.op("dve", lambda e: e.tensor_tensor(out=yT[orow, chunk, cols], in0=t_t[orow, :],
                                                                  in1=GT[orow, cols], op=ALU.mult),
                                 reads=[t_b, bG], writes=[ByT])
                        else:
                            S.op("dve", lambda e: e.reciprocal(out=r_t[orow, :], in_=O[drow, :]), reads=[bO],
                                 writes=[r_b])
                            S.op("dve", lambda e: e.tensor_tensor(out=t_t[orow, :], in0=O[orow, :], in1=r_t[orow, :],
                                                                  op=ALU.mult), reads=[bO, r_b], writes=[t_b])
                            S.op("pool", lambda e: e.tensor_tensor(out=yT[orow, chunk, cols], in0=t_t[orow, :],
                                                                   in1=GT[orow, cols], op=ALU.mult),
                                 reads=[t_b, bG], writes=[ByT])
                    else:
                        S.op("dve", lambda e: e.tensor_tensor(out=yT[orow, chunk, cols], in0=O[orow, :],
                                                              in1=GT[orow, cols], op=ALU.mult),
                             reads=[bO, bG], writes=[ByT])

                def vslice(VA, ki, par):
                    return VA[:, ki, 64:192] if par == 0 else VA[:, ki, 0:128]

                with ExitStack() as ef:
                    NF = NT * 4
                    xf2 = xf_all[:].rearrange("p n h -> p (n h)")
                    S.op("act", lambda e: e.activation(out=xf2, in_=xf2, func=AF.Exp, scale=-1.0), reads=[BxF], writes=[BxF])
                    S.op("act", lambda e: e.activation(out=xf2, in_=xf2, func=AF.Ln, bias=1.0), reads=[BxF], writes=[BxF])
                    S.op("dve", lambda e: e.tensor_scalar(out=xf2, in0=xf2, scalar1=-1.0, scalar2=None, op0=ALU.mult),
                         reads=[BxF], writes=[BxF])
                    p1, b1 = psM.next()
                    p2, b2 = psM.next()
                    S.op("pe", lambda e: e.matmul(p1[:, 0:NF], lhsT=tri_f[:], rhs=xf2, start=True, stop=True),
                         reads=[BxF, Bconst], writes=[b1])
                    S.op("pe", lambda e: e.matmul(p2[:, 0:NF], lhsT=ones_f[:], rhs=xf2, start=True, stop=True),
                         reads=[BxF, Bconst], writes=[b2])
                    tot = sb("f_tot", [128, NT, 4], F32, ef)
                    exc = sb("f_exc", [128, NT, 4], F32, ef)
                    Bt = Buf()
                    S.op("dve", lambda e: e.tensor_copy(out=tot[:].rearrange("p n h -> p (n h)"), in_=p2[:, 0:NF]),
                         reads=[b2], writes=[Bt])
                    S.op("dve", lambda e: e.memset(exc[:, 0, :], 0.0), writes=[Bt])
                    for i in range(1, NT):
                        S.op("dve", lambda e, i=i: e.tensor_tensor(out=exc[:, i, :], in0=exc[:, i - 1, :],
                                                                   in1=tot[:, i - 1, :], op=ALU.add),
                             reads=[Bt], writes=[Bt])
                    S.op("dve", lambda e: e.tensor_tensor(out=c_all[:].rearrange("p n h -> p (n h)"), in0=p1[:, 0:NF],
                                                          in1=exc[:].rearrange("p n h -> p (n h)"), op=ALU.add),
                         reads=[b1, Bt], writes=[Bc])
                    spl = sb("f_spl", [128, NT, 4, 6], BF16, ef)
                    r1 = sb("f_r1", [128, NT, 4], F32, ef)
                    r2 = sb("f_r2", [128, NT, 4], F32, ef)
                    Bs = Buf()
                    S.op("dve", lambda e: e.tensor_copy(out=spl[:, :, :, 0], in_=c_all[:]), reads=[Bc], writes=[Bs])
                    S.op("dve", lambda e: e.tensor_tensor(out=r1[:], in0=c_all[:], in1=spl[:, :, :, 0], op=ALU.subtract),
                         reads=[Bc, Bs], writes=[Bs])
                    S.op("dve", lambda e: e.tensor_copy(out=spl[:, :, :, 1], in_=r1[:]), reads=[Bs], writes=[Bs])
                    S.op("dve", lambda e: e.tensor_tensor(out=r2[:], in0=r1[:], in1=spl[:, :, :, 1], op=ALU.subtract),
                         reads=[Bs], writes=[Bs])
                    S.op("dve", lambda e: e.tensor_copy(out=spl[:, :, :, 2], in_=r2[:]), reads=[Bs], writes=[Bs])
                    S.op("dve", lambda e: e.tensor_scalar(out=spl[:, :, :, 3:6], in0=spl[:, :, :, 0:3], scalar1=-1.0,
                                                          scalar2=None, op0=ALU.mult), reads=[Bs], writes=[Bs])
                    augs = Rot([(sb("f_aug%d" % i, [KROWS, 512], BF16, ef), Buf()) for i in range(2)])
                    for h in range(4):
                        for g in range(NG):
                            pm, bm = psM.next()
                            for j in range(4):
                                i = g * 4 + j
                                S.op("pe", lambda e, pm=pm, i=i, j=j, h=h: e.matmul(
                                    pm[64:70, j * 128:(j + 1) * 128], lhsT=spl[:, i, h, :], rhs=ident_b[:],
                                    start=True, stop=True), reads=[Bs, Bconst], writes=[bm])
                            a_t, a_b = augs.next()
                            S.op("act", lambda e, pm=pm, a_t=a_t: e.activation(out=a_t[64:70, :], in_=pm[64:70, :],
                                                                               func=AF.Copy), reads=[bm], writes=[a_b])
                            S.dma(lambda e, a_t=a_t, h=h, g=g: e.dma_start(
                                out=qk_scr[ID_FQ + h, 64:67, g * 512:(g + 1) * 512], in_=a_t[64:67, :]),
                                reads=[a_b], writes=[Bqk[ID_FQ + h]])
                            S.dma(lambda e, a_t=a_t, h=h, g=g: e.dma_start(
                                out=qk_scr[ID_FK + h, 67:70, g * 512:(g + 1) * 512], in_=a_t[67:70, :]),
                                reads=[a_b], writes=[Bqk[ID_FK + h]])

                S.barrier()
                def softmax_tile(QA, qbufs, KA, bK, VA, bV, O, bO, g, ki, par, krows, bias_tiles, act_bias, fin):
                    j = max(ki - 4 * g, 0)
                    c0 = j * 128
                    last = 4 * g + 3
                    S_, bS = psS.next()
                    P, bP = Ps.next()

                    def f0():
                        S.op("pe", lambda e: e.matmul(
                            S_[:, c0:512], lhsT=KA[0:krows, ki * 128:(ki + 1) * 128],
                            rhs=QA[0:krows, g * 512 + c0:(g + 1) * 512], start=True, stop=True),
                            reads=list(qbufs) + [bK], writes=[bS])
                        if bias_tiles is None:
                            if ki >= 4 * g:
                                S.op("pe", lambda e: e.matmul(
                                    S_[:, c0:c0 + 128], lhsT=ident_b[:], rhs=mask_b[:, 0, :], start=False, stop=True),
                                    reads=[Bconst], writes=[bS])
                        else:
                            for jj in range(j, 4):
                                dl = 4 * g + jj - ki
                                if dl < 8:
                                    S.op("pe", lambda e: e.matmul(
                                        S_[:, jj * 128:(jj + 1) * 128], lhsT=ident_b[:], rhs=bias_tiles[:, dl, :],
                                        start=False, stop=True), reads=[Bconst], writes=[bS])

                    def f1():
                        if act_bias is None:
                            S.op("act", lambda e: e.activation(out=P[:, c0:512], in_=S_[:, c0:512], func=AF.Exp),
                                 reads=[bS], writes=[bP])
                        else:
                            S.op("act", lambda e: e.activation(out=P[:, c0:512], in_=S_[:, c0:512], func=AF.Exp,
                                                               bias=act_bias), reads=[bS, Bconst], writes=[bP])

                    def f2():
                        S.op("pe", lambda e: e.matmul(
                            O[:, c0:512], lhsT=vslice(VA, ki, par), rhs=P[:, c0:512], start=(ki == 0),
                            stop=(ki == last)), reads=[bP, bV], writes=[bO])
                        if ki == last:
                            fin()
                    return [(0, f0), (1, f1), (2, f2)]

                def softmax_head(QA, bQ, KA, bK, VA, bV, GT, bG, par, chunk, krows, bias_tiles=None, act_bias=None,
                                 qaug=None, hooks=None):
                    items = []
                    for g in range(NG):
                        O, bO = psO.next()
                        qbufs = [bQ] if qaug is None else [bQ, qaug[g % 2]]

                        def fin(O=O, bO=bO, g=g):
                            finalize(O, bO, par, GT, bG, chunk, g, True)
                        for ki in range(4 * g + 4):
                            it = softmax_tile(QA, qbufs, KA, bK, VA, bV, O, bO, g, ki, par, krows, bias_tiles, act_bias,
                                              fin)
                            if hooks is not None:
                                if ki == 0 and (g, 0) in hooks:
                                    it.insert(0, (0, hooks[(g, 0)]))
                                if ki == 4 * g + 3 and (g, 1) in hooks:
                                    it.append((0, hooks[(g, 1)]))
                            items.append(it)
                    run_pipeline(items)

                for h in range(4):
                    QA, bQ, KA, bK, VA, bV = load_head(ID_FQ + h, ID_FK + h, h, 70, 70)
                    if h % 2 == 0:
                        GT, bG = load_gate(h // 2)
                    softmax_head(QA, bQ, KA, bK, VA, bV, GT, bG, h % 2, h // 2, 70)

                with ExitStack() as em:
                    km_b = sb("km_b", [64, 4, NB], BF16, em)
                    Bkmb = Buf()
                    for h in range(4):
                        S.op("dve", lambda e: e.tensor_copy(out=km_b[0:64, h, :],
                                                            in_=kmean[(h % 2) * 64:(h % 2) * 64 + 64, h // 2, :]),
                             reads=[Bkm], writes=[Bkmb])
                    gms = Rot([(sb("gm%d" % i, [128, 4, 16], F32, em), Buf()) for i in range(2)])
                    t8s = Rot([(sb("t8%d" % i, [128, 4, 8], F32, em), Buf()) for i in range(2)])
                    msl = Rot([(sb("msl%d" % i, [128, 4, 16], BF16, em), Buf()) for i in range(2)])

                    def make_sel(QA, bQ, qaug, h, g):
                        st = {}

                        def part1():
                            pg, bg_ = psM.next()
                            for j in range(4):
                                i = g * 4 + j
                                S.op("pe", lambda e: e.matmul(
                                    pg[:, j * 16:j * 16 + NB], lhsT=QA[0:64, i * 128:(i + 1) * 128], rhs=km_b[0:64, h, :],
                                    start=True, stop=True), reads=[bQ, Bkmb], writes=[bg_])
                            gm, bgm = gms.next()
                            S.op("pool", lambda e: e.memset(gm[:], -1e30), writes=[bgm])
                            for j in range(4):
                                qb = (g * 4 + j) // 2
                                if qb > 0:
                                    S.op("dve", lambda e: e.tensor_copy(out=gm[:, j, 0:qb], in_=pg[:, j * 16:j * 16 + qb]),
                                         reads=[bg_], writes=[bgm])
                            t8, bt8 = t8s.next()
                            for j in range(4):
                                S.op("dve", lambda e: e.max(out=t8[:, j, :], in_=gm[:, j, :]), reads=[bgm], writes=[bt8])
                            ms, bms = msl.next()
                            for j in range(4):
                                S.op("dve", lambda e: e.tensor_scalar(
                                    out=ms[:, j, :], in0=gm[:, j, :], scalar1=t8[:, j, 2:3], scalar2=NEG, op0=ALU.is_lt,
                                    op1=ALU.mult), reads=[bgm, bt8], writes=[bms])
                                qb = (g * 4 + j) // 2
                                S.op("dve", lambda e: e.memset(ms[:, j, qb:qb + 1], 0.0), writes=[bms])
                            st["ms"] = (ms, bms)

                        def part2():
                            ms, bms = st["ms"]
                            pm, bm = psM.next()
                            for j in range(4):
                                S.op("pe", lambda e: e.matmul(
                                    pm[64:80, j * 128:(j + 1) * 128], lhsT=ms[:, j, :], rhs=ident_b[:], start=True,
                                    stop=True), reads=[bms, Bconst], writes=[bm])
                            S.op("act", lambda e: e.activation(out=QA[64:80, g * 512:(g + 1) * 512], in_=pm[64:80, :],
                                                               func=AF.Copy), reads=[bm], writes=[qaug[g % 2]])
                        return part1, part2

                    for h in range(4):
                        QA, bQ, KA, bK, VA, bV = load_head(ID_MQ + h, ID_MK + h, 4 + h, 64, 80)
                        if h % 2 == 0:
                            GT, bG = load_gate(2 + h // 2)
                        qaug = [Buf(), Buf()]
                        hooks = {}
                        p1, p2 = make_sel(QA, bQ, qaug, h, 0)
                        p1()
                        p2()
                        for g in range(NG - 1):
                            p1, p2 = make_sel(QA, bQ, qaug, h, g + 1)
                            hooks[(g, 0)] = p1
                            hooks[(g, 1)] = p2
                        softmax_head(QA, bQ, KA, bK, VA, bV, GT, bG, h % 2, 2 + h // 2, 80,
                                     bias_tiles=biasM[:, h, :, :], act_bias=rb31_t[:, h:h + 1], qaug=qaug, hooks=hooks)

                S.barrier()
                with ExitStack() as esb:
                    Es = Rot([(sb("sbE%d" % i, [128, 512], F32, esb), Buf()) for i in range(3)])
                    SPs = Rot([(sb("sbSP%d" % i, [128, 512], BF16, esb), Buf()) for i in range(3)])
                    ARs = Rot([(sb("sbAR%d" % i, [128, 512], F32, esb), Buf()) for i in range(3)])
                    carry = sb("sbcarry", [128, 512], F32, esb)
                    Bcar = Buf()
                    psT3 = Rot([(PS[i], BPS[i]) for i in (5, 6, 7)])

                    def sb_tile(QA, bQ, KA, bK, VA, bV, O, bO, g, ki, par, fin):
                        j = max(ki - 4 * g, 0)
                        c0 = j * 128
                        first = 4 * g + 3
                        Z, bZ = psS.next()
                        E, bE = Es.next()
                        SP, bSP = SPs.next()
                        Tb, bT = psT3.next()
                        AR, bAR = ARs.next()
                        P, bP = Ps.next()

                        def fA():
                            S.op("pe", lambda e: e.matmul(
                                Z[:, c0:512], lhsT=KA[0:64, ki * 128:(ki + 1) * 128],
                                rhs=QA[0:64, g * 512 + c0:(g + 1) * 512], start=True, stop=True),
                                reads=[bQ, bK], writes=[bZ])
                            if ki >= 4 * g:
                                S.op("pe", lambda e: e.matmul(
                                    Z[:, c0:c0 + 128], lhsT=ident_b[:], rhs=mask_b[:, 1, :], start=False, stop=True),
                                    reads=[Bconst], writes=[bZ])

                        def fB():
                            S.op("act", lambda e: e.activation(out=E[:, c0:512], in_=Z[:, c0:512], func=AF.Exp),
                                 reads=[bZ], writes=[bE])
                            S.op("act", lambda e: e.activation(out=SP[:, c0:512], in_=E[:, c0:512], func=AF.Ln, bias=1.0),
                                 reads=[bE], writes=[bSP])

                        def fC():
                            S.op("pe", lambda e: e.matmul(Z[:, c0:512], lhsT=negU_b[:], rhs=SP[:, c0:512], start=False,
                                                          stop=True), reads=[bSP, Bconst], writes=[bZ])
                            S.op("pe", lambda e: e.matmul(Tb[:, c0:512], lhsT=ones_b[:], rhs=SP[:, c0:512], start=True,
                                                          stop=True), reads=[bSP, Bconst], writes=[bT])

                        def fD():
                            if ki == first:
                                S.op("dve", lambda e: e.memset(carry[:], 0.0), writes=[Bcar])
                            S.op("dve", lambda e: e.tensor_tensor(out=AR[:, c0:512], in0=Z[:, c0:512],
                                                                  in1=carry[:, c0:512], op=ALU.subtract),
                                 reads=[bZ, Bcar], writes=[bAR])
                            S.op("dve", lambda e: e.tensor_tensor(out=carry[:, c0:512], in0=Tb[:, c0:512],
                                                                  in1=carry[:, c0:512], op=ALU.add),
                                 reads=[bT, Bcar], writes=[Bcar])

                        def fE():
                            S.op("act", lambda e: e.activation(out=P[:, c0:512], in_=AR[:, c0:512], func=AF.Exp),
                                 reads=[bAR], writes=[bP])

                        def fF():
                            lh = VA[:, ki, 64:128] if par == 0 else VA[:, ki, 0:128]
                            orows = slice(0, 64) if par == 0 else slice(0, 128)
                            S.op("pe", lambda e: e.matmul(O[orows, c0:512], lhsT=lh, rhs=P[:, c0:512],
                                                          start=(ki == first), stop=(ki == 0)),
                                 reads=[bP, bV], writes=[bO])
                            if ki == 0:
                                fin()
                        return [(0, fA), (0, fB), (1, fC), (1, fD), (2, fE), (3, fF)]

                    for h in range(4):
                        QA, bQ, KA, bK, VA, bV = load_head(ID_SQ + h, ID_SK + h, 8 + h, 64, 64)
                        par = h % 2
                        if par == 0:
                            GT, bG = load_gate(4 + h // 2)
                        items = []
                        for g in range(NG):
                            O, bO = psO.next()

                            def fin(O=O, bO=bO, g=g, par=par, GT=GT, bG=bG, h=h):
                                finalize(O, bO, par, GT, bG, 4 + h // 2, g, False)
                            for ki in range(4 * g + 3, -1, -1):
                                items.append(sb_tile(QA, bQ, KA, bK, VA, bV, O, bO, g, ki, par, fin))
                        run_pipeline(items)

                S.barrier()
                KA = None
                for h in range(4):
                    kv = h // 2
                    par = h % 2
                    if par == 0:
                        QA, bQ, KA, bK, VA, bV = load_head(ID_WQ + h, ID_WK + kv, 12 + kv, 64)
                        GT, bG = load_gate(6 + h // 2)
                    else:
                        QA, bQ = QAs.next()
                        S.dma(lambda e, QA=QA, h=h: e.dma_start(out=QA[0:64, :], in_=qk_scr[ID_WQ + h, 0:64, :]),
                              reads=[Bqk[ID_WQ + h]], writes=[bQ])
                    for g in range(NG):
                        O, bO = psO.next()
                        for jp in range(2):
                            S_, bS = psS.next()
                            for jq in range(2):
                                j = jp * 2 + jq
                                qi = g * 4 + j
                                qcols = slice(qi * 128, (qi + 1) * 128)
                                for dl in (1, 0):
                                    ki = qi - dl
                                    if ki < 0:
                                        continue
                                    sc = slice(jq * 256 + (1 - dl) * 128, jq * 256 + (1 - dl) * 128 + 128)
                                    S.op("pe", lambda e, S_=S_, sc=sc, ki=ki, qcols=qcols, QA=QA, KA=KA: e.matmul(
                                        S_[:, sc], lhsT=KA[0:64, ki * 128:(ki + 1) * 128], rhs=QA[0:64, qcols],
                                        start=True, stop=True), reads=[bQ, bK], writes=[bS])
                                    S.op("pe", lambda e, S_=S_, sc=sc, dl=dl, h=h: e.matmul(
                                        S_[:, sc], lhsT=ident_b[:], rhs=biasW[:, h, dl, :], start=False, stop=True),
                                        reads=[Bconst], writes=[bS])
                            P, bP = Ps.next()
                            a0 = 128 if (g == 0 and jp == 0) else 0
                            S.op("act", lambda e, S_=S_, P=P, a0=a0: e.activation(out=P[:, a0:512], in_=S_[:, a0:512],
                                                                                  func=AF.Exp), reads=[bS], writes=[bP])
                            for jq in range(2):
                                j = jp * 2 + jq
                                qi = g * 4 + j
                                dls = [d_ for d_ in (1, 0) if qi - d_ >= 0]
                                for n_, dl in enumerate(dls):
                                    ki = qi - dl
                                    sc = slice(jq * 256 + (1 - dl) * 128, jq * 256 + (1 - dl) * 128 + 128)
                                    S.op("pe", lambda e, O=O, P=P, sc=sc, ki=ki, j=j, n_=n_, dls=dls, VA=VA, par=par: e.matmul(
                                        O[:, j * 128:(j + 1) * 128], lhsT=vslice(VA, ki, par), rhs=P[:, sc],
                                        start=(n_ == 0), stop=(n_ == len(dls) - 1)), reads=[bP, bV], writes=[bO])
                        finalize(O, bO, par, GT, bG, 6 + h // 2, g, True, extra_den=esink[:, l, h:h + 1])
                S.barrier()

        def phase3(l, yT, ByT, xsrc, dst, Bdst):
            with ExitStack() as e3:
                wo = sb("wo", [128, 8, D_MODEL], BF16, e3)
                Bwo = Buf()
                wst = Rot([(sb("wost%d" % i, [128, 8, 512], F32, e3), Buf()) for i in range(2)])
                wv = w_out[l].rearrange("(c p) e -> p c e", p=128)
                for cc in range(2):
                    t_, b_ = wst.next()
                    S.dma(lambda e, t_=t_, cc=cc: e.dma_start(out=t_[:], in_=wv[:, :, cc * 512:(cc + 1) * 512]), writes=[b_])
                    S.op("dve" if cc == 0 else "pool",
                         lambda e, t_=t_, cc=cc: e.tensor_copy(out=wo[:, :, cc * 512:(cc + 1) * 512], in_=t_[:]),
                         reads=[b_], writes=[Bwo])
                xts = Rot([(sb("x3t%d" % i, [128, D_MODEL], F32, e3), Buf()) for i in range(6)])
                ots = Rot([(sb("o3t%d" % i, [128, D_MODEL], F32, e3), Buf()) for i in range(6)])
                psA = Rot([(PS[i], BPS[i]) for i in range(8)])
                for i in range(NT):
                    xt, bx = xts.next()
                    S.dma(lambda e, xt=xt, i=i: e.dma_start(out=xt[:], in_=xsrc[i * 128:(i + 1) * 128, :]),
                          reads=[Bx1], writes=[bx])
                    ot, bo = ots.next()
                    for half in range(2):
                        pp, bp = psA.next()
                        for c in range(8):
                            S.op("pe", lambda e, pp=pp, c=c, i=i, half=half: e.matmul(
                                pp[:, :], lhsT=yT[:, c, i * 128:(i + 1) * 128], rhs=wo[:, c, half * 512:(half + 1) * 512],
                                start=(c == 0), stop=(c == 7)), reads=[ByT, Bwo], writes=[bp])
                        S.op("dve", lambda e, pp=pp, xt=xt, ot=ot, half=half: e.tensor_tensor(
                            out=ot[:, half * 512:(half + 1) * 512], in0=pp[:, :], in1=xt[:, half * 512:(half + 1) * 512],
                            op=ALU.add), reads=[bp, bx], writes=[bo])
                    S.dma(lambda e, ot=ot, i=i: e.dma_start(out=dst[i * 128:(i + 1) * 128, :], in_=ot[:]),
                          reads=[bo], writes=[Bdst])
                S.barrier()

        Bout = Buf()
        for l in range(L):
            xsrc = x_in if l == 0 else x1
            dst = out if l == L - 1 else x1
            phase1(l, xsrc)
            with ExitStack() as ey:
                yT = sb("yT", [128, 8, T], BF16, ey)
                ByT = Buf()
                phase2(l, yT, ByT)
                if dbg and l == 0:
                    with ExitStack() as ed:
                        dt_ = sb("dbgt", [128, T], F32, ed)
                        Bd = Buf()
                        for c in range(8):
                            S.op("dve", lambda e, c=c: e.tensor_copy(out=dt_[:], in_=yT[:, c, :]), reads=[ByT], writes=[Bd])
                            S.dma(lambda e, c=c: e.dma_start(out=dbg_y[c], in_=dt_[:]), reads=[Bd], writes=[Bout])
                        S.barrier()
                phase3(l, yT, ByT, xsrc, dst, Bx1 if dst is x1 else Bout)
        S.final_wait()
        block = es.enter_context(nc.Block())
        S.emit(block)
    return nc


def _rel_bucket(dist):
    d = np.maximum(dist, 0)
    lr = np.log(np.maximum(d, 1).astype(np.float32) / 16) / math.log(1024 / 16)
    large = 16 + (lr * 16).astype(np.int32)
    large = np.minimum(large, 31)
    return np.where(d < 16, d, large)


def host_consts(T):
    k = np.arange(128)[:, None]
    q = np.arange(128)[None, :]
    cst = np.zeros((6, 128, 128), np.float32)
    cst[0] = np.eye(128, dtype=np.float32)
    cst[1] = (k <= q).astype(np.float32)
    cst[2] = -(k >= q).astype(np.float32)
    cst[3] = ((k // 64) == (q // 64)).astype(np.float32)
    cst[4] = np.where(k <= q, 0.0, NEG).astype(np.float32)
    cst[5] = np.where(k < q, 0.0, NEG).astype(np.float32)
    onehot = (np.arange(T)[None, :] // 256 == np.arange(16)[:, None]).astype(np.float32)
    idx_m = np.stack([_rel_bucket(dl * 128 + q - k) for dl in range(8)])
    idx_w = np.stack([_rel_bucket(dl * 128 + q - k) for dl in range(2)])
    return cst, onehot, idx_m, idx_w


_CACHE = {}


def kernel(x, norm_gain, w_in, b_forget, fox_qk_gain, moba_qk_gain, swa_qk_gain, sinks, w_out, rel_bias):
    x = np.asarray(x, dtype=np.float32)
    B, T, _ = x.shape
    if T not in _CACHE:
        _CACHE[T] = build(T)
    nc = _CACHE[T]
    cst, onehot, idx_m, idx_w = host_consts(T)
    rel_bias = np.asarray(rel_bias, dtype=np.float32)
    bias_moba = np.ascontiguousarray(np.stack([rel_bias[idx_m, h] for h in range(4)]))
    bias_swa = np.ascontiguousarray(np.stack([rel_bias[idx_w, 4 + h] for h in range(4)]))
    qk_gain = np.ascontiguousarray(np.concatenate(
        [np.asarray(fox_qk_gain, np.float32), np.asarray(moba_qk_gain, np.float32),
         np.asarray(swa_qk_gain, np.float32)], axis=1))
    shared = {
        "norm_gain": np.ascontiguousarray(np.asarray(norm_gain, np.float32)),
        "w_in": np.ascontiguousarray(np.asarray(w_in, np.float32)),
        "b_forget": np.ascontiguousarray(np.asarray(b_forget, np.float32)),
        "qk_gain": qk_gain,
        "sinks": np.ascontiguousarray(np.asarray(sinks, np.float32)),
        "w_out": np.ascontiguousarray(np.asarray(w_out, np.float32)),
        "rb31": np.ascontiguousarray(rel_bias[31:32, :]),
        "bias_moba": bias_moba, "bias_swa": bias_swa, "cst": cst, "onehot": onehot,
    }
    in_maps = []
    for b in range(B):
        m = dict(shared)
        m["x"] = np.ascontiguousarray(x[b])
        in_maps.append(m)
    res = run_bass_kernel_spmd(nc, in_maps, core_ids=list(range(B)))
    return np.stack([np.asarray(r["out"], dtype=np.float32) for r in res.results], axis=0)
```

```python
import math
import numpy as np
from contextlib import ExitStack
import concourse.bass as bass
import concourse.mybir as mybir
from concourse.bass_utils import run_bass_kernel_spmd

F32 = mybir.dt.float32
BF16 = mybir.dt.bfloat16
ALU = mybir.AluOpType
AF = mybir.ActivationFunctionType
AX = mybir.AxisListType

D_MODEL = 1024
DEPTH = 2
HD = 64
IN_WIDTH = 3844
NEG = -30000.0
A_Q, A_K, A_V, A_F, A_G = 0, 256, 512, 768, 772
B_Q, B_K, B_V, B_G = 1028, 1284, 1540, 1796
C_Q, C_K, C_V, C_G = 2052, 2308, 2564, 2820
D_Q, D_K, D_V, D_G = 3076, 3332, 3460, 3588
ID_FQ, ID_FK, ID_MQ, ID_MK, ID_SQ, ID_SK, ID_WQ, ID_WK = 0, 4, 8, 12, 16, 20, 24, 28
N_QK = 30
KROWS = 80


class Buf:
    __slots__ = ("w", "r")

    def __init__(self):
        self.w = {}
        self.r = {}


class Sched:
    CE = ("pe", "act", "dve", "pool")

    def __init__(self, nc, es, n_dma_sems=16):
        self.nc = nc
        self.sems = {}
        for e in self.CE:
            self.sems[e] = es.enter_context(nc.semaphore("s_" + e))
        self.dma_names = ["d%d" % i for i in range(n_dma_sems)]
        for d in self.dma_names:
            self.sems[d] = es.enter_context(nc.semaphore("s_" + d))
        self.dma_cnt = {d: 0 for d in self.dma_names}
        self.dma_rr = 0
        self.streams = {e: [] for e in ("pe", "act", "dve", "pool", "sp")}
        self.n = {e: 0 for e in self.CE}
        self.seen = {e: {} for e in self.streams}
        self.needed = {e: set() for e in self.CE}
        self.pending = {e: {} for e in self.streams}

    def _deps(self, reads, writes, q):
        deps = dict(self.pending[q])
        self.pending[q] = {}
        for b in reads:
            for s, v in b.w.items():
                if deps.get(s, 0) < v:
                    deps[s] = v
        for b in writes:
            for s, v in b.w.items():
                if deps.get(s, 0) < v:
                    deps[s] = v
            for s, v in b.r.items():
                if deps.get(s, 0) < v:
                    deps[s] = v
        return deps

    def _waits(self, q, deps):
        waits = []
        seen = self.seen[q]
        for s, v in deps.items():
            if s == "pe" and q == "pe":
                continue
            if seen.get(s, 0) < v:
                waits.append((s, v))
                seen[s] = v
                if s in self.needed:
                    self.needed[s].add(v)
        return waits

    def _mark(self, sem, val, reads, writes):
        for b in reads:
            if b.r.get(sem, 0) < val:
                b.r[sem] = val
        for b in writes:
            if b.w.get(sem, 0) < val:
                b.w[sem] = val

    def op(self, eng, fn, reads=(), writes=()):
        deps = self._deps(reads, writes, eng)
        waits = self._waits(eng, deps)
        self.n[eng] += 1
        idx = self.n[eng]
        self.streams[eng].append((waits, _record(fn), eng, idx, False))
        self._mark(eng, idx, reads, writes)

    def dma(self, fn, reads=(), writes=(), q="sp"):
        deps = self._deps(reads, writes, q)
        d = self.dma_names[self.dma_rr]
        self.dma_rr = (self.dma_rr + 1) % len(self.dma_names)
        if self.dma_cnt[d] > 0 and deps.get(d, 0) < self.dma_cnt[d]:
            deps[d] = self.dma_cnt[d]
        waits = self._waits(q, deps)
        self.dma_cnt[d] += 16
        val = self.dma_cnt[d]
        self.streams[q].append((waits, _record(fn), d, val, True))
        self._mark(d, val, reads, writes)

    def barrier(self):
        snap = {e: self.n[e] for e in self.CE if self.n[e] > 0}
        for d in self.dma_names:
            if self.dma_cnt[d] > 0:
                snap[d] = self.dma_cnt[d]
        for q in self.pending:
            p = self.pending[q]
            for s, v in snap.items():
                if p.get(s, 0) < v:
                    p[s] = v

    def final_wait(self, q="sp"):
        self.barrier()
        deps = dict(self.pending[q])
        self.pending[q] = {}
        waits = self._waits(q, deps)
        self.streams[q].append((waits, None, None, None, False))

    def emit(self, block):
        cmap = {}
        for e in self.CE:
            ks = sorted(self.needed[e])
            cmap[e] = {k: i + 1 for i, k in enumerate(ks)}
        sems = self.sems
        streams = self.streams

        def runner(ename):
            def body(engobj):
                for (waits, fn, sem, idx, is_dma) in streams[ename]:
                    for (s, v) in waits:
                        vv = cmap[s][v] if s in cmap else v
                        engobj.wait_ge(sems[s], vv)
                    if fn is None:
                        continue
                    ins = getattr(engobj, fn[0])(*fn[1], **fn[2])
                    if is_dma:
                        ins.then_inc(sems[sem], 16)
                    elif idx in cmap[sem]:
                        ins.then_inc(sems[sem], 1)
            return body
        block.tensor(runner("pe"))
        block.scalar(runner("act"))
        block.vector(runner("dve"))
        block.gpsimd(runner("pool"))
        block.sync(runner("sp"))


class _Rec:
    def __init__(self):
        self.call = None

    def __getattr__(self, name):
        def f(*a, **k):
            self.call = (name, a, k)
            return None
        return f


def _record(fn):
    r = _Rec()
    fn(r)
    assert r.call is not None
    return r.call


class Rot:
    def __init__(self, items):
        self.items = items
        self.i = 0

    def next(self):
        it = self.items[self.i]
        self.i = (self.i + 1) % len(self.items)
        return it


def build(T, L=DEPTH, dbg=False):
    NT = T // 128
    NG = T // 512
    NB = T // 256
    nc = bass.Bass("TRN2", target_bir_lowering=False)
    x_in = nc.dram_tensor("x", [T, D_MODEL], F32, kind="ExternalInput").ap()
    norm_gain = nc.dram_tensor("norm_gain", [DEPTH, D_MODEL], F32, kind="ExternalInput").ap()
    w_in = nc.dram_tensor("w_in", [DEPTH, D_MODEL, IN_WIDTH], F32, kind="ExternalInput").ap()
    b_forget = nc.dram_tensor("b_forget", [DEPTH, 4], F32, kind="ExternalInput").ap()
    qk_gain = nc.dram_tensor("qk_gain", [DEPTH, 6, 64], F32, kind="ExternalInput").ap()
    sinks = nc.dram_tensor("sinks", [DEPTH, 4], F32, kind="ExternalInput").ap()
    w_out = nc.dram_tensor("w_out", [DEPTH, D_MODEL, D_MODEL], F32, kind="ExternalInput").ap()
    rb31 = nc.dram_tensor("rb31", [1, 8], F32, kind="ExternalInput").ap()
    bias_moba = nc.dram_tensor("bias_moba", [4, 8, 128, 128], F32, kind="ExternalInput").ap()
    bias_swa = nc.dram_tensor("bias_swa", [4, 2, 128, 128], F32, kind="ExternalInput").ap()
    cst = nc.dram_tensor("cst", [6, 128, 128], F32, kind="ExternalInput").ap()
    onehot = nc.dram_tensor("onehot", [16, T], F32, kind="ExternalInput").ap()
    out = nc.dram_tensor("out", [T, D_MODEL], F32, kind="ExternalOutput").ap()
    x1 = nc.dram_tensor("x1_scr", [T, D_MODEL], F32, kind="Internal").ap()
    qk_scr = nc.dram_tensor("qk_scr", [N_QK, KROWS, T], BF16, kind="Internal").ap()
    g_scr = nc.dram_tensor("g_scr", [8, 128, T], BF16, kind="Internal").ap()
    v_scr = nc.dram_tensor("v_scr", [T, 14, 192], BF16, kind="Internal").ap()
    if dbg:
        dbg_y = nc.dram_tensor("dbg_y", [8, 128, T], F32, kind="ExternalOutput").ap()

    with ExitStack() as es:
        S = Sched(nc, es)

        _uid = [0]

        def sb(name, shape, dt, scope=es):
            _uid[0] += 1
            return scope.enter_context(nc.sbuf_tensor("%s_%d" % (name, _uid[0]), shape, dt))

        PS = [es.enter_context(nc.psum_tensor("ps%d" % i, [128, 512], F32)) for i in range(8)]
        BPS = [Buf() for _ in range(8)]

        ident_b = sb("ident_b", [128, 128], BF16)
        ident_f = sb("ident_f", [128, 128], F32)
        tri_f = sb("tri_f", [128, 128], F32)
        ones_f = sb("ones_f", [128, 128], F32)
        ones_b = sb("ones_b", [128, 128], BF16)
        negU_b = sb("negU_b", [128, 128], BF16)
        blk_b = sb("blk_b", [128, 128], BF16)
        mask_b = sb("mask_b", [128, 3, 128], BF16)
        Bconst = Buf()
        gcols = sb("gcols", [128, DEPTH, 6], F32)
        ngain = sb("ngain", [128, DEPTH, D_MODEL], F32)
        bfg = sb("bfg", [128, DEPTH, 4], F32)
        esink = sb("esink", [128, DEPTH, 4], F32)
        rb31_t = sb("rb31_t", [128, 8], F32)
        biasM = sb("biasM", [128, 4, 8, 128], BF16)
        biasW = sb("biasW", [128, 4, 2, 128], BF16)
        xf_all = sb("xf_all", [128, NT, 4], F32)
        BxF = Buf()
        kmean = sb("kmean", [128, 2, NB], F32)
        Bkm = Buf()
        c_all = sb("c_all", [128, NT, 4], F32)
        Bc = Buf()

        with ExitStack() as e0:
            stg = sb("c_stg", [128, 6, 128], F32, e0)
            Bstg = Buf()
            S.dma(lambda e: e.dma_start(out=stg[:], in_=cst.rearrange("a p n -> p a n")), writes=[Bstg])
            S.op("dve", lambda e: e.tensor_copy(out=ident_f[:], in_=stg[:, 0, :]), reads=[Bstg], writes=[Bconst])
            S.op("dve", lambda e: e.tensor_copy(out=ident_b[:], in_=stg[:, 0, :]), reads=[Bstg], writes=[Bconst])
            S.op("dve", lambda e: e.tensor_copy(out=tri_f[:], in_=stg[:, 1, :]), reads=[Bstg], writes=[Bconst])
            S.op("dve", lambda e: e.tensor_copy(out=negU_b[:], in_=stg[:, 2, :]), reads=[Bstg], writes=[Bconst])
            S.op("dve", lambda e: e.tensor_copy(out=blk_b[:], in_=stg[:, 3, :]), reads=[Bstg], writes=[Bconst])
            S.op("dve", lambda e: e.memset(ones_f[:], 1.0), writes=[Bconst])
            S.op("dve", lambda e: e.memset(ones_b[:], 1.0), writes=[Bconst])
            S.op("dve", lambda e: e.tensor_copy(out=mask_b[:, 0, :], in_=stg[:, 4, :]), reads=[Bstg], writes=[Bconst])
            S.op("dve", lambda e: e.tensor_copy(out=mask_b[:, 1, :], in_=stg[:, 5, :]), reads=[Bstg], writes=[Bconst])
            S.op("dve", lambda e: e.tensor_scalar(out=mask_b[:, 2, :], in0=stg[:, 4, :], scalar1=-1.0, scalar2=NEG,
                                                  op0=ALU.mult, op1=ALU.add), reads=[Bstg], writes=[Bconst])
            for l in range(DEPTH):
                S.dma(lambda e, l=l: e.dma_start(out=ngain[:, l, :], in_=norm_gain[l:l + 1, :].partition_broadcast(128)),
                      writes=[Bconst])
                S.dma(lambda e, l=l: e.dma_start(out=bfg[:, l, :], in_=b_forget[l:l + 1, :].partition_broadcast(128)),
                      writes=[Bconst])
                S.dma(lambda e, l=l: e.dma_start(out=esink[:, l, :], in_=sinks[l:l + 1, :].partition_broadcast(128)),
                      writes=[Bconst])
                for half in range(2):
                    S.dma(lambda e, l=l, half=half: e.dma_start(
                        out=gcols[half * 64:(half + 1) * 64, l, :], in_=qk_gain[l].rearrange("s d -> d s"),
                        allow_slow_non_contiguous=True), writes=[Bconst])
            S.dma(lambda e: e.dma_start(out=rb31_t[:], in_=rb31[0:1, :].partition_broadcast(128)), writes=[Bconst])
            for l in range(DEPTH):
                for j in (0, 2, 4):
                    S.op("dve", lambda e, l=l, j=j: e.tensor_scalar(out=gcols[:, l, j:j + 1], in0=gcols[:, l, j:j + 1],
                                                                    scalar1=0.125, scalar2=None, op0=ALU.mult),
                         reads=[Bconst], writes=[Bconst])
                S.op("act", lambda e, l=l: e.activation(out=esink[:, l, :], in_=esink[:, l, :], func=AF.Exp),
                     reads=[Bconst], writes=[Bconst])
            bst = [sb("b_stg%d" % i, [128, 8, 128], F32, e0) for i in range(2)]
            Bbst = [Buf(), Buf()]
            for h in range(4):
                t_, b_ = bst[h % 2], Bbst[h % 2]
                S.dma(lambda e, h=h, t_=t_: e.dma_start(out=t_[:], in_=bias_moba[h].rearrange("a p n -> p a n")), writes=[b_])
                S.op("dve", lambda e, h=h, t_=t_: e.tensor_scalar(out=t_[:], in0=t_[:], scalar1=rb31_t[:, h:h + 1],
                                                                  scalar2=None, op0=ALU.subtract),
                     reads=[b_, Bconst], writes=[b_])
                S.op("dve", lambda e, h=h, t_=t_: e.tensor_tensor(out=t_[:, 0, :], in0=t_[:, 0, :], in1=stg[:, 4, :],
                                                                  op=ALU.add), reads=[b_, Bstg], writes=[b_])
                S.op("dve", lambda e, h=h, t_=t_: e.tensor_copy(out=biasM[:, h, :, :], in_=t_[:]), reads=[b_], writes=[Bconst])
            for h in range(4):
                t_, b_ = bst[h % 2], Bbst[h % 2]
                S.dma(lambda e, h=h, t_=t_: e.dma_start(out=t_[:, 0:2, :], in_=bias_swa[h].rearrange("a p n -> p a n")),
                      writes=[b_])
                S.op("dve", lambda e, t_=t_: e.tensor_tensor(out=t_[:, 0, :], in0=t_[:, 0, :], in1=stg[:, 4, :],
                                                             op=ALU.add), reads=[b_, Bstg], writes=[b_])
                S.op("dve", lambda e, t_=t_: e.tensor_tensor(out=t_[:, 1, :], in0=t_[:, 1, :], in1=stg[:, 4, :],
                                                             op=ALU.subtract), reads=[b_, Bstg], writes=[b_])
                S.op("dve", lambda e, t_=t_: e.tensor_scalar(out=t_[:, 1, :], in0=t_[:, 1, :], scalar1=NEG,
                                                             scalar2=None, op0=ALU.add), reads=[b_], writes=[b_])
                S.op("dve", lambda e, h=h, t_=t_: e.tensor_copy(out=biasW[:, h, :, :], in_=t_[:, 0:2, :]),
                     reads=[b_], writes=[Bconst])
            oh = sb("oh_stg", [KROWS, T], F32, e0)
            ohb = sb("oh_b", [KROWS, T], BF16, e0)
            Boh = Buf()
            S.dma(lambda e: e.dma_start(out=oh[64:80, :], in_=onehot[:, :]), writes=[Boh])
            S.op("dve", lambda e: e.tensor_copy(out=ohb[64:80, :], in_=oh[64:80, :]), reads=[Boh], writes=[Boh])
            Bqk = [Buf() for _ in range(N_QK)]
            for h in range(4):
                S.dma(lambda e, h=h: e.dma_start(out=qk_scr[ID_MK + h, 64:80, :], in_=ohb[64:80, :]), reads=[Boh],
                      writes=[Bqk[ID_MK + h]])
            onesr = sb("onesr", [KROWS, T], BF16, e0)
            Bor = Buf()
            S.op("pool", lambda e: e.memset(onesr[64:70, :], 1.0), writes=[Bor])
            for h in range(4):
                S.dma(lambda e, h=h: e.dma_start(out=qk_scr[ID_FK + h, 64:67, :], in_=onesr[64:67, :]), reads=[Bor],
                      writes=[Bqk[ID_FK + h]])
                S.dma(lambda e, h=h: e.dma_start(out=qk_scr[ID_FQ + h, 67:70, :], in_=onesr[64:67, :]), reads=[Bor],
                      writes=[Bqk[ID_FQ + h]])
            S.barrier()
        Bg = Buf()
        Bv = Buf()
        Bx1 = Buf()

        def run_pipeline(items):
            n = len(items)
            maxoff = max(o for it in items for (o, _) in it)
            for s_ in range(n + maxoff):
                for off in range(maxoff + 1):
                    t = s_ - off
                    if 0 <= t < n:
                        for (o, fn) in items[t]:
                            if o == off:
                                fn()

        def phase1(l, xsrc):
            with ExitStack() as e1:
                win = sb("win", [128, 8, IN_WIDTH], BF16, e1)
                Bwin = [Buf() for _ in range(8)]
                xts = Rot([(sb("xt%d" % i, [128, D_MODEL], F32, e1), Buf()) for i in range(3)])
                hns = Rot([(sb("hn%d" % i, [128, D_MODEL], BF16, e1), Buf()) for i in range(2)])
                hnTs = Rot([(sb("hnT%d" % i, [128, 8, 512], BF16, e1), Buf()) for i in range(2)])
                junk = sb("junk", [128, D_MODEL], BF16, e1)
                Bjunk = Buf()
                stat = Rot([(sb("stat%d" % i, [128, 4], F32, e1), Buf()) for i in range(3)])
                sqs = Rot([(sb("sq%d" % i, [128, 512], BF16, e1), Buf()) for i in range(3)])
                lns = Rot([(sb("ln%d" % i, [128, 512], F32, e1), Buf()) for i in range(3)])
                rss = Rot([(sb("rs%d" % i, [128, 512], F32, e1), Buf()) for i in range(3)])
                kn32 = Rot([(sb("kn%d" % i, [128, 512], F32, e1), Buf()) for i in range(2)])
                ost = Rot([(sb("ost%d" % i, [128, 512], BF16, e1), Buf()) for i in range(4)])
                vst = Rot([(sb("vst%d" % i, [128, 14, 192], BF16, e1), Buf()) for i in range(2)])
                for (t_, b_) in vst.items:
                    S.op("dve", lambda e: e.memset(t_[:], 1.0), writes=[b_])
                psF = Rot([(PS[i], BPS[i]) for i in (0, 1, 2)])
                psQ = Rot([(PS[i], BPS[i]) for i in (3,)])
                psT = Rot([(PS[i], BPS[i]) for i in (4, 5)])
                psV = Rot([(PS[i], BPS[i]) for i in (6, 7)])
                wv = w_in[l].rearrange("(c p) e -> p c e", p=128)
                S.op("dve", lambda e: e.memset(kmean[:], 0.0), writes=[Bkm])

                def wbufs(c0, w):
                    return [Bwin[k] for k in range(c0 // 512, (c0 + w - 1) // 512 + 1)]

                def prep_item(i, hnT, BhnT, j, dma_now=False):
                    xt, bx = xts.next()
                    st, bst_ = stat.next()
                    hn, bhn = hns.next()
                    pt, bpt = psT.next()
                    ptb = pt[:].bitcast(BF16)
                    if dma_now:
                        S.dma(lambda e: e.dma_start(out=xt[:], in_=xsrc[i * 128:(i + 1) * 128, :]), reads=[Bx1], writes=[bx])

                    def f0():
                        if not dma_now:
                            S.dma(lambda e: e.dma_start(out=xt[:], in_=xsrc[i * 128:(i + 1) * 128, :]), reads=[Bx1],
                                  writes=[bx])
                        S.op("act", lambda e: e.activation(out=junk[:], in_=xt[:], func=AF.Square, accum_out=st[:, 0:1]),
                             reads=[bx], writes=[Bjunk, bst_])
                        S.op("act", lambda e: e.activation(out=st[:, 1:2], in_=st[:, 0:1], func=AF.Ln,
                                                           scale=1.0 / D_MODEL, bias=1e-6), reads=[bst_], writes=[bst_])
                        S.op("act", lambda e: e.activation(out=st[:, 2:3], in_=st[:, 1:2], func=AF.Exp, scale=-0.5),
                             reads=[bst_], writes=[bst_])
                        S.op("dve", lambda e: e.scalar_tensor_tensor(
                            out=hn[:], in0=xt[:], scalar=st[:, 2:3], in1=ngain[:, l, :], op0=ALU.mult, op1=ALU.mult),
                            reads=[bx, bst_, Bconst], writes=[bhn])

                    def f1():
                        for c in range(8):
                            S.op("pe", lambda e: e.transpose(ptb[:, c * 128:(c + 1) * 128], hn[:, c * 128:(c + 1) * 128],
                                                             ident_b[:]), reads=[bhn, Bconst], writes=[bpt])

                    def f2():
                        if j % 2 == 0:
                            S.op("act", lambda e: e.activation(
                                out=hnT[:, :, j * 128:(j + 1) * 128], in_=ptb.rearrange("p (c n) -> p c n", n=128),
                                func=AF.Copy), reads=[bpt], writes=[BhnT])
                        else:
                            S.op("dve", lambda e: e.tensor_copy(
                                out=hnT[:, :, j * 128:(j + 1) * 128], in_=ptb.rearrange("p (c n) -> p c n", n=128)),
                                reads=[bpt], writes=[BhnT])
                    return [(0, f0), (1, f1), (2, f2)]

                def qk_item(hnT, BhnT, g, col0, gj, dst_id, scale_only=False, is_mk=False):
                    M = 128
                    pp, bp = psF.next()
                    o_t, o_b = ost.next()
                    if not scale_only:
                        sq, bsq = sqs.next()
                        pq, bq = psQ.next()
                        ln_t, ln_b = lns.next()
                        rs_t, rs_b = rss.next()
                        if is_mk:
                            k32, bk32 = kn32.next()

                    def f0():
                        for c in range(8):
                            S.op("pe", lambda e: e.matmul(pp[0:M, :], lhsT=win[:, c, col0:col0 + M], rhs=hnT[:, c, :],
                                                          start=(c == 0), stop=(c == 7)),
                                 reads=wbufs(col0, M) + [BhnT], writes=[bp])

                    def store():
                        for hh in range(2):
                            S.dma(lambda e: e.dma_start(out=qk_scr[dst_id + hh, 0:64, g * 512:(g + 1) * 512],
                                                        in_=o_t[hh * 64:(hh + 1) * 64, :]),
                                  reads=[o_b], writes=[Bqk[dst_id + hh]], q="pool")

                    def f1():
                        if scale_only:
                            sc = 0.125 if gj else 1.0
                            S.op("act", lambda e: e.activation(out=o_t[0:M, :], in_=pp[0:M, :], func=AF.Copy, scale=sc),
                                 reads=[bp], writes=[o_b])
                            store()
                        else:
                            S.op("act", lambda e: e.activation(out=sq[0:M, :], in_=pp[0:M, :], func=AF.Square),
                                 reads=[bp], writes=[bsq])

                    def f2():
                        if scale_only:
                            return
                        S.op("pe", lambda e: e.matmul(pq[0:M, :], lhsT=blk_b[0:M, 0:M], rhs=sq[0:M, :], start=True,
                                                      stop=True), reads=[bsq, Bconst], writes=[bq])
                        S.op("act", lambda e: e.activation(out=ln_t[0:M, :], in_=pq[0:M, :], func=AF.Ln,
                                                           scale=1.0 / 64.0, bias=1e-6), reads=[bq], writes=[ln_b])
                        S.op("act", lambda e: e.activation(out=rs_t[0:M, :], in_=ln_t[0:M, :], func=AF.Exp, scale=-0.5),
                             reads=[ln_b], writes=[rs_b])
                        if is_mk:
                            S.op("dve", lambda e: e.scalar_tensor_tensor(
                                out=k32[:, :], in0=pp[:, :], scalar=gcols[:, l, gj:gj + 1], in1=rs_t[:, :],
                                op0=ALU.mult, op1=ALU.mult), reads=[bp, rs_b, Bconst], writes=[bk32])
                            S.op("act", lambda e: e.activation(out=o_t[:, :], in_=k32[:, :], func=AF.Copy), reads=[bk32],
                                 writes=[o_b])
                            pr = (dst_id - ID_MK) // 2
                            S.op("dve", lambda e: e.tensor_reduce(
                                out=kmean[:, pr, 2 * g:2 * g + 2], in_=k32[:, :].rearrange("p (a b) -> p a b", b=256),
                                axis=AX.X, op=ALU.add), reads=[bk32], writes=[Bkm])
                        else:
                            S.op("dve", lambda e: e.scalar_tensor_tensor(
                                out=o_t[0:M, :], in0=pp[0:M, :], scalar=gcols[0:M, l, gj:gj + 1], in1=rs_t[0:M, :],
                                op0=ALU.mult, op1=ALU.mult), reads=[bp, rs_b, Bconst], writes=[o_b])
                        store()
                    return [(0, f0), (1, f1), (2, f2)]

                def gate_item(hnT, BhnT, g, col0, gid):
                    pp, bp = psF.next()
                    o_t, o_b = ost.next()

                    def f0():
                        for c in range(8):
                            S.op("pe", lambda e: e.matmul(pp[:, :], lhsT=win[:, c, col0:col0 + 128], rhs=hnT[:, c, :],
                                                          start=(c == 0), stop=(c == 7)),
                                 reads=wbufs(col0, 128) + [BhnT], writes=[bp])

                    def f1():
                        S.op("act", lambda e: e.activation(out=o_t[:, :], in_=pp[:, :], func=AF.Silu),
                             reads=[bp], writes=[o_b])
                        S.dma(lambda e: e.dma_start(out=g_scr[gid, :, g * 512:(g + 1) * 512], in_=o_t[:, :]),
                              reads=[o_b], writes=[Bg], q="pool")
                    return [(0, f0), (1, f1)]

                def v_item(hnT, BhnT, g, j):
                    i = g * 4 + j
                    pa, ba = psV.next()
                    pb, bb = psV.next()
                    vt, bvt = vst.next()

                    def f0():
                        specs = [(pa, ba, 0, A_V, 256), (pa, ba, 256, B_V, 256), (pb, bb, 0, C_V, 256),
                                 (pb, bb, 256, D_V, 128), (pb, bb, 384, A_F, 4)]
                        for (pp, bp, o0, c0, w) in specs:
                            for c in range(8):
                                S.op("pe", lambda e: e.matmul(
                                    pp[:, o0:o0 + w], lhsT=hnT[:, c, j * 128:(j + 1) * 128], rhs=win[:, c, c0:c0 + w],
                                    start=(c == 0), stop=(c == 7)), reads=wbufs(c0, w) + [BhnT], writes=[bp])

                    def f1():
                        S.op("act", lambda e: e.activation(
                            out=vt[:, 0:8, 64:128], in_=pa[:, 0:512].rearrange("p (h d) -> p h d", d=64), func=AF.Copy),
                            reads=[ba], writes=[bvt])
                        S.op("dve", lambda e: e.tensor_copy(
                            out=vt[:, 8:14, 64:128], in_=pb[:, 0:384].rearrange("p (h d) -> p h d", d=64)),
                            reads=[bb], writes=[bvt])
                        S.op("dve", lambda e: e.tensor_tensor(out=xf_all[:, i, :], in0=pb[:, 384:388], in1=bfg[:, l, :],
                                                              op=ALU.add), reads=[bb, Bconst], writes=[BxF])
                        S.dma(lambda e: e.dma_start(out=v_scr[i * 128:(i + 1) * 128, :, :], in_=vt[:]), reads=[bvt],
                              writes=[Bv], q="pool")
                    return [(0, f0), (1, f1)]

                items = []
                hn_bufs = [hnTs.next() for _ in range(2)]
                for j in range(3):
                    items.append(prep_item(j, hn_bufs[0][0], hn_bufs[0][1], j, dma_now=True))
                items.append(prep_item(3, hn_bufs[0][0], hn_bufs[0][1], 3))
                wst = Rot([(sb("wst%d" % i, [128, 8, 256], F32, e1), Buf()) for i in range(5)])
                for cc in range((IN_WIDTH + 255) // 256):
                    c0 = cc * 256
                    cw = min(256, IN_WIDTH - c0)
                    t_, b_ = wst.next()
                    S.dma(lambda e: e.dma_start(out=t_[:, :, 0:cw], in_=wv[:, :, c0:c0 + cw]), writes=[b_])
                    if cc % 2 == 0:
                        S.op("dve", lambda e: e.tensor_copy(out=win[:, :, c0:c0 + cw], in_=t_[:, :, 0:cw]),
                             reads=[b_], writes=[Bwin[c0 // 512]])
                    else:
                        S.op("act", lambda e: e.activation(out=win[:, :, c0:c0 + cw], in_=t_[:, :, 0:cw], func=AF.Copy),
                             reads=[b_], writes=[Bwin[c0 // 512]])
                items.append([])
                items.append([])
                for g in range(NG):
                    hnT, BhnT = hn_bufs[g % 2]
                    gi = []
                    for pr in range(2):
                        gi.append(qk_item(hnT, BhnT, g, A_Q + pr * 128, 0, ID_FQ + 2 * pr))
                        gi.append(qk_item(hnT, BhnT, g, A_K + pr * 128, 1, ID_FK + 2 * pr))
                    gi.append(v_item(hnT, BhnT, g, 0))
                    for pr in range(2):
                        gi.append(qk_item(hnT, BhnT, g, B_Q + pr * 128, 2, ID_MQ + 2 * pr))
                        gi.append(qk_item(hnT, BhnT, g, B_K + pr * 128, 3, ID_MK + 2 * pr, is_mk=True))
                    gi.append(v_item(hnT, BhnT, g, 1))
                    for pr in range(2):
                        gi.append(qk_item(hnT, BhnT, g, C_Q + pr * 128, 1, ID_SQ + 2 * pr, scale_only=True))
                        gi.append(qk_item(hnT, BhnT, g, C_K + pr * 128, 0, ID_SK + 2 * pr, scale_only=True))
                    gi.append(v_item(hnT, BhnT, g, 2))
                    for pr in range(2):
                        gi.append(qk_item(hnT, BhnT, g, D_Q + pr * 128, 4, ID_WQ + 2 * pr))
                    gi.append(qk_item(hnT, BhnT, g, D_K, 5, ID_WK))
                    gi.append(v_item(hnT, BhnT, g, 3))
                    for m, gc in enumerate((A_G, B_G, C_G, D_G)):
                        for pr in range(2):
                            gi.append(gate_item(hnT, BhnT, g, gc + pr * 128, m * 2 + pr))
                    if g + 1 < NG:
                        nh, nb = hn_bufs[(g + 1) % 2]
                        pos = [2, 7, 12, 17]
                        for j in range(4):
                            gi.insert(pos[j] + j, prep_item((g + 1) * 4 + j, nh, nb, j))
                    items.extend(gi)
                run_pipeline(items)
                S.barrier()

        def phase2(l, yT, ByT):
            with ExitStack() as e2:
                QAs = Rot([(sb("QA%d" % i, [KROWS, T], BF16, e2), Buf()) for i in range(2)])
                KAs = Rot([(sb("KA%d" % i, [KROWS, T], BF16, e2), Buf()) for i in range(2)])
                VAs = Rot([(sb("VA%d" % i, [128, NT, 192], BF16, e2), Buf()) for i in range(2)])
                GTs = Rot([(sb("GT%d" % i, [128, T], BF16, e2), Buf()) for i in range(2)])
                Ps = Rot([(sb("P%d" % i, [128, 512], BF16, e2), Buf()) for i in range(3)])
                rdn = Rot([(sb("rdn%d" % i, [128, 512], F32, e2), Buf()) for i in range(2)])
                tmo = Rot([(sb("tmo%d" % i, [128, 512], F32, e2), Buf()) for i in range(2)])
                psS = Rot([(PS[i], BPS[i]) for i in (0, 1, 2)])
                psO = Rot([(PS[i], BPS[i]) for i in (3, 4)])
                psM = Rot([(PS[i], BPS[i]) for i in (5, 6)])
                psC = Rot([(PS[i], BPS[i]) for i in (7,)])

                def load_head(qid, kid, vh, nq, nk=None):
                    nk = nq if nk is None else nk
                    QA, bQ = QAs.next()
                    KA, bK = KAs.next()
                    VA, bV = VAs.next()
                    S.dma(lambda e: e.dma_start(out=QA[0:nq, :], in_=qk_scr[qid, 0:nq, :]), reads=[Bqk[qid]],
                          writes=[bQ])
                    S.dma(lambda e: e.dma_start(out=KA[0:nk, :], in_=qk_scr[kid, 0:nk, :]), reads=[Bqk[kid]],
                          writes=[bK])
                    S.dma(lambda e: e.dma_start(out=VA[:], in_=v_scr[:, vh, :].rearrange("(n p) d -> p n d", p=128)),
                          reads=[Bv], writes=[bV])
                    return QA, bQ, KA, bK, VA, bV

                def load_gate(gid):
                    GT, bG = GTs.next()
                    S.dma(lambda e: e.dma_start(out=GT[:], in_=g_scr[gid]), reads=[Bg], writes=[bG])
                    return GT, bG

                def finalize(O, bO, par, GT, bG, chunk, g, normalize, extra_den=None):
                    orow = slice(0, 64) if par == 0 else slice(64, 128)
                    drow = slice(64, 128) if par == 0 else slice(0, 64)
                    cols = slice(g * 512, (g + 1) * 512)
                    if normalize:
                        r_t, r_b = rdn.next()
                        t_t, t_b = tmo.next()
                        if extra_den is not None:
                            S.op("act", lambda e: e.activation(out=r_t[orow, :], in_=O[drow, :], func=AF.Ln,
                                                               bias=extra_den[drow]), reads=[bO, Bconst], writes=[r_b])
                            S.op("act", lambda e: e.activation(out=r_t[orow, :], in_=r_t[orow, :], func=AF.Exp,
                                                               scale=-1.0), reads=[r_b], writes=[r_b])
                            S.op("dve", lambda e: e.tensor_tensor(out=t_t[orow, :], in0=O[orow, :], in1=r_t[orow, :],
                                                                  op=ALU.mult), reads=[bO, r_b], writes=[t_b])
                            S.op("dve", lambda e: e.tensor_tensor(out=yT[orow, chunk, cols], in0=t_t[orow, :],
                                                                  in1=GT[orow, cols], op=ALU.mult),
                                 reads=[t_b, bG], writes=[ByT])
                        else:
                            S.op("dve", lambda e: e.reciprocal(out=r_t[orow, :], in_=O[drow, :]), reads=[bO],
                                 writes=[r_b])
                            S.op("dve", lambda e: e.tensor_tensor(out=t_t[orow, :], in0=O[orow, :], in1=r_t[orow, :],
                                                                  op=ALU.mult), reads=[bO, r_b], writes=[t_b])
                            S.op("pool", lambda e: e.tensor_tensor(out=yT[orow, chunk, cols], in0=t_t[orow, :],
                                                                   in1=GT[orow, cols], op=ALU.mult),
                                 reads=[t_b, bG], writes=[ByT])
                    else:
                        S.op("dve", lambda e: e.tensor_tensor(out=yT[orow, chunk, cols], in0=O[orow, :],
                                                              in1=GT[orow, cols], op=ALU.mult),
                             reads=[bO, bG], writes=[ByT])

                def vslice(VA, ki, par):
                    return VA[:, ki, 64:192] if par == 0 else VA[:, ki, 0:128]

                with ExitStack() as ef:
                    NF = NT * 4
                    xf2 = xf_all[:].rearrange("p n h -> p (n h)")
                    S.op("act", lambda e: e.activation(out=xf2, in_=xf2, func=AF.Exp, scale=-1.0), reads=[BxF], writes=[BxF])
                    S.op("act", lambda e: e.activation(out=xf2, in_=xf2, func=AF.Ln, bias=1.0), reads=[BxF], writes=[BxF])
                    S.op("dve", lambda e: e.tensor_scalar(out=xf2, in0=xf2, scalar1=-1.0, scalar2=None, op0=ALU.mult),
                         reads=[BxF], writes=[BxF])
                    p1, b1 = psM.next()
                    p2, b2 = psM.next()
                    S.op("pe", lambda e: e.matmul(p1[:, 0:NF], lhsT=tri_f[:], rhs=xf2, start=True, stop=True),
                         reads=[BxF, Bconst], writes=[b1])
                    S.op("pe", lambda e: e.matmul(p2[:, 0:NF], lhsT=ones_f[:], rhs=xf2, start=True, stop=True),
                         reads=[BxF, Bconst], writes=[b2])
                    tot = sb("f_tot", [128, NT, 4], F32, ef)
                    exc = sb("f_exc", [128, NT, 4], F32, ef)
                    Bt = Buf()
                    S.op("dve", lambda e: e.tensor_copy(out=tot[:].rearrange("p n h -> p (n h)"), in_=p2[:, 0:NF]),
                         reads=[b2], writes=[Bt])
                    S.op("dve", lambda e: e.memset(exc[:, 0, :], 0.0), writes=[Bt])
                    for i in range(1, NT):
                        S.op("dve", lambda e, i=i: e.tensor_tensor(out=exc[:, i, :], in0=exc[:, i - 1, :],
                                                                   in1=tot[:, i - 1, :], op=ALU.add),
                             reads=[Bt], writes=[Bt])
                    S.op("dve", lambda e: e.tensor_tensor(out=c_all[:].rearrange("p n h -> p (n h)"), in0=p1[:, 0:NF],
                                                          in1=exc[:].rearrange("p n h -> p (n h)"), op=ALU.add),
                         reads=[b1, Bt], writes=[Bc])
                    spl = sb("f_spl", [128, NT, 4, 6], BF16, ef)
                    r1 = sb("f_r1", [128, NT, 4], F32, ef)
                    r2 = sb("f_r2", [128, NT, 4], F32, ef)
                    Bs = Buf()
                    S.op("dve", lambda e: e.tensor_copy(out=spl[:, :, :, 0], in_=c_all[:]), reads=[Bc], writes=[Bs])
                    S.op("dve", lambda e: e.tensor_tensor(out=r1[:], in0=c_all[:], in1=spl[:, :, :, 0], op=ALU.subtract),
                         reads=[Bc, Bs], writes=[Bs])
                    S.op("dve", lambda e: e.tensor_copy(out=spl[:, :, :, 1], in_=r1[:]), reads=[Bs], writes=[Bs])
                    S.op("dve", lambda e: e.tensor_tensor(out=r2[:], in0=r1[:], in1=spl[:, :, :, 1], op=ALU.subtract),
                         reads=[Bs], writes=[Bs])
                    S.op("dve", lambda e: e.tensor_copy(out=spl[:, :, :, 2], in_=r2[:]), reads=[Bs], writes=[Bs])
                    S.op("dve", lambda e: e.tensor_scalar(out=spl[:, :, :, 3:6], in0=spl[:, :, :, 0:3], scalar1=-1.0,
                                                          scalar2=None, op0=ALU.mult), reads=[Bs], writes=[Bs])
                    augs = Rot([(sb("f_aug%d" % i, [KROWS, 512], BF16, ef), Buf()) for i in range(2)])
                    for h in range(4):
                        for g in range(NG):
                            pm, bm = psM.next()
                            for j in range(4):
                                i = g * 4 + j
                                S.op("pe", lambda e, pm=pm, i=i, j=j, h=h: e.matmul(
                                    pm[64:70, j * 128:(j + 1) * 128], lhsT=spl[:, i, h, :], rhs=ident_b[:],
                                    start=True, stop=True), reads=[Bs, Bconst], writes=[bm])
                            a_t, a_b = augs.next()
                            S.op("act", lambda e, pm=pm, a_t=a_t: e.activation(out=a_t[64:70, :], in_=pm[64:70, :],
                                                                               func=AF.Copy), reads=[bm], writes=[a_b])
                            S.dma(lambda e, a_t=a_t, h=h, g=g: e.dma_start(
                                out=qk_scr[ID_FQ + h, 64:67, g * 512:(g + 1) * 512], in_=a_t[64:67, :]),
                                reads=[a_b], writes=[Bqk[ID_FQ + h]])
                            S.dma(lambda e, a_t=a_t, h=h, g=g: e.dma_start(
                                out=qk_scr[ID_FK + h, 67:70, g * 512:(g + 1) * 512], in_=a_t[67:70, :]),
                                reads=[a_b], writes=[Bqk[ID_FK + h]])

                S.barrier()
                def softmax_tile(QA, qbufs, KA, bK, VA, bV, O, bO, g, ki, par, krows, bias_tiles, act_bias, fin):
                    j = max(ki - 4 * g, 0)
                    c0 = j * 128
                    last = 4 * g + 3
                    S_, bS = psS.next()
                    P, bP = Ps.next()

                    def f0():
                        S.op("pe", lambda e: e.matmul(
                            S_[:, c0:512], lhsT=KA[0:krows, ki * 128:(ki + 1) * 128],
                            rhs=QA[0:krows, g * 512 + c0:(g + 1) * 512], start=True, stop=True),
                            reads=list(qbufs) + [bK], writes=[bS])
                        if bias_tiles is None:
                            if ki >= 4 * g:
                                S.op("pe", lambda e: e.matmul(
                                    S_[:, c0:c0 + 128], lhsT=ident_b[:], rhs=mask_b[:, 0, :], start=False, stop=True),
                                    reads=[Bconst], writes=[bS])
                        else:
                            for jj in range(j, 4):
                                dl = 4 * g + jj - ki
                                if dl < 8:
                                    S.op("pe", lambda e: e.matmul(
                                        S_[:, jj * 128:(jj + 1) * 128], lhsT=ident_b[:], rhs=bias_tiles[:, dl, :],
                                        start=False, stop=True), reads=[Bconst], writes=[bS])

                    def f1():
                        if act_bias is None:
                            S.op("act", lambda e: e.activation(out=P[:, c0:512], in_=S_[:, c0:512], func=AF.Exp),
                                 reads=[bS], writes=[bP])
                        else:
                            S.op("act", lambda e: e.activation(out=P[:, c0:512], in_=S_[:, c0:512], func=AF.Exp,
                                                               bias=act_bias), reads=[bS, Bconst], writes=[bP])

                    def f2():
                        S.op("pe", lambda e: e.matmul(
                            O[:, c0:512], lhsT=vslice(VA, ki, par), rhs=P[:, c0:512], start=(ki == 0),
                            stop=(ki == last)), reads=[bP, bV], writes=[bO])
                        if ki == last:
                            fin()
                    return [(0, f0), (1, f1), (2, f2)]

                def softmax_head(QA, bQ, KA, bK, VA, bV, GT, bG, par, chunk, krows, bias_tiles=None, act_bias=None,
                                 qaug=None, hooks=None):
                    items = []
                    for g in range(NG):
                        O, bO = psO.next()
                        qbufs = [bQ] if qaug is None else [bQ, qaug[g % 2]]

                        def fin(O=O, bO=bO, g=g):
                            finalize(O, bO, par, GT, bG, chunk, g, True)
                        for ki in range(4 * g + 4):
                            it = softmax_tile(QA, qbufs, KA, bK, VA, bV, O, bO, g, ki, par, krows, bias_tiles, act_bias,
                                              fin)
                            if hooks is not None:
                                if ki == 0 and (g, 0) in hooks:
                                    it.insert(0, (0, hooks[(g, 0)]))
                                if ki == 4 * g + 3 and (g, 1) in hooks:
                                    it.append((0, hooks[(g, 1)]))
                            items.append(it)
                    run_pipeline(items)

                for h in range(4):
                    QA, bQ, KA, bK, VA, bV = load_head(ID_FQ + h, ID_FK + h, h, 70, 70)
                    if h % 2 == 0:
                        GT, bG = load_gate(h // 2)
                    softmax_head(QA, bQ, KA, bK, VA, bV, GT, bG, h % 2, h // 2, 70)

                with ExitStack() as em:
                    km_b = sb("km_b", [64, 4, NB], BF16, em)
                    Bkmb = Buf()
                    for h in range(4):
                        S.op("dve", lambda e: e.tensor_copy(out=km_b[0:64, h, :],
                                                            in_=kmean[(h % 2) * 64:(h % 2) * 64 + 64, h // 2, :]),
                             reads=[Bkm], writes=[Bkmb])
                    gms = Rot([(sb("gm%d" % i, [128, 4, 16], F32, em), Buf()) for i in range(2)])
                    t8s = Rot([(sb("t8%d" % i, [128, 4, 8], F32, em), Buf()) for i in range(2)])
                    msl = Rot([(sb("msl%d" % i, [128, 4, 16], BF16, em), Buf()) for i in range(2)])

                    def make_sel(QA, bQ, qaug, h, g):
                        st = {}

                        def part1():
                            pg, bg_ = psM.next()
                            for j in range(4):
                                i = g * 4 + j
                                S.op("pe", lambda e: e.matmul(
                                    pg[:, j * 16:j * 16 + NB], lhsT=QA[0:64, i * 128:(i + 1) * 128], rhs=km_b[0:64, h, :],
                                    start=True, stop=True), reads=[bQ, Bkmb], writes=[bg_])
                            gm, bgm = gms.next()
                            S.op("pool", lambda e: e.memset(gm[:], -1e30), writes=[bgm])
                            for j in range(4):
                                qb = (g * 4 + j) // 2
                                if qb > 0:
                                    S.op("dve", lambda e: e.tensor_copy(out=gm[:, j, 0:qb], in_=pg[:, j * 16:j * 16 + qb]),
                                         reads=[bg_], writes=[bgm])
                            t8, bt8 = t8s.next()
                            for j in range(4):
                                S.op("dve", lambda e: e.max(out=t8[:, j, :], in_=gm[:, j, :]), reads=[bgm], writes=[bt8])
                            ms, bms = msl.next()
                            for j in range(4):
                                S.op("dve", lambda e: e.tensor_scalar(
                                    out=ms[:, j, :], in0=gm[:, j, :], scalar1=t8[:, j, 2:3], scalar2=NEG, op0=ALU.is_lt,
                                    op1=ALU.mult), reads=[bgm, bt8], writes=[bms])
                                qb = (g * 4 + j) // 2
                                S.op("dve", lambda e: e.memset(ms[:, j, qb:qb + 1], 0.0), writes=[bms])
                            st["ms"] = (ms, bms)

                        def part2():
                            ms, bms = st["ms"]
                            pm, bm = psM.next()
                            for j in range(4):
                                S.op("pe", lambda e: e.matmul(
                                    pm[64:80, j * 128:(j + 1) * 128], lhsT=ms[:, j, :], rhs=ident_b[:], start=True,
                                    stop=True), reads=[bms, Bconst], writes=[bm])
                            S.op("act", lambda e: e.activation(out=QA[64:80, g * 512:(g + 1) * 512], in_=pm[64:80, :],
                                                               func=AF.Copy), reads=[bm], writes=[qaug[g % 2]])
                        return part1, part2

                    for h in range(4):
                        QA, bQ, KA, bK, VA, bV = load_head(ID_MQ + h, ID_MK + h, 4 + h, 64, 80)
                        if h % 2 == 0:
                            GT, bG = load_gate(2 + h // 2)
                        qaug = [Buf(), Buf()]
                        hooks = {}
                        p1, p2 = make_sel(QA, bQ, qaug, h, 0)
                        p1()
                        p2()
                        for g in range(NG - 1):
                            p1, p2 = make_sel(QA, bQ, qaug, h, g + 1)
                            hooks[(g, 0)] = p1
                            hooks[(g, 1)] = p2
                        softmax_head(QA, bQ, KA, bK, VA, bV, GT, bG, h % 2, 2 + h // 2, 80,
                                     bias_tiles=biasM[:, h, :, :], act_bias=rb31_t[:, h:h + 1], qaug=qaug, hooks=hooks)

                S.barrier()
                with ExitStack() as esb:
                    Es = Rot([(sb("sbE%d" % i, [128, 512], F32, esb), Buf()) for i in range(3)])
                    SPs = Rot([(sb("sbSP%d" % i, [128, 512], BF16, esb), Buf()) for i in range(3)])
                    ARs = Rot([(sb("sbAR%d" % i, [128, 512], F32, esb), Buf()) for i in range(3)])
                    carry = sb("sbcarry", [128, 512], F32, esb)
                    Bcar = Buf()
                    psT3 = Rot([(PS[i], BPS[i]) for i in (5, 6, 7)])

                    def sb_tile(QA, bQ, KA, bK, VA, bV, O, bO, g, ki, par, fin):
                        j = max(ki - 4 * g, 0)
                        c0 = j * 128
                        first = 4 * g + 3
                        Z, bZ = psS.next()
                        E, bE = Es.next()
                        SP, bSP = SPs.next()
                        Tb, bT = psT3.next()
                        AR, bAR = ARs.next()
                        P, bP = Ps.next()

                        def fA():
                            S.op("pe", lambda e: e.matmul(
                                Z[:, c0:512], lhsT=KA[0:64, ki * 128:(ki + 1) * 128],
                                rhs=QA[0:64, g * 512 + c0:(g + 1) * 512], start=True, stop=True),
                                reads=[bQ, bK], writes=[bZ])
                            if ki >= 4 * g:
                                S.op("pe", lambda e: e.matmul(
                                    Z[:, c0:c0 + 128], lhsT=ident_b[:], rhs=mask_b[:, 1, :], start=False, stop=True),
                                    reads=[Bconst], writes=[bZ])

                        def fB():
                            S.op("act", lambda e: e.activation(out=E[:, c0:512], in_=Z[:, c0:512], func=AF.Exp),
                                 reads=[bZ], writes=[bE])
                            S.op("act", lambda e: e.activation(out=SP[:, c0:512], in_=E[:, c0:512], func=AF.Ln, bias=1.0),
                                 reads=[bE], writes=[bSP])

                        def fC():
                            S.op("pe", lambda e: e.matmul(Z[:, c0:512], lhsT=negU_b[:], rhs=SP[:, c0:512], start=False,
                                                          stop=True), reads=[bSP, Bconst], writes=[bZ])
                            S.op("pe", lambda e: e.matmul(Tb[:, c0:512], lhsT=ones_b[:], rhs=SP[:, c0:512], start=True,
                                                          stop=True), reads=[bSP, Bconst], writes=[bT])

                        def fD():
                            if ki == first:
                                S.op("dve", lambda e: e.memset(carry[:], 0.0), writes=[Bcar])
                            S.op("dve", lambda e: e.tensor_tensor(out=AR[:, c0:512], in0=Z[:, c0:512],
                                                                  in1=carry[:, c0:512], op=ALU.subtract),
                                 reads=[bZ, Bcar], writes=[bAR])
                            S.op("dve", lambda e: e.tensor_tensor(out=carry[:, c0:512], in0=Tb[:, c0:512],
                                                                  in1=carry[:, c0:512], op=ALU.add),
                                 reads=[bT, Bcar], writes=[Bcar])

                        def fE():
                            S.op("act", lambda e: e.activation(out=P[:, c0:512], in_=AR[:, c0:512], func=AF.Exp),
                                 reads=[bAR], writes=[bP])

                        def fF():
                            lh = VA[:, ki, 64:128] if par == 0 else VA[:, ki, 0:128]
                            orows = slice(0, 64) if par == 0 else slice(0, 128)
                            S.op("pe", lambda e: e.matmul(O[orows, c0:512], lhsT=lh, rhs=P[:, c0:512],
                                                          start=(ki == first), stop=(ki == 0)),
                                 reads=[bP, bV], writes=[bO])
                            if ki == 0:
                                fin()
                        return [(0, fA), (0, fB), (1, fC), (1, fD), (2, fE), (3, fF)]

                    for h in range(4):
                        QA, bQ, KA, bK, VA, bV = load_head(ID_SQ + h, ID_SK + h, 8 + h, 64, 64)
                        par = h % 2
                        if par == 0:
                            GT, bG = load_gate(4 + h // 2)
                        items = []
                        for g in range(NG):
                            O, bO = psO.next()

                            def fin(O=O, bO=bO, g=g, par=par, GT=GT, bG=bG, h=h):
                                finalize(O, bO, par, GT, bG, 4 + h // 2, g, False)
                            for ki in range(4 * g + 3, -1, -1):
                                items.append(sb_tile(QA, bQ, KA, bK, VA, bV, O, bO, g, ki, par, fin))
                        run_pipeline(items)

                S.barrier()
                KA = None
                for h in range(4):
                    kv = h // 2
                    par = h % 2
                    if par == 0:
                        QA, bQ, KA, bK, VA, bV = load_head(ID_WQ + h, ID_WK + kv, 12 + kv, 64)
                        GT, bG = load_gate(6 + h // 2)
                    else:
                        QA, bQ = QAs.next()
                        S.dma(lambda e, QA=QA, h=h: e.dma_start(out=QA[0:64, :], in_=qk_scr[ID_WQ + h, 0:64, :]),
                              reads=[Bqk[ID_WQ + h]], writes=[bQ])
                    for g in range(NG):
                        O, bO = psO.next()
                        for jp in range(2):
                            S_, bS = psS.next()
                            for jq in range(2):
                                j = jp * 2 + jq
                                qi = g * 4 + j
                                qcols = slice(qi * 128, (qi + 1) * 128)
                                for dl in (1, 0):
                                    ki = qi - dl
                                    if ki < 0:
                                        continue
                                    sc = slice(jq * 256 + (1 - dl) * 128, jq * 256 + (1 - dl) * 128 + 128)
                                    S.op("pe", lambda e, S_=S_, sc=sc, ki=ki, qcols=qcols, QA=QA, KA=KA: e.matmul(
                                        S_[:, sc], lhsT=KA[0:64, ki * 128:(ki + 1) * 128], rhs=QA[0:64, qcols],
                                        start=True, stop=True), reads=[bQ, bK], writes=[bS])
                                    S.op("pe", lambda e, S_=S_, sc=sc, dl=dl, h=h: e.matmul(
                                        S_[:, sc], lhsT=ident_b[:], rhs=biasW[:, h, dl, :], start=False, stop=True),
                                        reads=[Bconst], writes=[bS])
                            P, bP = Ps.next()
                            a0 = 128 if (g == 0 and jp == 0) else 0
                            S.op("act", lambda e, S_=S_, P=P, a0=a0: e.activation(out=P[:, a0:512], in_=S_[:, a0:512],
                                                                                  func=AF.Exp), reads=[bS], writes=[bP])
                            for jq in range(2):
                                j = jp * 2 + jq
                                qi = g * 4 + j
                                dls = [d_ for d_ in (1, 0) if qi - d_ >= 0]
                                for n_, dl in enumerate(dls):
                                    ki = qi - dl
                                    sc = slice(jq * 256 + (1 - dl) * 128, jq * 256 + (1 - dl) * 128 + 128)
                                    S.op("pe", lambda e, O=O, P=P, sc=sc, ki=ki, j=j, n_=n_, dls=dls, VA=VA, par=par: e.matmul(
                                        O[:, j * 128:(j + 1) * 128], lhsT=vslice(VA, ki, par), rhs=P[:, sc],
                                        start=(n_ == 0), stop=(n_ == len(dls) - 1)), reads=[bP, bV], writes=[bO])
                        finalize(O, bO, par, GT, bG, 6 + h // 2, g, True, extra_den=esink[:, l, h:h + 1])
                S.barrier()

        def phase3(l, yT, ByT, xsrc, dst, Bdst):
            with ExitStack() as e3:
                wo = sb("wo", [128, 8, D_MODEL], BF16, e3)
                Bwo = Buf()
                wst = Rot([(sb("wost%d" % i, [128, 8, 512], F32, e3), Buf()) for i in range(2)])
                wv = w_out[l].rearrange("(c p) e -> p c e", p=128)
                for cc in range(2):
                    t_, b_ = wst.next()
                    S.dma(lambda e, t_=t_, cc=cc: e.dma_start(out=t_[:], in_=wv[:, :, cc * 512:(cc + 1) * 512]), writes=[b_])
                    S.op("dve" if cc == 0 else "pool",
                         lambda e, t_=t_, cc=cc: e.tensor_copy(out=wo[:, :, cc * 512:(cc + 1) * 512], in_=t_[:]),
                         reads=[b_], writes=[Bwo])
                xts = Rot([(sb("x3t%d" % i, [128, D_MODEL], F32, e3), Buf()) for i in range(6)])
                ots = Rot([(sb("o3t%d" % i, [128, D_MODEL], F32, e3), Buf()) for i in range(6)])
                psA = Rot([(PS[i], BPS[i]) for i in range(8)])
                for i in range(NT):
                    xt, bx = xts.next()
                    S.dma(lambda e, xt=xt, i=i: e.dma_start(out=xt[:], in_=xsrc[i * 128:(i + 1) * 128, :]),
                          reads=[Bx1], writes=[bx])
                    ot, bo = ots.next()
                    for half in range(2):
                        pp, bp = psA.next()
                        for c in range(8):
                            S.op("pe", lambda e, pp=pp, c=c, i=i, half=half: e.matmul(
                                pp[:, :], lhsT=yT[:, c, i * 128:(i + 1) * 128], rhs=wo[:, c, half * 512:(half + 1) * 512],
                                start=(c == 0), stop=(c == 7)), reads=[ByT, Bwo], writes=[bp])
                        S.op("dve", lambda e, pp=pp, xt=xt, ot=ot, half=half: e.tensor_tensor(
                            out=ot[:, half * 512:(half + 1) * 512], in0=pp[:, :], in1=xt[:, half * 512:(half + 1) * 512],
                            op=ALU.add), reads=[bp, bx], writes=[bo])
                    S.dma(lambda e, ot=ot, i=i: e.dma_start(out=dst[i * 128:(i + 1) * 128, :], in_=ot[:]),
                          reads=[bo], writes=[Bdst], q="pool")
                S.barrier()

        Bout = Buf()
        for l in range(L):
            xsrc = x_in if l == 0 else x1
            dst = out if l == L - 1 else x1
            phase1(l, xsrc)
            with ExitStack() as ey:
                yT = sb("yT", [128, 8, T], BF16, ey)
                ByT = Buf()
                phase2(l, yT, ByT)
                if dbg and l == 0:
                    with ExitStack() as ed:
                        dt_ = sb("dbgt", [128, T], F32, ed)
                        Bd = Buf()
                        for c in range(8):
                            S.op("dve", lambda e, c=c: e.tensor_copy(out=dt_[:], in_=yT[:, c, :]), reads=[ByT], writes=[Bd])
                            S.dma(lambda e, c=c: e.dma_start(out=dbg_y[c], in_=dt_[:]), reads=[Bd], writes=[Bout])
                        S.barrier()
                phase3(l, yT, ByT, xsrc, dst, Bx1 if dst is x1 else Bout)
        S.final_wait()
        block = es.enter_context(nc.Block())
        S.emit(block)
    return nc


def _rel_bucket(dist):
    d = np.maximum(dist, 0)
    lr = np.log(np.maximum(d, 1).astype(np.float32) / 16) / math.log(1024 / 16)
    large = 16 + (lr * 16).astype(np.int32)
    large = np.minimum(large, 31)
    return np.where(d < 16, d, large)


def host_consts(T):
    k = np.arange(128)[:, None]
    q = np.arange(128)[None, :]
    cst = np.zeros((6, 128, 128), np.float32)
    cst[0] = np.eye(128, dtype=np.float32)
    cst[1] = (k <= q).astype(np.float32)
    cst[2] = -(k >= q).astype(np.float32)
    cst[3] = ((k // 64) == (q // 64)).astype(np.float32)
    cst[4] = np.where(k <= q, 0.0, NEG).astype(np.float32)
    cst[5] = np.where(k < q, 0.0, NEG).astype(np.float32)
    onehot = (np.arange(T)[None, :] // 256 == np.arange(16)[:, None]).astype(np.float32)
    idx_m = np.stack([_rel_bucket(dl * 128 + q - k) for dl in range(8)])
    idx_w = np.stack([_rel_bucket(dl * 128 + q - k) for dl in range(2)])
    return cst, onehot, idx_m, idx_w


_CACHE = {}


def kernel(x, norm_gain, w_in, b_forget, fox_qk_gain, moba_qk_gain, swa_qk_gain, sinks, w_out, rel_bias):
    x = np.asarray(x, dtype=np.float32)
    B, T, _ = x.shape
    if T not in _CACHE:
        _CACHE[T] = build(T)
    nc = _CACHE[T]
    cst, onehot, idx_m, idx_w = host_consts(T)
    rel_bias = np.asarray(rel_bias, dtype=np.float32)
    bias_moba = np.ascontiguousarray(np.stack([rel_bias[idx_m, h] for h in range(4)]))
    bias_swa = np.ascontiguousarray(np.stack([rel_bias[idx_w, 4 + h] for h in range(4)]))
    qk_gain = np.ascontiguousarray(np.concatenate(
        [np.asarray(fox_qk_gain, np.float32), np.asarray(moba_qk_gain, np.float32),
         np.asarray(swa_qk_gain, np.float32)], axis=1))
    shared = {
        "norm_gain": np.ascontiguousarray(np.asarray(norm_gain, np.float32)),
        "w_in": np.ascontiguousarray(np.asarray(w_in, np.float32)),
        "b_forget": np.ascontiguousarray(np.asarray(b_forget, np.float32)),
        "qk_gain": qk_gain,
        "sinks": np.ascontiguousarray(np.asarray(sinks, np.float32)),
        "w_out": np.ascontiguousarray(np.asarray(w_out, np.float32)),
        "rb31": np.ascontiguousarray(rel_bias[31:32, :]),
        "bias_moba": bias_moba, "bias_swa": bias_swa, "cst": cst, "onehot": onehot,
    }
    in_maps = []
    for b in range(B):
        m = dict(shared)
        m["x"] = np.ascontiguousarray(x[b])
        in_maps.append(m)
    res = run_bass_kernel_spmd(nc, in_maps, core_ids=list(range(B)))
    return np.stack([np.asarray(r["out"], dtype=np.float32) for r in res.results], axis=0)
```
